# Optimizing a Trainium2 kernel written in Bass

```python
import math
import jax, jax.numpy as jnp
from jax import lax
import numpy as np

D_MODEL = 1024
BATCH = 8
SEQ = 2048
DEPTH = 1
DEC_BATCH = 32
DEC_SEQ = 8
PAST_LEN = 16384
PAGE_SIZE = 128

HEAD_DIM = 64
N_HEADS_A = 16
W_A = N_HEADS_A * HEAD_DIM
W_R = D_MODEL
N_LRU_BLOCKS = 16
LRU_BLOCK = W_R // N_LRU_BLOCKS
D_MIX = W_A + W_R
D_PROJ = 4 * W_A + 2 * W_R
CONV_W = 4
LRU_C = 8.0
DILATIONS = (1, 4, 16)
N_STEPS = 128
WINDOW_MAX = 2048
BLOCK = 128
N_BUCKETS = 32
MAX_EXACT = 16
MAX_DISTANCE = 2048
EPS = 1e-6
SCALE = HEAD_DIM ** -0.5
NEG_INF = -1e30

kernel_name = "hymba_dilated_swa_rglru_step"


def rmsnorm(x, g):
    xf = x.astype(jnp.float32)
    y = xf * lax.rsqrt(jnp.mean(xf * xf, axis=-1, keepdims=True) + EPS)
    return (y * g.astype(jnp.float32)).astype(x.dtype)


def t5_bucket(dist):
    nf = jnp.maximum(dist, 1).astype(jnp.float32)
    large = MAX_EXACT + (jnp.log(nf / MAX_EXACT) / math.log(MAX_DISTANCE / MAX_EXACT)
                         * (N_BUCKETS - MAX_EXACT)).astype(jnp.int32)
    large = jnp.minimum(large, N_BUCKETS - 1)
    return jnp.where(dist < MAX_EXACT, dist, large)


def dilated_band_prompt(q, k, v, rel_bias, dil):
    B, L, H, Dh = q.shape
    span = dil * BLOCK
    Lp = -(-L // span) * span
    nb = Lp // span

    def to_sub(x):
        x = jnp.pad(x, ((0, 0), (0, Lp - L), (0, 0), (0, 0)))
        x = x.reshape(B, Lp // dil, dil, H, Dh).transpose(0, 2, 1, 3, 4)
        return x.reshape(B, dil, nb, BLOCK, H, Dh)

    qs, ks, vs = to_sub(q), to_sub(k), to_sub(v)

    def with_prev(x):
        prev = jnp.pad(x, ((0, 0), (0, 0), (1, 0), (0, 0), (0, 0), (0, 0)))[:, :, :nb]
        return jnp.concatenate([prev, x], axis=3)

    kb, vb = with_prev(ks), with_prev(vs)
    i = jnp.arange(BLOCK)[:, None]
    j = jnp.arange(2 * BLOCK)[None, :]
    diff = i - j + BLOCK
    band = (diff >= 0) & (diff <= N_STEPS)
    first = (jnp.arange(nb) > 0)[:, None, None] | (j >= BLOCK)[None]
    valid = band[None] & first
    bias = rel_bias[t5_bucket(jnp.maximum(diff, 0) * dil)].transpose(2, 0, 1)
    logits = jnp.einsum('brnqhd,brnkhd->brnhqk', qs, kb).astype(jnp.float32) * SCALE
    logits = logits + bias.astype(jnp.float32)
    logits = jnp.where(valid[None, None, :, None], logits, NEG_INF)
    lse = jax.nn.logsumexp(logits, axis=-1)
    p = jnp.exp(logits - lse[..., None]).astype(v.dtype)
    o = jnp.einsum('brnhqk,brnkhd->brnqhd', p, vb)
    o = o.reshape(B, dil, Lp // dil, H, Dh).transpose(0, 2, 1, 3, 4).reshape(B, Lp, H, Dh)[:, :L]
    lse = lse.transpose(0, 1, 2, 4, 3).reshape(B, dil, Lp // dil, H)
    lse = lse.transpose(0, 2, 1, 3).reshape(B, Lp, H)[:, :L]
    return o, lse


def dilated_gather_sample(q, k_ctx, v_ctx, rel_bias, dil, q_start):
    T = q.shape[1]
    steps = jnp.arange(N_STEPS + 1)
    idx = q_start + jnp.arange(T)[:, None] - dil * steps[None, :]
    valid = idx >= 0
    idx_c = jnp.maximum(idx, 0)
    kg = k_ctx[:, idx_c]
    vg = v_ctx[:, idx_c]
    bias = rel_bias[t5_bucket(dil * steps)].T[:, None, :]
    logits = jnp.einsum('bthd,btshd->bhts', q, kg).astype(jnp.float32) * SCALE
    logits = logits + bias.astype(jnp.float32)
    logits = jnp.where(valid[None, None], logits, NEG_INF)
    lse = jax.nn.logsumexp(logits, axis=-1)
    p = jnp.exp(logits - lse[..., None]).astype(v_ctx.dtype)
    o = jnp.einsum('bhts,btshd->bthd', p, vg)
    return o, lse.transpose(0, 2, 1)


def mixer_layer(x, k_past, v_past, conv_past, h_past, rel_bias, norm_in, w_in, norm_attn,
                norm_lru, conv_w, conv_b, w_gate_x, b_gate_x, w_gate_a, b_gate_a, lru_param, w_out):
    B, T, _ = x.shape
    h = rmsnorm(x, norm_in)
    proj = h @ w_in
    q, k, v, g_a, x_r, g_r = jnp.split(
        proj, [W_A, 2 * W_A, 3 * W_A, 4 * W_A, 4 * W_A + W_R], axis=-1)
    q = q.reshape(B, T, N_HEADS_A, HEAD_DIM)
    k = k.reshape(B, T, N_HEADS_A, HEAD_DIM)
    v = v.reshape(B, T, N_HEADS_A, HEAD_DIM)

    if k_past is None:
        outs = [dilated_band_prompt(q, k, v, rel_bias, d) for d in DILATIONS]
        nkeep = min(WINDOW_MAX, T)
        new_k, new_v = k[:, -nkeep:], v[:, -nkeep:]
    else:
        wb = k_past.shape[1]
        k_ctx = jnp.concatenate([k_past, k], axis=1)
        v_ctx = jnp.concatenate([v_past, v], axis=1)
        outs = [dilated_gather_sample(q, k_ctx, v_ctx, rel_bias, d, wb) for d in DILATIONS]
        new_k, new_v = k_ctx[:, -wb:], v_ctx[:, -wb:]
    lses = jnp.stack([o_l[1] for o_l in outs], axis=0)
    wts = jax.nn.softmax(lses, axis=0)
    o_att = jnp.sum(wts[..., None] * jnp.stack([o_l[0] for o_l in outs], axis=0).astype(jnp.float32), axis=0)
    o_att = o_att.astype(x.dtype).reshape(B, T, W_A)
    y_att = rmsnorm(o_att, norm_attn) * jax.nn.silu(g_a)

    xpad = jnp.concatenate([conv_past.astype(x_r.dtype), x_r], axis=1)
    xc = conv_b + sum(conv_w[tap] * xpad[:, tap:tap + T] for tap in range(CONV_W))
    new_conv = xpad[:, -(CONV_W - 1):]
    xb = xc.reshape(B, T, N_LRU_BLOCKS, LRU_BLOCK)
    gate_x = jax.nn.sigmoid(jnp.einsum('bthi,hij->bthj', xb, w_gate_x).reshape(B, T, W_R) + b_gate_x)
    gate_a = jax.nn.sigmoid(jnp.einsum('bthi,hij->bthj', xb, w_gate_a).reshape(B, T, W_R) + b_gate_a)
    log_a = -LRU_C * gate_a.astype(jnp.float32) * jax.nn.softplus(-lru_param.astype(jnp.float32))
    a = jnp.exp(log_a)
    bx = jnp.sqrt(-jnp.expm1(2.0 * log_a)) * (gate_x * xc).astype(jnp.float32)

    def step(hc, ab):
        a_t, b_t = ab
        hc = a_t * hc + b_t
        return hc, hc

    h_last, hs = lax.scan(step, h_past.astype(jnp.float32),
                          (a.transpose(1, 0, 2), bx.transpose(1, 0, 2)))
    o_lru = hs.transpose(1, 0, 2).astype(x.dtype)
    y_lru = rmsnorm(o_lru, norm_lru) * jax.nn.silu(g_r)

    y = x + jnp.concatenate([y_att, y_lru], axis=-1) @ w_out
    return y, new_k, new_v, new_conv, h_last.astype(h_past.dtype)


def setup_inputs(seed: int = 0) -> dict:
    key = jax.random.key(seed)
    ks = jax.random.split(key, 20)
    wb = min(WINDOW_MAX, PAST_LEN)
    nrm = jax.random.normal
    a0 = jax.random.uniform(ks[16], (DEPTH, W_R), minval=0.9, maxval=0.999)
    s = a0 ** (1.0 / LRU_C)
    return {
        'x_prompt': nrm(ks[0], (BATCH, SEQ, D_MODEL), jnp.float32),
        'x_sample': nrm(ks[1], (DEC_BATCH, DEC_SEQ, D_MODEL), jnp.float32),
        'cache_win_k': nrm(ks[2], (DEPTH, DEC_BATCH, wb, N_HEADS_A, HEAD_DIM), jnp.float32),
        'cache_win_v': nrm(ks[3], (DEPTH, DEC_BATCH, wb, N_HEADS_A, HEAD_DIM), jnp.float32),
        'state_conv': nrm(ks[4], (DEPTH, DEC_BATCH, CONV_W - 1, W_R), jnp.float32),
        'state_lru': 0.5 * nrm(ks[5], (DEPTH, DEC_BATCH, W_R), jnp.float32),
        'rel_bias': 0.1 * nrm(ks[6], (N_BUCKETS, N_HEADS_A), jnp.float32),
        'norm_in': 1.0 + 0.01 * nrm(ks[7], (DEPTH, D_MODEL), jnp.float32),
        'w_in': nrm(ks[8], (DEPTH, D_MODEL, D_PROJ), jnp.float32) * D_MODEL ** -0.5,
        'norm_attn': 1.0 + 0.01 * nrm(ks[9], (DEPTH, W_A), jnp.float32),
        'norm_lru': 1.0 + 0.01 * nrm(ks[10], (DEPTH, W_R), jnp.float32),
        'conv_w': nrm(ks[11], (DEPTH, CONV_W, W_R), jnp.float32) * CONV_W ** -0.5,
        'conv_b': 0.01 * nrm(ks[12], (DEPTH, W_R), jnp.float32),
        'w_gate_x': nrm(ks[13], (DEPTH, N_LRU_BLOCKS, LRU_BLOCK, LRU_BLOCK), jnp.float32) * LRU_BLOCK ** -0.5,
        'b_gate_x': 0.01 * nrm(ks[14], (DEPTH, W_R), jnp.float32),
        'w_gate_a': nrm(ks[15], (DEPTH, N_LRU_BLOCKS, LRU_BLOCK, LRU_BLOCK), jnp.float32) * LRU_BLOCK ** -0.5,
        'b_gate_a': 0.01 * nrm(ks[17], (DEPTH, W_R), jnp.float32),
        'lru_param': jnp.log(s) - jnp.log1p(-s),
        'w_out': nrm(ks[18], (DEPTH, D_MIX, D_MODEL), jnp.float32) * D_MIX ** -0.5,
        'norm_final': 1.0 + 0.01 * nrm(ks[19], (D_MODEL,), jnp.float32),
    }


def reference(x_prompt, x_sample, cache_win_k, cache_win_v, state_conv, state_lru, rel_bias,
              norm_in, w_in, norm_attn, norm_lru, conv_w, conv_b, w_gate_x, b_gate_x,
              w_gate_a, b_gate_a, lru_param, w_out, norm_final):
    xp, xs = x_prompt, x_sample
    kp_l, vp_l, cp_l, hp_l = [], [], [], []
    ks_l, vs_l, cs_l, hs_l = [], [], [], []
    for l in range(DEPTH):
        layer_w = (rel_bias, norm_in[l], w_in[l], norm_attn[l], norm_lru[l], conv_w[l], conv_b[l],
                   w_gate_x[l], b_gate_x[l], w_gate_a[l], b_gate_a[l], lru_param[l], w_out[l])
        conv0 = jnp.zeros((xp.shape[0], CONV_W - 1, W_R), xp.dtype)
        h0 = jnp.zeros((xp.shape[0], W_R), state_lru.dtype)
        xp, kp, vp, cp, hp = mixer_layer(xp, None, None, conv0, h0, *layer_w)
        xs, ks_, vs_, cs_, hs_ = mixer_layer(xs, cache_win_k[l], cache_win_v[l], state_conv[l],
                                             state_lru[l], *layer_w)
        kp_l.append(kp); vp_l.append(vp); cp_l.append(cp); hp_l.append(hp)
        ks_l.append(ks_); vs_l.append(vs_); cs_l.append(cs_); hs_l.append(hs_)
    y_prompt = rmsnorm(xp, norm_final)
    y_sample = rmsnorm(xs, norm_final)
    return (y_prompt, y_sample,
            jnp.stack(kp_l), jnp.stack(vp_l), jnp.stack(cp_l), jnp.stack(hp_l),
            jnp.stack(ks_l), jnp.stack(vs_l), jnp.stack(cs_l), jnp.stack(hs_l))
```

```python
import math
from contextlib import ExitStack

import numpy as np
import concourse.bass as bass
import concourse.mybir as mybir
from concourse.bass_utils import run_bass_kernel_spmd

F32 = mybir.dt.float32
BF16 = mybir.dt.bfloat16
ALU = mybir.AluOpType
ACTF = mybir.ActivationFunctionType

NCORES = 8
D = 1024
L = 2048
NS = 32
NT = L + NS
NSEQ = 4
TS = 8
WB = 2048
DPROJ = 6144
EPS = 1e-6
NEG = -30000.0
TBLK = [(0, 512), (512, 512), (1024, 512), (1536, 512), (2048, NS)]
TPL = 383
TSL = 2063
ENGS = ("pe", "act", "dve", "pool", "sp")


class Sched:
    NDMA = 28

    def __init__(self, nc, stack):
        self.nc = nc
        self.q = {e: [] for e in ENGS}
        self.sem = {e: stack.enter_context(nc.semaphore("c_" + e)) for e in ENGS}
        self.cnt = {e: 0 for e in ENGS}
        self.seen = {e: {} for e in ENGS}
        self.dsem = [stack.enter_context(nc.semaphore("d%d" % i)) for i in range(self.NDMA)]
        self.dval = [0] * self.NDMA
        self.dnext = 0
        self.lastw = {}
        self.readers = {}
        self.out_events = []

    def _deps(self, reads, writes):
        ev = []
        for r in reads:
            w = self.lastw.get(r)
            if w is not None:
                ev.append(w)
        for w_ in writes:
            w = self.lastw.get(w_)
            if w is not None:
                ev.append(w)
            ev.extend(self.readers.get(w_, ()))
        return ev

    def _emit_waits(self, eng, events):
        need = {}
        for ev in events:
            if ev[0] == "dma":
                key = ("dma", ev[1])
                val = ev[2]
            else:
                if ev[0] == eng and eng in ("pe", "sp"):
                    continue
                key = ev[0]
                val = ev[1]
            if self.seen[eng].get(key, 0) >= val:
                continue
            if need.get(key, 0) < val:
                need[key] = val
        for key, val in need.items():
            self.seen[eng][key] = val
            sem = self.dsem[key[1]] if isinstance(key, tuple) else self.sem[key]
            self.q[eng].append(("wait", sem, val))

    def _record(self, reads, writes, event):
        for r in reads:
            self.readers.setdefault(r, []).append(event)
        for w in writes:
            self.lastw[w] = event
            self.readers[w] = []

    def op(self, eng, fn, reads=(), writes=()):
        self._emit_waits(eng, self._deps(reads, writes))
        self.cnt[eng] += 1
        event = (eng, self.cnt[eng])
        self.q[eng].append(("op", fn, self.sem[eng], 1))
        self._record(reads, writes, event)
        return event

    def dma(self, eng, fn, reads=(), writes=(), is_output=False):
        k = self.dnext
        self.dnext = (self.dnext + 1) % self.NDMA
        events = self._deps(reads, writes)
        if self.dval[k] > 0:
            events.append(("dma", k, self.dval[k]))
        self._emit_waits(eng, events)
        self.dval[k] += 16
        event = ("dma", k, self.dval[k])
        self.q[eng].append(("op", fn, self.dsem[k], 16))
        self._record(reads, writes, event)
        if is_output:
            self.out_events.append(event)
        return event

    def all_events(self):
        evs = []
        for k in range(self.NDMA):
            if self.dval[k] > 0:
                evs.append(("dma", k, self.dval[k]))
        for e in ENGS:
            if self.cnt[e] > 0:
                evs.append((e, self.cnt[e]))
        return evs

    def barrier(self):
        evs = self.all_events()
        for e in ENGS:
            self._emit_waits(e, [x for x in evs if x[0] != e or e not in ("pe", "sp")])
        self.lastw = {}
        self.readers = {}

    def finish(self):
        self._emit_waits("sp", self.all_events())

    def replay(self):
        nc = self.nc
        sched = self

        def run(name, h):
            for item in sched.q[name]:
                if item[0] == "wait":
                    h.wait_ge(item[1], item[2])
                else:
                    item[1](h).then_inc(item[2], item[3])

        with nc.Block() as block:
            @block.tensor
            def _(h):
                run("pe", h)

            @block.scalar
            def _(h):
                run("act", h)

            @block.vector
            def _(h):
                run("dve", h)

            @block.gpsimd
            def _(h):
                run("pool", h)

            @block.sync
            def _(h):
                run("sp", h)


class Arena:
    def __init__(self, nc, base=16512, end=229376):
        self.nc = nc
        self.cur = base
        self.end = end
        self.n = 0

    def mark(self):
        return self.cur

    def release(self, m):
        self.cur = m

    def alloc(self, name, shape, dt):
        nbytes = int(np.prod(shape[1:])) * (2 if dt == BF16 else 4)
        off = (self.cur + 63) // 64 * 64
        assert off + nbytes <= self.end, (name, off, nbytes, self.end)
        self.cur = off + nbytes
        self.n += 1
        return self.nc.alloc_sbuf_tensor_at("%s_%d" % (name, self.n), shape, dt, offset=off)


def _t5_bucket(dist):
    dist = np.asarray(dist)
    nf = np.maximum(dist, 1).astype(np.float32)
    large = 16 + (np.log(nf / np.float32(16)) / np.float32(math.log(2048 / 16)) * np.float32(16)).astype(np.int32)
    large = np.minimum(large, 31)
    return np.where(dist < 16, dist, large)


def _structural_constants():
    bucket = _t5_bucket
    ohp = np.zeros((35, 3, TPL), np.float32)
    for p, dil in enumerate((1, 4, 16)):
        for x in range(TPL):
            diff = 255 - x
            if 0 <= diff <= 128:
                ohp[int(bucket(np.array([diff * dil]))[0]), p, x] = 8.0
            else:
                ohp[32, p, x] = 1.0
    ohs = np.zeros((35, TSL), np.float32)
    dists = 2055 - np.arange(TSL)
    bks = bucket(np.maximum(dists, 0))
    for y in range(TSL):
        dist = int(dists[y])
        mult = 0
        if dist >= 0:
            mult += 1 if dist <= 128 else 0
            mult += 1 if (dist % 4 == 0 and dist <= 512) else 0
            mult += 1 if (dist % 16 == 0 and dist <= 2048) else 0
        if mult == 0:
            ohs[32, y] = 1.0
        else:
            ohs[int(bks[y]), y] = 8.0
            if mult == 2:
                ohs[33, y] = 8.0
            elif mult == 3:
                ohs[34, y] = 8.0
    rbc = np.zeros((3, 16), np.float32)
    rbc[0] = NEG
    rbc[1] = math.log(2.0)
    rbc[2] = math.log(3.0)
    ident = np.eye(128, dtype=np.float32)
    jm = np.zeros((128, 128), np.float32)
    for m in range(128):
        jm[m, (m // 64) * 64 + 63 - (m % 64)] = 1.0
    sel = np.zeros((128, 16, 8), np.float32)
    for h in range(16):
        for tp in range(8):
            sel[8 * h + tp, h, 7 - tp] = 1.0
    return dict(ohp=ohp, ohs=ohs, rbc=rbc, ident=ident, jm=jm, sel=sel)


V_NIN, V_NATT, V_NLRU, V_CW, V_CB, V_BGX, V_BGA, V_LAM = 0, 8, 16, 24, 56, 64, 72, 80
NVEC = 88


def _mm(out, lhsT, rhs, start, stop):
    return lambda h: h.matmul(out, lhsT=lhsT, rhs=rhs, start=start, stop=stop)


def build_nc(stages=("a0", "a1", "a2", "b", "c", "d"), debug=False, cache_copy=True):
    nc = bass.Bass("TRN2", target_bir_lowering=False)
    di = lambda name, shape: nc.dram_tensor(name, shape, F32, kind="ExternalInput").ap()
    do = lambda name, shape: nc.dram_tensor(name, shape, F32, kind="ExternalOutput").ap()
    xp_d = di("xp", [L, D])
    xs_d = di("xs", [NS, D])
    ck_d = di("ck", [NSEQ, WB, D])
    cv_d = di("cv", [NSEQ, WB, D])
    sconv_d = di("sconv", [128, 8, NSEQ, 3])
    slru_d = di("slru", [128, 8, NSEQ])
    win_d = di("w_in", [D, DPROJ])
    wout_d = di("w_out", [2 * D, D])
    vecs_d = di("vecs", [128, NVEC])
    ginr_d = di("ginrow", [128, D])
    gfin_d = di("gfinrow", [128, D])
    wgx_d = di("wgx", [128, 8, 128])
    wga_d = di("wga", [128, 8, 128])
    rba_d = di("rb_aug", [35, 16])
    ohp_d = di("ohp", [35, 3, TPL])
    ohs_d = di("ohs", [35, TSL])
    id_d = di("ident", [128, 128])
    jm_d = di("jm", [128, 128])
    sel_d = di("sel", [128, 16, 8])

    yp_d = do("yp", [L, D])
    ys_d = do("ys", [NS, D])
    kp_d = do("kp", [L, D])
    vp_d = do("vp", [L, D])
    convp_d = do("convp", [3, D])
    lrup_d = do("lrup", [D])
    ks_d = do("ks", [NSEQ, WB, D])
    vs_d = do("vs", [NSEQ, WB, D])
    convs_d = do("convs", [NSEQ, 3, D])
    lrus_d = do("lrus", [NSEQ, D])

    tp_scr = nc.dram_tensor("tp_scr", [3, 16, TPL], F32, kind="Internal").ap()
    ts_scr = nc.dram_tensor("ts_scr", [16, TSL], F32, kind="Internal").ap()
    yl_scr = nc.dram_tensor("yl_scr", [8, 128, NT], BF16, kind="Internal").ap()
    sga_scr = nc.dram_tensor("sga_scr", [8, 128, NT], BF16, kind="Internal").ap()

    with ExitStack() as st:
        S = Sched(nc, st)
        A = Arena(nc)
        ps = [nc.alloc_psum_tensor("ps%d" % i, [128, 512], F32) for i in range(8)]
        psn = ["ps%d" % i for i in range(8)]

        identb = A.alloc("identb", [128, 128], BF16)
        jb = A.alloc("jb", [128, 128], BF16)
        onesb = A.alloc("onesb", [128, 128], BF16)
        vecs = A.alloc("vecs", [128, NVEC], F32)
        dcol = A.alloc("dcol", [128, 64], F32)
        HL = A.alloc("HL", [128, 8, 5], F32)
        small = A.alloc("small", [128, 256], F32)
        nhalf = A.alloc("nhalf", [128, 32], F32)
        hT = A.alloc("hT", [128, 8, NT], BF16)
        OA = hT
        wbuf = [A.alloc("wbuf", [128, 8, 512], BF16) for _ in range(2)]
        QTs = A.alloc("QTs", [128, 8, NS], BF16)
        KTs = A.alloc("KTs", [128, 8, NS], BF16)
        VTs = A.alloc("VTs", [128, 8, NS], BF16)
        PT = [A.alloc("PT", [128, 512], BF16) for _ in range(3)]
        BTp = [A.alloc("BTp", [128, 3, 2, 256], BF16) for _ in range(2)]
        Vpair = A.alloc("Vpair", [128, 48, 192], BF16)
        stage = [A.alloc("stage", [128, 512], F32) for _ in range(4)]
        sgast = [A.alloc("sgast", [128, NT], BF16) for _ in range(2)]
        rec = [A.alloc("rec", [128, 512], F32) for _ in range(2)]
        arena0 = A.mark()

        SM_SSQX, SM_MSX, SM_RSTDX = 0, 20, 40
        SM_LRU, SM_ATT = 60, 80
        SM_RL, SM_RA = 100, 120
        SM_SSY, SM_RSY = 140, 160
        DC_HBGX, DC_HBGA, DC_HC, DC_C, DC_GL05, DC_GA05 = 0, 8, 16, 24, 32, 40

        S.dma("pool", lambda h: h.dma_start(out=identb[:], in_=id_d), writes=["identb"])
        S.dma("pool", lambda h: h.dma_start(out=jb[:], in_=jm_d), writes=["jb"])
        S.dma("sp", lambda h: h.dma_start(out=vecs[:], in_=vecs_d), writes=["vecs"])
        S.op("pool", lambda h: h.memset(onesb[:], 1.0), writes=["onesb"])
        S.op("pool", lambda h: h.memset(nhalf[:], -0.5), writes=["nhalf"])
        S.op("pool", lambda h: h.memset(HL[:], 0.0), writes=["HL"])
        S.op("dve", lambda h: h.tensor_scalar(out=dcol[:, DC_HBGX:DC_HBGX + 8], in0=vecs[:, V_BGX:V_BGX + 8],
                                              scalar1=0.5, scalar2=None, op0=ALU.mult), reads=["vecs"], writes=["dc_hbgx"])
        S.op("dve", lambda h: h.tensor_scalar(out=dcol[:, DC_HBGA:DC_HBGA + 8], in0=vecs[:, V_BGA:V_BGA + 8],
                                              scalar1=0.5, scalar2=None, op0=ALU.mult), reads=["vecs"], writes=["dc_hbga"])
        S.op("dve", lambda h: h.tensor_scalar(out=dcol[:, DC_GL05:DC_GL05 + 8], in0=vecs[:, V_NLRU:V_NLRU + 8],
                                              scalar1=0.5, scalar2=None, op0=ALU.mult), reads=["vecs"], writes=["dc_gl"])
        S.op("dve", lambda h: h.tensor_scalar(out=dcol[:, DC_GA05:DC_GA05 + 8], in0=vecs[:, V_NATT:V_NATT + 8],
                                              scalar1=0.5, scalar2=None, op0=ALU.mult), reads=["vecs"], writes=["dc_ga"])
        S.op("act", lambda h: h.activation(out=dcol[:, 48:56], in_=vecs[:, V_LAM:V_LAM + 8], func=ACTF.Exp, scale=-1.0),
             reads=["vecs"], writes=["dc_t0"])
        S.op("act", lambda h: h.activation(out=dcol[:, 56:64], in_=dcol[:, 48:56], func=ACTF.Ln, bias=1.0),
             reads=["dc_t0"], writes=["dc_t1"])
        S.op("dve", lambda h: h.tensor_scalar(out=dcol[:, DC_C:DC_C + 8], in0=dcol[:, 56:64], scalar1=-8.0, scalar2=None,
                                              op0=ALU.mult), reads=["dc_t1"], writes=["dc_c"])
        S.op("dve", lambda h: h.tensor_scalar(out=dcol[:, DC_HC:DC_HC + 8], in0=dcol[:, 56:64], scalar1=-4.0, scalar2=None,
                                              op0=ALU.mult), reads=["dc_t1"], writes=["dc_hc"])

        m0 = A.mark()
        rba = A.alloc("rba", [35, 16], F32)
        ohp = A.alloc("ohp", [35, 3, TPL], F32)
        ohs = A.alloc("ohs", [35, TSL], F32)
        tps = A.alloc("tps", [16, 3, TPL], F32)
        tss = A.alloc("tss", [16, TSL], F32)
        S.dma("sp", lambda h: h.dma_start(out=rba[:], in_=rba_d), writes=["rba"])
        S.dma("sp", lambda h: h.dma_start(out=ohp[:], in_=ohp_d), writes=["ohp"])
        S.dma("sp", lambda h: h.dma_start(out=ohs[:], in_=ohs_d), writes=["ohs"])
        for p in range(3):
            S.op("pe", _mm(ps[6][0:16, 0:TPL], rba[:], ohp[:, p, :], True, True), reads=["rba", "ohp"], writes=[psn[6]])
            S.op("act", lambda h, p=p: h.activation(out=tps[:, p, :], in_=ps[6][0:16, 0:TPL], func=ACTF.Copy),
                 reads=[psn[6]], writes=["tps"])
        for i, c0 in enumerate(range(0, TSL, 512)):
            n = min(512, TSL - c0)
            S.op("pe", _mm(ps[6][0:16, 0:n], rba[:], ohs[:, c0:c0 + n], True, True), reads=["rba", "ohs"], writes=[psn[6]])
            S.op("act", lambda h, c0=c0, n=n: h.activation(out=tss[:, c0:c0 + n], in_=ps[6][0:16, 0:n], func=ACTF.Copy),
                 reads=[psn[6]], writes=["tss"])
        S.dma("sp", lambda h: h.dma_start(out=tp_scr.rearrange("p h x -> h p x"), in_=tps[:]), reads=["tps"], writes=["tp_scr"])
        S.dma("sp", lambda h: h.dma_start(out=ts_scr, in_=tss[:]), reads=["tss"], writes=["ts_scr"])

        def load_bt_pair(j, buf):
            for hp in range(2):
                for half in range(2):
                    src = bass.AP(tp_scr.tensor, (2 * j + hp) * TPL + 64 - 64 * half,
                                  [[1, 64], [16 * TPL, 3], [1, 256]])
                    S.dma("pool", lambda h, src=src, hp=hp, half=half, buf=buf: h.dma_start(
                        out=BTp[buf][hp * 64:hp * 64 + 64, :, half, :], in_=src),
                        reads=["tp_scr"], writes=["BTp%d" % buf])

        wstate = {"n": 0}

        def load_w_in(col0):
            b = wstate["n"] % 2
            wstate["n"] += 1
            src = win_d.rearrange("(kc p) n -> p kc n", p=128)[:, :, col0:col0 + 512]
            S.dma("pool", lambda h, b=b, src=src: h.dma_start(out=wbuf[b][:], in_=src), writes=["wbuf%d" % b])
            return b

        if "a0" in stages:
            mA0 = A.mark()
            xin = [A.alloc("xin", [128, D], F32) for _ in range(2)]
            xsb = [A.alloc("xsb", [128, D], BF16) for _ in range(2)]
            ginr = A.alloc("ginr", [128, D], F32)
            junk = A.alloc("junk", [128, D], BF16)
            S.dma("sp", lambda h: h.dma_start(out=ginr[:], in_=ginr_d), writes=["ginr"])
            for t in range(17):
                b = t % 2
                rows = 128 if t < 16 else NS
                src = xp_d[t * 128:(t + 1) * 128, :] if t < 16 else xs_d
                S.dma("sp", lambda h, b=b, rows=rows, src=src: h.dma_start(out=xin[b][0:rows, :], in_=src), writes=["xin%d" % b])
                S.op("act", lambda h, b=b, rows=rows, t=t: h.activation(
                    out=junk[0:rows, :], in_=xin[b][0:rows, :], func=ACTF.Square,
                    accum_out=small[0:rows, SM_SSQX + t:SM_SSQX + t + 1]), reads=["xin%d" % b], writes=["junk", "ssqx%d" % t])
                S.op("dve", lambda h, rows=rows, t=t: h.tensor_scalar(
                    out=small[0:rows, SM_MSX + t:SM_MSX + t + 1], in0=small[0:rows, SM_SSQX + t:SM_SSQX + t + 1],
                    scalar1=1.0 / D, scalar2=EPS, op0=ALU.mult, op1=ALU.add), reads=["ssqx%d" % t], writes=["msx%d" % t])
                S.op("pool", lambda h, rows=rows, t=t: h.tensor_tensor(
                    out=small[0:rows, SM_RSTDX + t:SM_RSTDX + t + 1], in0=small[0:rows, SM_MSX + t:SM_MSX + t + 1],
                    in1=nhalf[0:rows, 0:1], op=ALU.pow), reads=["msx%d" % t, "nhalf"], writes=["rstdx%d" % t])
                S.op("dve", lambda h, b=b, rows=rows, t=t: h.scalar_tensor_tensor(
                    out=xsb[b][0:rows, :], in0=xin[b][0:rows, :], scalar=small[0:rows, SM_RSTDX + t:SM_RSTDX + t + 1],
                    in1=ginr[0:rows, :], op0=ALU.mult, op1=ALU.mult),
                    reads=["xin%d" % b, "rstdx%d" % t, "ginr"], writes=["xsb%d" % b])
                pb = t % 2
                pv = ps[pb][:].bitcast(BF16)
                for kc in range(8):
                    S.op("pe", lambda h, pv=pv, b=b, rows=rows, kc=kc: h.transpose(
                        out=pv[:, kc * 128:kc * 128 + rows], in_=xsb[b][0:rows, kc * 128:(kc + 1) * 128],
                        identity=identb[0:rows, 0:rows]), reads=["xsb%d" % b, "identb"], writes=[psn[pb]])
                blk = min(t // 4, 4)
                dst = hT[:, :, t * 128:t * 128 + rows]
                srcp = pv.rearrange("p (k t) -> p k t", k=8)[:, :, 0:rows]
                if t % 2 == 0:
                    S.op("act", lambda h, dst=dst, srcp=srcp: h.activation(out=dst, in_=srcp, func=ACTF.Copy),
                         reads=[psn[pb]], writes=["hT%d" % blk])
                else:
                    S.op("dve", lambda h, dst=dst, srcp=srcp: h.tensor_copy(out=dst, in_=srcp),
                         reads=[psn[pb]], writes=["hT%d" % blk])
        A.release(arena0)
        S.barrier()

        hTn = ["hT%d" % i for i in range(5)]

        prot = {"n": 0}

        def proj_fm(wb, wcol, consume):
            for tbi, (t0, tn) in enumerate(TBLK):
                k = prot["n"] % 4
                prot["n"] += 1
                for kc in range(8):
                    S.op("pe", _mm(ps[k][:, 0:tn], wbuf[wb][:, kc, wcol:wcol + 128], hT[:, kc, t0:t0 + tn],
                                   kc == 0, kc == 7), reads=["wbuf%d" % wb, hTn[tbi]], writes=[psn[k]])
                consume(tbi, t0, tn, k)

        if "a1" in stages:
            mA1 = A.mark()
            wgx = A.alloc("wgx", [128, 8, 128], BF16)
            wga = A.alloc("wga", [128, 8, 128], BF16)
            XPq = [A.alloc("XPq", [128, NT + 3], F32) for _ in range(2)]
            XPs = [A.alloc("XPs", [128, NSEQ, 3 + TS], F32) for _ in range(2)]
            XCh = [A.alloc("XCh", [128, NT], F32) for _ in range(2)]
            XCb = [A.alloc("XCb", [128, NT], BF16) for _ in range(2)]
            THx = [A.alloc("THx", [128, NT], F32) for _ in range(2)]
            THa = [A.alloc("THa", [128, NT], F32) for _ in range(2)]
            SQb = [A.alloc("SQb", [128, NT], BF16) for _ in range(2)]
            OLb = [A.alloc("OLb", [128, NT], BF16) for _ in range(2)]
            SGc = A.alloc("SGc", [128, NT], BF16)
            slru = A.alloc("slru", [128, 8, NSEQ], F32)
            YLst = sgast
            S.dma("pool", lambda h: h.dma_start(out=wgx[:], in_=wgx_d), writes=["wgx"])
            S.dma("pool", lambda h: h.dma_start(out=wga[:], in_=wga_d), writes=["wga"])
            S.dma("sp", lambda h: h.dma_start(out=slru[:], in_=slru_d), writes=["slru"])
            ssq_first = {"v": True}
            gcnt = {"n": 0}
            wbx = {}
            wbg = {}

            def lru_p1(c):
                s_ = c % 2
                half, cc = c // 4, c % 4
                if cc == 0:
                    wb_x = load_w_in(4 * D + 512 * half)
                    wbx[half] = wb_x
                    for (lo, m, nm) in ((L - 3, 3, "p"), (L, NS, "s")):
                        for kc in range(8):
                            S.op("pe", _mm(ps[6][0:m, :], hT[:, kc, lo:lo + m], wbuf[wb_x][:, kc, :], kc == 0, kc == 7),
                                 reads=["wbuf%d" % wb_x, hTn[3] if nm == "p" else hTn[4]], writes=[psn[6]])
                        sb_ = stage[0] if nm == "p" else stage[1]
                        sn = "stage0" if nm == "p" else "stage1"
                        S.op("act", lambda h, sb_=sb_, m=m: h.activation(out=sb_[0:m, :], in_=ps[6][0:m, :], func=ACTF.Copy),
                             reads=[psn[6]], writes=[sn])
                        if nm == "p":
                            S.dma("sp", lambda h, half=half: h.dma_start(out=convp_d[:, 512 * half:512 * half + 512],
                                                                          in_=stage[0][0:3, :]), reads=[sn], is_output=True)
                        else:
                            for b in range(NSEQ):
                                S.dma("sp", lambda h, half=half, b=b: h.dma_start(
                                    out=convs_d[b, :, 512 * half:512 * half + 512],
                                    in_=stage[1][TS * b + TS - 3:TS * b + TS, :]), reads=[sn], is_output=True)
                wb_x = wbx[half]
                S.dma("sp", lambda h, c=c, s_=s_: h.dma_start(out=XPs[s_][:, :, 0:3], in_=sconv_d[:, c, :, :]),
                      writes=["XPs%d" % s_])
                S.op("pool", lambda h, s_=s_: h.memset(XPq[s_][:, 0:3], 0.0), writes=["XPq%d" % s_])

                def cons_x(tbi, t0, tn, k, s_=s_):
                    if tbi < 4:
                        S.op("act", lambda h: h.activation(out=XPq[s_][:, 3 + t0:3 + t0 + tn], in_=ps[k][:, 0:tn], func=ACTF.Copy),
                             reads=[psn[k]], writes=["XPq%d" % s_])
                    else:
                        S.op("act", lambda h: h.activation(
                            out=XPs[s_][:, :, 3:3 + TS], in_=ps[k][:, 0:NS].rearrange("p (b t) -> p b t", b=NSEQ), func=ACTF.Copy),
                            reads=[psn[k]], writes=["XPs%d" % s_])
                proj_fm(wb_x, 128 * cc, cons_x)

            def lru_p2(c):
                s_ = c % 2
                XC = XCh[s_]
                cw = lambda tap: vecs[:, V_CW + 8 * tap + c:V_CW + 8 * tap + c + 1]
                cb = vecs[:, V_CB + c:V_CB + c + 1]
                XCsv = XC[:, L:NT].rearrange("p (b t) -> p b t", b=NSEQ)
                for (src_of, dstv, rn) in ((lambda tap: XPq[s_][:, tap:tap + L], XC[:, 0:L], "XPq%d" % s_),
                                           (lambda tap: XPs[s_][:, :, tap:tap + TS], XCsv, "XPs%d" % s_)):
                    S.op("dve", lambda h, src_of=src_of, dstv=dstv: h.tensor_scalar(
                        out=dstv, in0=src_of(0), scalar1=cw(0), scalar2=cb, op0=ALU.mult, op1=ALU.add),
                        reads=[rn, "vecs"], writes=["XC%d" % s_])
                    for tap in (1, 2, 3):
                        S.op("dve", lambda h, src_of=src_of, dstv=dstv, tap=tap: h.scalar_tensor_tensor(
                            out=dstv, in0=src_of(tap), scalar=cw(tap), in1=dstv, op0=ALU.mult, op1=ALU.add),
                            reads=[rn, "vecs", "XC%d" % s_], writes=["XC%d" % s_])
                S.op("act", lambda h: h.activation(out=XCb[s_][:], in_=XC[:], func=ACTF.Copy),
                     reads=["XC%d" % s_], writes=["XCb%d" % s_])
                for (wg, wgn, TH, thn, dcb) in ((wgx, "wgx", THx[s_], "THx%d" % s_, DC_HBGX),
                                                (wga, "wga", THa[s_], "THa%d" % s_, DC_HBGA)):
                    for tbi, (t0, tn) in enumerate(TBLK):
                        k = 4 + gcnt["n"] % 2
                        gcnt["n"] += 1
                        S.op("pe", _mm(ps[k][:, 0:tn], wg[:, c, :], XCb[s_][:, t0:t0 + tn], True, True),
                             reads=[wgn, "XCb%d" % s_], writes=[psn[k]])
                        S.op("act", lambda h, TH=TH, t0=t0, tn=tn, k=k, dcb=dcb: h.activation(
                            out=TH[:, t0:t0 + tn], in_=ps[k][:, 0:tn], func=ACTF.Tanh, scale=0.5,
                            bias=dcol[:, dcb + c:dcb + c + 1]), reads=[psn[k], "dc_hbgx", "dc_hbga"], writes=[thn])

            def lru_p3a(c):
                s_ = c % 2
                SQ = XPq[s_][:, 0:NT]
                Aa = THa[s_]
                sqn, than = "XPq%d" % s_, "THa%d" % s_
                S.op("act", lambda h: h.activation(out=SQ, in_=THa[s_][:], func=ACTF.Exp,
                                                   scale=dcol[:, DC_C + c:DC_C + c + 1], bias=dcol[:, DC_C + c:DC_C + c + 1]),
                     reads=[than, "dc_c"], writes=[sqn])
                S.op("act", lambda h: h.activation(out=Aa[:], in_=THa[s_][:], func=ACTF.Exp,
                                                   scale=dcol[:, DC_HC + c:DC_HC + c + 1], bias=dcol[:, DC_HC + c:DC_HC + c + 1]),
                     reads=[than, "dc_hc"], writes=[than])
                S.op("act", lambda h: h.activation(out=SQ, in_=SQ, func=ACTF.Sqrt, scale=-1.0, bias=1.0),
                     reads=[sqn], writes=[sqn])

            def lru_p3b(c):
                s_ = c % 2
                XC = XCh[s_]
                Hh = XCh[s_]
                SQ = XPq[s_][:, 0:NT]
                Aa = THa[s_]
                xn, hn, sqn, thxn, than = "XC%d" % s_, "XC%d" % s_, "XPq%d" % s_, "THx%d" % s_, "THa%d" % s_
                S.op("dve", lambda h: h.scalar_tensor_tensor(out=THx[s_][:], in0=THx[s_][:], scalar=1.0, in1=XC[:],
                                                             op0=ALU.add, op1=ALU.mult), reads=[thxn, xn], writes=[thxn])
                S.op("dve", lambda h: h.scalar_tensor_tensor(out=THx[s_][:], in0=THx[s_][:], scalar=0.5, in1=SQ,
                                                             op0=ALU.mult, op1=ALU.mult), reads=[thxn, sqn], writes=[thxn])
                S.op("dve", lambda h: h.tensor_tensor_scan(out=Hh[:, 0:L], data0=Aa[:, 0:L], data1=THx[s_][:, 0:L], initial=0.0,
                                                           op0=ALU.mult, op1=ALU.add), reads=[than, thxn], writes=[hn])
                for b in range(NSEQ):
                    S.op("dve", lambda h, b=b: h.tensor_tensor_scan(
                        out=Hh[:, L + TS * b:L + TS * b + TS], data0=Aa[:, L + TS * b:L + TS * b + TS],
                        data1=THx[s_][:, L + TS * b:L + TS * b + TS], initial=slru[:, c, b:b + 1],
                        op0=ALU.mult, op1=ALU.add), reads=[than, thxn, "slru"], writes=[hn])

            def lru_gr(c):
                half, cc = c // 4, c % 4
                if cc == 0:
                    wbg[half] = load_w_in(5 * D + 512 * half)

                def cons_g(tbi, t0, tn, k):
                    S.op("act", lambda h: h.activation(out=SGc[:, t0:t0 + tn], in_=ps[k][:, 0:tn], func=ACTF.Silu),
                         reads=[psn[k]], writes=["SGc"])
                proj_fm(wbg[half], 128 * cc, cons_g)

            def lru_p3c(c):
                s_ = c % 2
                Hh = XCh[s_]
                hn = "XC%d" % s_
                S.op("pool", lambda h: h.tensor_copy(out=HL[:, c, 0:1], in_=Hh[:, L - 1:L]), reads=[hn], writes=["HL"])
                S.op("pool", lambda h: h.tensor_copy(
                    out=HL[:, c, 1:5], in_=Hh[:, L:NT].rearrange("p (b t) -> p b t", b=NSEQ)[:, :, TS - 1]),
                    reads=[hn], writes=["HL"])
                S.op("pool", lambda h: h.tensor_tensor(out=SQb[s_][:], in0=Hh[:], in1=Hh[:], op=ALU.mult),
                     reads=[hn], writes=["SQb%d" % s_])
                S.op("act", lambda h: h.activation(out=OLb[s_][:], in_=Hh[:], func=ACTF.Copy),
                     reads=[hn], writes=["OLb%d" % s_])
                for t in range(17):
                    rows = 128 if t < 16 else NS
                    S.op("pe", _mm(ps[7][0:rows, t:t + 1], SQb[s_][:, t * 128:t * 128 + rows], onesb[:, 0:1],
                                   ssq_first["v"], c == 7 and t == 16), reads=["SQb%d" % s_, "onesb"], writes=[psn[7]])
                    ssq_first["v"] = False
                yb = c % 2
                S.op("dve", lambda h: h.scalar_tensor_tensor(
                    out=YLst[yb][:], in0=SGc[:], scalar=vecs[:, V_NLRU + c:V_NLRU + c + 1],
                    in1=OLb[s_][:], op0=ALU.mult, op1=ALU.mult),
                    reads=["SGc", "OLb%d" % s_, "vecs"], writes=["YLst%d" % yb])
                S.dma("sp", lambda h: h.dma_start(out=yl_scr[c], in_=YLst[yb][:]),
                      reads=["YLst%d" % yb], writes=["yl_scr"])

            lru_p1(0)
            lru_p2(0)
            lru_p1(1)
            for c in range(8):
                lru_p3a(c)
                if c + 1 < 8:
                    lru_p2(c + 1)
                lru_p3b(c)
                lru_gr(c)
                if c + 2 < 8:
                    lru_p1(c + 2)
                lru_p3c(c)
            S.op("dve", lambda h: h.tensor_copy(out=small[:, SM_LRU:SM_LRU + 17], in_=ps[7][:, 0:17]),
                 reads=[psn[7]], writes=["ssq_lru"])
            S.dma("sp", lambda h: h.dma_start(out=lrup_d.rearrange("(c p) -> p c", p=128), in_=HL[:, :, 0],
                                              allow_slow_non_contiguous=True), reads=["HL"], is_output=True)
            for b in range(NSEQ):
                S.dma("sp", lambda h, b=b: h.dma_start(out=lrus_d[b].rearrange("(c p) -> p c", p=128), in_=HL[:, :, 1 + b],
                                                       allow_slow_non_contiguous=True), reads=["HL"], is_output=True)
            A.release(mA1)
            S.barrier()

        QT = KT = VT = None
        if "a2" in stages:
            QT = A.alloc("QT", [128, 8, L], BF16)
            KT = A.alloc("KT", [128, 8, L], BF16)
            VT = A.alloc("VT", [128, 8, L], BF16)
            stg = {"n": 0}
            for (which, colbase, dstT, dstS, nm) in (("q", 0, QT, QTs, "QT"), ("k", D, KT, KTs, "KT"), ("v", 2 * D, VT, VTs, "VT")):
                for half in range(2):
                    wb = load_w_in(colbase + 512 * half)
                    for cc in range(4):
                        j = 4 * half + cc

                        def cons_qkv(tbi, t0, tn, k, j=j, dstT=dstT, dstS=dstS, nm=nm):
                            dst = dstT[:, j, t0:t0 + tn] if tbi < 4 else dstS[:, j, :]
                            if (tbi + j) % 2 == 0:
                                S.op("act", lambda h: h.activation(out=dst, in_=ps[k][:, 0:tn], func=ACTF.Copy),
                                     reads=[psn[k]], writes=["%s%d_%d" % (nm, j, tbi)])
                            else:
                                S.op("dve", lambda h: h.tensor_copy(out=dst, in_=ps[k][:, 0:tn]),
                                     reads=[psn[k]], writes=["%s%d_%d" % (nm, j, tbi)])
                        proj_fm(wb, 128 * cc, cons_qkv)
                    if which in ("k", "v"):
                        o_p = kp_d if which == "k" else vp_d
                        o_s = ks_d if which == "k" else vs_d
                        for t in range(17):
                            rows = 128 if t < 16 else NS
                            k = 4 + stg["n"] % 2
                            sg = stg["n"] % 4
                            stg["n"] += 1
                            for kc in range(8):
                                S.op("pe", _mm(ps[k][0:rows, :], hT[:, kc, t * 128:t * 128 + rows], wbuf[wb][:, kc, :],
                                               kc == 0, kc == 7), reads=["wbuf%d" % wb, hTn[min(t // 4, 4)]], writes=[psn[k]])
                            if t % 2 == 0:
                                S.op("act", lambda h, sg=sg, rows=rows, k=k: h.activation(
                                    out=stage[sg][0:rows, :], in_=ps[k][0:rows, :], func=ACTF.Copy),
                                    reads=[psn[k]], writes=["stage%d" % sg])
                            else:
                                S.op("dve", lambda h, sg=sg, rows=rows, k=k: h.tensor_copy(
                                    out=stage[sg][0:rows, :], in_=ps[k][0:rows, :]), reads=[psn[k]], writes=["stage%d" % sg])
                            if t < 16:
                                S.dma("sp", lambda h, sg=sg, t=t, o_p=o_p, half=half: h.dma_start(
                                    out=o_p[t * 128:(t + 1) * 128, 512 * half:512 * half + 512], in_=stage[sg][:]),
                                    reads=["stage%d" % sg], is_output=True)
                            else:
                                for b in range(NSEQ):
                                    S.dma("sp", lambda h, sg=sg, b=b, o_s=o_s, half=half: h.dma_start(
                                        out=o_s[b, WB - TS:WB, 512 * half:512 * half + 512],
                                        in_=stage[sg][TS * b:TS * b + TS, :]), reads=["stage%d" % sg], is_output=True)
            mg = A.mark()
            for half in range(2):
                wb = load_w_in(3 * D + 512 * half)
                for cc in range(4):
                    c = 4 * half + cc
                    sb_i = c % 2

                    def cons_ga(tbi, t0, tn, k, sb_i=sb_i):
                        S.op("act", lambda h: h.activation(out=sgast[sb_i][:, t0:t0 + tn], in_=ps[k][:, 0:tn], func=ACTF.Silu),
                             reads=[psn[k]], writes=["sgast%d" % sb_i])
                    proj_fm(wb, 128 * cc, cons_ga)
                    S.dma("sp", lambda h, c=c, sb_i=sb_i: h.dma_start(out=sga_scr[c], in_=sgast[sb_i][:]),
                          reads=["sgast%d" % sb_i], writes=["sga_scr"])
            A.release(mg)
            S.barrier()

        if "b" in stages:
            for b in range(NSEQ if cache_copy else 0):
                for (src, dst) in ((ck_d, ks_d), (cv_d, vs_d)):
                    for half in range(2):
                        r0 = 1020 * half
                        S.dma("sp", lambda h, src=src, dst=dst, b=b, r0=r0: h.dma_start(
                            out=dst[b, r0:r0 + 1020, :], in_=src[b, TS + r0:TS + r0 + 1020, :]), is_output=True)

            S.op("pool", lambda h: h.memset(Vpair[:, :, 64:128], 1.0), writes=["Vpair"])
            sb_rot = {"n": 0}
            ob_rot = {"n": 0}
            pt_rot = {"n": 0}
            load_bt_pair(0, 0)
            assert nc.lookup_mloc(stage[1]).addr == nc.lookup_mloc(stage[0]).addr + 2048
            PT16 = nc.alloc_sbuf_tensor_at("PT16_al", [128, 2048], BF16, offset=nc.lookup_mloc(stage[0]).addr)
            PTB = list(PT) + [nc.alloc_sbuf_tensor_at("PTx%d" % i, [128, 512], BF16,
                                                      offset=nc.lookup_mloc(stage[2]).addr + 1024 * i) for i in range(2)]
            SBK = [0, 1, 2, 7]
            Eh = [nc.alloc_sbuf_tensor_at("Eh%d" % i, [128, 640], BF16, offset=nc.lookup_mloc(wbuf[0]).addr + 2048 * i)
                  for i in range(2)]
            EIDX = {(0, 0): 0, (0, 128): 1, (1, 0): 2, (1, 128): 3, (2, 128): 4}
            eh_rot = {"n": 0}

            def mask_mul(ptile, ncols, eidx_list, eb):
                nt_ = len(eidx_list)
                en = "Eh%d" % eb
                if nt_ == 4 and eidx_list[0] == eidx_list[2] and eidx_list[1] == eidx_list[3] and eidx_list[1] == eidx_list[0] + 1:
                    e0 = eidx_list[0]
                    in1 = bass.AP(Eh[eb], 128 * e0, [[640, 128], [0, 2], [1, 256]])
                    pv_ = ptile[:, 0:512].rearrange("p (a x) -> p a x", a=2)
                    S.op("dve", lambda h: h.tensor_tensor(out=pv_, in0=pv_, in1=in1, op=ALU.mult),
                         reads=[pn_(ptile), en], writes=[pn_(ptile)])
                elif nt_ == 4 and len(set(eidx_list)) == 1:
                    e0 = eidx_list[0]
                    in1 = bass.AP(Eh[eb], 128 * e0, [[640, 128], [0, 4], [1, 128]])
                    pv_ = ptile[:, 0:512].rearrange("p (a x) -> p a x", a=4)
                    S.op("dve", lambda h: h.tensor_tensor(out=pv_, in0=pv_, in1=in1, op=ALU.mult),
                         reads=[pn_(ptile), en], writes=[pn_(ptile)])
                else:
                    for ti_, e0 in enumerate(eidx_list):
                        S.op("dve", lambda h, ti_=ti_, e0=e0: h.tensor_tensor(
                            out=ptile[:, 128 * ti_:128 * ti_ + 128], in0=ptile[:, 128 * ti_:128 * ti_ + 128],
                            in1=Eh[eb][:, 128 * e0:128 * e0 + 128], op=ALU.mult), reads=[pn_(ptile), en], writes=[pn_(ptile)])

            pnames = {}
            pend = {"f": None, "age": 0}

            def flush_norm(force=False):
                if pend["f"] is not None:
                    pend["age"] += 1
                    if force or pend["age"] >= 2:
                        pend["f"]()
                        pend["f"] = None

            def pn_(t_):
                return pnames[id(t_)]

            def vtile_ap(arr, ti, hp, nkeys=128):
                return Vpair[0:nkeys, arr * 16 + ti, 64 * hp:64 * hp + 128]

            for j in range(8):
                bt = j % 2
                if j + 1 < 8:
                    load_bt_pair(j + 1, (j + 1) % 2)
                for arr in range(3):
                    for g in range(2):
                        pb = 5 + (arr * 2 + g) % 2
                        pv = ps[pb][:].bitcast(BF16)
                        for u in range(8):
                            ti = 8 * g + u
                            if arr == 0:
                                cols = VT[:, j, ti * 128:(ti + 1) * 128]
                            elif arr == 1:
                                r, n = ti // 4, ti % 4
                                cols = VT[:, j, 512 * n + r:512 * n + 512:4]
                            else:
                                cols = VT[:, j, ti:L:16]
                            S.op("pe", lambda h, pv=pv, u=u, cols=cols: h.transpose(
                                out=pv[:, u * 128:(u + 1) * 128], in_=cols, identity=identb[:]),
                                reads=["VT%d_%d" % (j, q) for q in range(4)] + ["identb"], writes=[psn[pb]])
                        t0_ = arr * 16 + 8 * g
                        vdst = Vpair[:, t0_:t0_ + 8, :].rearrange("p t (a f) -> p t a f", a=3)[:, :, 0:3:2, :]
                        vsrc = pv.rearrange("p (t a f) -> p t a f", t=8, a=2)
                        if (arr + g) % 2 == 0:
                            S.op("dve", lambda h, vdst=vdst, vsrc=vsrc: h.tensor_copy(out=vdst, in_=vsrc),
                                 reads=[psn[pb]], writes=["Vpair"])
                        else:
                            S.op("act", lambda h, vdst=vdst, vsrc=vsrc: h.activation(out=vdst, in_=vsrc, func=ACTF.Copy),
                                 reads=[psn[pb]], writes=["Vpair"])
                for hp in range(2):
                    P0 = 64 * hp
                    qn = lambda c: ["QT%d_%d" % (j, c)]
                    eb = eh_rot["n"] % 2
                    eh_rot["n"] += 1
                    sb_i = SBK[sb_rot["n"] % 4]
                    sb_rot["n"] += 1
                    Sb = ps[sb_i]
                    sb2_i = SBK[sb_rot["n"] % 4]
                    sb_rot["n"] += 1
                    Sb2 = ps[sb2_i]
                    for (bp, jj0), ei in EIDX.items():
                        tgt, tn_, e_ = (Sb, psn[sb_i], ei) if ei < 4 else (Sb2, psn[sb2_i], 0)
                        for half in range(2):
                            S.op("pe", _mm(tgt[:, 128 * e_ + 64 * half:128 * e_ + 64 * half + 64],
                                           BTp[bt][P0:P0 + 64, bp, half, jj0:jj0 + 128], jb[P0:P0 + 64, P0:P0 + 64],
                                           e_ == 0 and half == 0, (ei == 3 or ei == 4) and half == 1),
                                 reads=["BTp%d" % bt, "jb"], writes=[tn_])
                    S.op("act", lambda h, Sb=Sb, eb=eb: h.activation(
                        out=Eh[eb][:, 0:512], in_=Sb[:, 0:512], func=ACTF.Exp, scale=0.125),
                        reads=[psn[sb_i]], writes=["Eh%d" % eb])
                    S.op("act", lambda h, Sb2=Sb2, eb=eb: h.activation(
                        out=Eh[eb][:, 512:640], in_=Sb2[:, 0:128], func=ACTF.Exp, scale=0.125),
                        reads=[psn[sb2_i]], writes=["Eh%d" % eb])
                    for g16 in range(4):
                        sb_i = SBK[sb_rot["n"] % 4]
                        sb_rot["n"] += 1
                        Sb = ps[sb_i]
                        for u in range(4):
                            r = 4 * g16 + u
                            S.op("pe", _mm(Sb[:, 128 * u:128 * u + 128], KT[P0:P0 + 64, j, r:L:16], QT[P0:P0 + 64, j, r:L:16],
                                           u == 0, u == 3),
                                 reads=["KT%d_%d" % (j, q) for q in range(4)] + ["QT%d_%d" % (j, q) for q in range(4)],
                                 writes=[psn[sb_i]])
                        S.op("act", lambda h, Sb=Sb, g16=g16: h.activation(
                            out=PT16[:, 512 * g16:512 * g16 + 512], in_=Sb[:, :], func=ACTF.Exp, scale=0.125),
                            reads=[psn[sb_i]], writes=["PT16_%d" % g16])
                        in1 = bass.AP(Eh[eb], 128 * 4, [[640, 128], [0, 4], [1, 128]])
                        pv16 = PT16[:, 512 * g16:512 * g16 + 512].rearrange("p (a x) -> p a x", a=4)
                        S.op("dve", lambda h, pv16=pv16, in1=in1: h.tensor_tensor(out=pv16, in0=pv16, in1=in1, op=ALU.mult),
                             reads=["PT16_%d" % g16, "Eh%d" % eb], writes=["PT16_%d" % g16])
                        flush_norm()
                    for c in range(4):
                        groups = []
                        g1 = []
                        for u in range(4):
                            n = 4 * c + u
                            qa = QT[P0:P0 + 64, j, 128 * n:128 * n + 128]
                            oc = slice(128 * u, 128 * u + 128)
                            if n > 0:
                                g1.append((0, n - 1, KT[P0:P0 + 64, j, 128 * (n - 1):128 * n], qa, oc, (0, 0), 128, 128, 0))
                            g1.append((0, n, KT[P0:P0 + 64, j, 128 * n:128 * n + 128], qa, oc, (0, 128), 128, 128, 0))
                        groups.append(g1[:4])
                        if len(g1) > 4:
                            groups.append(g1[4:])
                        g2 = []
                        for r in range(4):
                            qa = QT[P0:P0 + 64, j, 512 * c + r:512 * c + 512:4]
                            oc = slice(r, 512, 4)
                            if c > 0:
                                g2.append((1, 4 * r + c - 1, KT[P0:P0 + 64, j, 512 * (c - 1) + r:512 * c:4], qa, oc, (1, 0), 128, 128, 0))
                            g2.append((1, 4 * r + c, KT[P0:P0 + 64, j, 512 * c + r:512 * c + 512:4], qa, oc, (1, 128), 128, 128, 0))
                        groups.append(g2[:4])
                        if len(g2) > 4:
                            groups.append(g2[4:])
                        g3 = []
                        nk = 32 * (c + 1)
                        for r in range(16):
                            qa = QT[P0:P0 + 64, j, r + 512 * c:512 * c + 512:16]
                            oc = slice(r, 512, 16)
                            g3.append((2, r, KT[P0:P0 + 64, j, r:min(L, r + 16 * nk):16], qa, oc, (2, 128), nk, 32, 32 * c))
                        groups.append(g3)

                        ob = 3 + ob_rot["n"] % 2
                        ob_rot["n"] += 1
                        first_o = True
                        for grp in groups:
                            if grp is groups[-1]:
                                for ti_, (arr, vt, ka, qa, oc, (bp, jj0), nkeys, nq, qoff) in enumerate(grp):
                                    lhs = vtile_ap(arr, vt, hp, nkeys)
                                    S.op("pe", _mm(ps[ob][:, oc], lhs, PT16[0:nkeys, 128 * vt + qoff:128 * vt + qoff + nq],
                                                   first_o, ti_ == len(grp) - 1),
                                         reads=["Vpair", "PT16_%d" % (vt // 4)], writes=[psn[ob]])
                                    first_o = False
                                continue
                            sb_i = SBK[sb_rot["n"] % 4]
                            sb_rot["n"] += 1
                            pt_i = pt_rot["n"] % 5
                            pt_rot["n"] += 1
                            Sb = ps[sb_i]
                            col = 0
                            cols_of = []
                            maxk = 0
                            for ti_, (arr, vt, ka, qa, oc, (bp, jj0), nkeys, nq, qoff) in enumerate(grp):
                                S.op("pe", _mm(Sb[0:nkeys, col:col + nq], ka, qa, ti_ == 0, ti_ == len(grp) - 1),
                                     reads=["KT%d_%d" % (j, q) for q in range(4)] + qn(c), writes=[psn[sb_i]])
                                cols_of.append(col)
                                col += nq
                                maxk = max(maxk, nkeys)
                            ncols = col
                            S.op("act", lambda h, Sb=Sb, pt_i=pt_i, maxk=maxk, ncols=ncols: h.activation(
                                out=PTB[pt_i][0:maxk, 0:ncols], in_=Sb[0:maxk, 0:ncols], func=ACTF.Exp, scale=0.125),
                                reads=[psn[sb_i]], writes=["PT%d" % pt_i])
                            pnames[id(PTB[pt_i])] = "PT%d" % pt_i
                            mask_mul(PTB[pt_i], ncols, [EIDX[(t_[5][0], t_[5][1])] for t_ in grp], eb)
                            flush_norm()
                            for ti_, (arr, vt, ka, qa, oc, (bp, jj0), nkeys, nq, qoff) in enumerate(grp):
                                lhs = vtile_ap(arr, vt, hp, nkeys)
                                is_last = (grp is groups[-1]) and ti_ == len(grp) - 1
                                S.op("pe", _mm(ps[ob][:, oc], lhs, PTB[pt_i][0:nkeys, cols_of[ti_]:cols_of[ti_] + nq],
                                               first_o, is_last), reads=["Vpair", "PT%d" % pt_i], writes=[psn[ob]])
                                first_o = False
                        rb = c % 2
                        D0 = 64 - P0

                        def norm(ob=ob, rb=rb, D0=D0, P0=P0, c=c, j=j):
                            S.op("dve", lambda h: h.reciprocal(out=rec[rb][P0:P0 + 64, :], in_=ps[ob][D0:D0 + 64, :]),
                                 reads=[psn[ob]], writes=["rec%d" % rb])
                            S.op("dve", lambda h: h.tensor_tensor(
                                out=OA[P0:P0 + 64, j, 512 * c:512 * c + 512], in0=ps[ob][P0:P0 + 64, :],
                                in1=rec[rb][P0:P0 + 64, :], op=ALU.mult), reads=[psn[ob], "rec%d" % rb], writes=["OA_%d" % c])
                        flush_norm(force=True)
                        pend["f"] = norm
                        pend["age"] = 0
            flush_norm(force=True)
            S.barrier()

        if "c" in stages:
            mC = A.mark()
            A.release(arena0)
            KH = 1024
            Kc = [A.alloc("Kc", [128, 8, D], BF16) for _ in range(2)]
            Vc = [A.alloc("Vc", [128, 8, D], BF16) for _ in range(2)]
            KcT = [A.alloc("KcT", [128, 8, KH], BF16) for _ in range(2)]
            BSb = A.alloc("BSb", [128, WB + TS], BF16)
            selb = A.alloc("selb", [128, 16, 8], BF16)
            Vn = [A.alloc("Vn", [8, D], BF16) for _ in range(2)]
            BSf = nc.alloc_sbuf_tensor_at("BSf_al", [128, WB + TS], F32, offset=nc.lookup_mloc(Vpair).addr)
            for hh in range(16):
                src = bass.AP(ts_scr.tensor, hh * TSL, [[1, 8], [1, WB + TS]])
                S.dma("sp", lambda h, src=src, hh=hh: h.dma_start(out=BSf[8 * hh:8 * hh + 8, :], in_=src),
                      reads=["ts_scr"], writes=["BSf"])
            S.op("dve", lambda h: h.tensor_copy(out=BSb[:], in_=BSf[:]), reads=["BSf"], writes=["BSb"])
            S.dma("pool", lambda h: h.dma_start(out=selb[:], in_=sel_d), writes=["selb"])
            selflat = selb[:].rearrange("p h t -> p (h t)")
            QPs = A.alloc("QPs", [128, 8, NSEQ, 16], BF16)
            onesf = A.alloc("onesf", [128, 128], F32)
            Pr = [A.alloc("Pr", [128, 16], F32) for _ in range(2)]
            recs = A.alloc("recs", [128, 8, 16], F32)
            S.op("pool", lambda h: h.memset(QPs[:], 0.0), writes=["QPs"])
            S.op("pool", lambda h: h.memset(onesf[:], 1.0), writes=["onesf"])
            for hp in range(2):
                S.op("pool", lambda h, hp=hp: h.tensor_copy(
                    out=QPs[64 * hp:64 * hp + 64, :, :, 8 * hp:8 * hp + 8],
                    in_=QTs[64 * hp:64 * hp + 64, :, :].rearrange("p j (b t) -> p j b t", b=NSEQ)),
                    reads=["QTs"], writes=["QPs"])
            rot = {"s": 0, "p": 0, "t": 0, "r": 0}

            def c_loads(sg):
                b, kh = sg // 2, sg % 2
                e = sg % 2
                for (src_d, dstt, nm) in ((ck_d, Kc, "Kc"), (cv_d, Vc, "Vc")):
                    S.dma("pool", lambda h, b=b, kh=kh, e=e, src_d=src_d, dstt=dstt: h.dma_start(
                        out=dstt[e][:].rearrange("p t f -> p (t f)"),
                        in_=src_d[b, KH * kh:KH * kh + KH, :].rearrange("(p t) f -> p (t f)", t=8),
                        max_dma_last_dim=8192), writes=["%s%d" % (nm, e)])

            def c_compute(sg):
                b, kh = sg // 2, sg % 2
                e = sg % 2
                vb = b % 2
                ob = 3 + b % 2
                ntile = 8 + kh
                if kh == 0:
                    for kc in range(8):
                        S.op("pe", lambda h, b=b, kc=kc: h.transpose(
                            out=ps[7][:].bitcast(BF16)[0:TS, kc * 128:(kc + 1) * 128],
                            in_=VTs[:, kc, TS * b:TS * b + TS], identity=identb[:]), reads=["VTs", "identb"], writes=[psn[7]])
                    S.op("dve", lambda h, vb=vb: h.tensor_copy(out=Vn[vb][:], in_=ps[7][:].bitcast(BF16)[0:TS, 0:D]),
                         reads=[psn[7]], writes=["Vn%d" % vb])
                for t in range(8):
                    pb = 5 + rot["t"] % 2
                    rot["t"] += 1
                    pv = ps[pb][:].bitcast(BF16)
                    for jj in range(8):
                        S.op("pe", lambda h, pv=pv, t=t, jj=jj, e=e: h.transpose(
                            out=pv[:, jj * 128:(jj + 1) * 128], in_=Kc[e][:, t, jj * 128:(jj + 1) * 128], identity=identb[:]),
                            reads=["Kc%d" % e, "identb"], writes=[psn[pb]])
                    dst = KcT[e][:, :, t * 128:(t + 1) * 128]
                    srcp = pv.rearrange("p (k t) -> p k t", k=8)
                    if t % 2 == 0:
                        S.op("act", lambda h, dst=dst, srcp=srcp: h.activation(out=dst, in_=srcp, func=ACTF.Copy),
                             reads=[psn[pb]], writes=["KcT%d" % e])
                    else:
                        S.op("dve", lambda h, dst=dst, srcp=srcp: h.tensor_copy(out=dst, in_=srcp),
                             reads=[psn[pb]], writes=["KcT%d" % e])
                for j in range(8):
                    sb_i = rot["s"] % 3
                    rot["s"] += 1
                    pt_i = rot["p"] % 3
                    rot["p"] += 1
                    Sb = ps[sb_i]
                    qa = QPs[:, j, b, :]
                    for t in range(ntile):
                        nk = 128 if t < 8 else TS
                        ka = KcT[e][:, j, t * 128:(t + 1) * 128] if t < 8 else KTs[:, j, TS * b:TS * b + TS]
                        S.op("pe", _mm(Sb[0:nk, t * 16:(t + 1) * 16], ka, qa, t == 0, False),
                             reads=["KcT%d" % e, "KTs", "QPs"], writes=[psn[sb_i]])
                    for t in range(ntile):
                        nk = 128 if t < 8 else TS
                        bl = BSb[:, KH * kh + t:KH * kh + KH:8] if t < 8 else BSb[:, WB:WB + TS]
                        S.op("pe", _mm(Sb[0:nk, t * 16:(t + 1) * 16], bl, selflat[:, 16 * j:16 * j + 16], False, t == ntile - 1),
                             reads=["BSb", "selb"], writes=[psn[sb_i]])
                    S.op("act", lambda h, Sb=Sb, pt_i=pt_i, ntile=ntile: h.activation(
                        out=PT[pt_i][:, 0:ntile * 16], in_=Sb[:, 0:ntile * 16], func=ACTF.Exp, scale=0.125),
                        reads=[psn[sb_i]], writes=["PT%d" % pt_i])
                    pr_i = rot["r"] % 2
                    rot["r"] += 1
                    S.op("dve", lambda h, pt_i=pt_i, pr_i=pr_i: h.tensor_reduce(
                        out=Pr[pr_i][:], in_=PT[pt_i][:, 0:128].rearrange("p (k q) -> p q k", q=16),
                        axis=mybir.AxisListType.X, op=ALU.add), reads=["PT%d" % pt_i], writes=["Pr%d" % pr_i])
                    first = (kh == 0 and j == 0)
                    for t in range(ntile):
                        nk = 128 if t < 8 else TS
                        lhs = Vc[e][:, t, 128 * j:128 * j + 128] if t < 8 else Vn[vb][:, 128 * j:128 * j + 128]
                        S.op("pe", _mm(ps[ob][:, 32 * j:32 * j + 16], lhs, PT[pt_i][0:nk, t * 16:(t + 1) * 16],
                                       first and t == 0, False),
                             reads=["Vc%d" % e, "Vn%d" % vb, "PT%d" % pt_i], writes=[psn[ob]])
                    S.op("pe", _mm(ps[ob][:, 32 * j + 16:32 * j + 32], onesf[:], Pr[pr_i][:], False, kh == 0),
                         reads=["onesf", "Pr%d" % pr_i], writes=[psn[ob]])
                    if kh == 1:
                        S.op("pe", _mm(ps[ob][:, 32 * j + 16:32 * j + 32], onesb[0:TS, :], PT[pt_i][0:TS, 128:144], False, True),
                             reads=["onesb", "PT%d" % pt_i], writes=[psn[ob]])
                if kh == 1:
                    Ov = ps[ob][:, 0:256].rearrange("p (j x) -> p j x", x=32)
                    S.op("dve", lambda h, Ov=Ov: h.reciprocal(out=recs[:], in_=Ov[:, :, 16:32]),
                         reads=[psn[ob]], writes=["recs"])
                    for hp in range(2):
                        P0 = 64 * hp
                        S.op("dve", lambda h, Ov=Ov, P0=P0, hp=hp, b=b: h.tensor_tensor(
                            out=OA[P0:P0 + 64, :, L + TS * b:L + TS * b + TS], in0=Ov[P0:P0 + 64, :, 8 * hp:8 * hp + 8],
                            in1=recs[P0:P0 + 64, :, 8 * hp:8 * hp + 8], op=ALU.mult),
                            reads=[psn[ob], "recs"], writes=["OA_4"])

            c_loads(0)
            for sg in range(2 * NSEQ):
                if sg + 1 < 2 * NSEQ:
                    c_loads(sg + 1)
                c_compute(sg)
            S.barrier()

        if debug:
            dbg_oa = nc.dram_tensor("dbg_oa", [128, 8, NT], BF16, kind="ExternalOutput").ap()
            dbg_yl = nc.dram_tensor("dbg_yl", [8, 128, NT], BF16, kind="ExternalOutput").ap()
            dbg_sg = nc.dram_tensor("dbg_sg", [8, 128, NT], BF16, kind="ExternalOutput").ap()
            S.dma("sp", lambda h: h.dma_start(out=dbg_oa, in_=OA[:]), is_output=True)
            S.dma("sp", lambda h: h.dma_start(out=dbg_yl, in_=yl_scr), is_output=True)
            S.dma("sp", lambda h: h.dma_start(out=dbg_sg, in_=sga_scr), is_output=True)
            S.barrier()

        if "d" in stages:
            A.release(arena0)
            wo = A.alloc("wo", [128, 16, D], BF16)
            gfin = A.alloc("gfin", [128, D], F32)
            YLb = [A.alloc("YLb", [128, 8, 512], BF16) for _ in range(2)]
            SGb = [A.alloc("SGb", [128, 8, 512], BF16) for _ in range(2)]
            YAb = [A.alloc("YAb", [128, 8, 512], BF16) for _ in range(2)]
            SQe = A.alloc("SQe", [128, 8, 512], BF16)
            xre = [A.alloc("xre", [128, D], F32) for _ in range(2)]
            yt = [A.alloc("yt", [128, D], F32) for _ in range(2)]
            junk2 = sgast[0]
            for q in range(4):
                S.dma("pool", lambda h, q=q: h.dma_start(
                    out=wo[:, 4 * q:4 * q + 4, :],
                    in_=wout_d[512 * q:512 * q + 512, :].rearrange("(kc p) n -> p kc n", p=128)), writes=["wo"])
            S.dma("sp", lambda h: h.dma_start(out=gfin[:], in_=gfin_d), writes=["gfin"])
            att_first = {"v": True}
            prot_d = {"n": 0}
            pending = {"f": None}
            S.op("dve", lambda h: h.tensor_scalar(
                out=small[:, SM_MSX:SM_MSX + 17], in0=small[:, SM_LRU:SM_LRU + 17],
                scalar1=1.0 / D, scalar2=EPS, op0=ALU.mult, op1=ALU.add), reads=["ssq_lru"], writes=["msl"])
            S.op("pool", lambda h: h.tensor_tensor(
                out=small[:, SM_RL:SM_RL + 17], in0=small[:, SM_MSX:SM_MSX + 17],
                in1=nhalf[:, 0:17], op=ALU.pow), reads=["msl", "nhalf"], writes=["rl"])
            for tbi, (t0, tn) in enumerate(TBLK):
                e = tbi % 2
                S.dma("sp", lambda h, e=e, t0=t0, tn=tn: h.dma_start(
                    out=YLb[e][:, :, 0:tn], in_=yl_scr[:, :, t0:t0 + tn].rearrange("c p t -> p c t")),
                    reads=["yl_scr"], writes=["YLb%d" % e])
                S.dma("sp", lambda h, e=e, t0=t0, tn=tn: h.dma_start(
                    out=SGb[e][:, :, 0:tn], in_=sga_scr[:, :, t0:t0 + tn].rearrange("c p t -> p c t")),
                    reads=["sga_scr"], writes=["SGb%d" % e])
                S.op("pool", lambda h, t0=t0, tn=tn: h.tensor_tensor(
                    out=SQe[:, :, 0:tn], in0=OA[:, :, t0:t0 + tn], in1=OA[:, :, t0:t0 + tn], op=ALU.mult),
                    reads=["OA_%d" % tbi], writes=["SQe"])
                ntile = (tn + 127) // 128
                tb0 = t0 // 128
                for tt in range(ntile):
                    rows = min(128, tn - 128 * tt)
                    t = tb0 + tt
                    for c in range(8):
                        S.op("pe", _mm(ps[7][0:rows, 32 + t:33 + t], SQe[:, c, 128 * tt:128 * tt + rows], onesb[:, 0:1],
                                       att_first["v"], c == 7), reads=["SQe", "onesb"], writes=[psn[7]])
                        att_first["v"] = False
                rws = 128 if tbi < 4 else NS
                S.op("dve", lambda h, rws=rws, tb0=tb0, ntile=ntile: h.tensor_scalar(
                    out=small[0:rws, SM_ATT + tb0:SM_ATT + tb0 + ntile], in0=ps[7][0:rws, 32 + tb0:32 + tb0 + ntile],
                    scalar1=1.0 / D, scalar2=EPS, op0=ALU.mult, op1=ALU.add), reads=[psn[7]], writes=["msa%d" % tbi])
                S.op("pool", lambda h, rws=rws, tb0=tb0, ntile=ntile: h.tensor_tensor(
                    out=small[0:rws, SM_RA + tb0:SM_RA + tb0 + ntile], in0=small[0:rws, SM_ATT + tb0:SM_ATT + tb0 + ntile],
                    in1=nhalf[0:rws, 0:ntile], op=ALU.pow), reads=["msa%d" % tbi, "nhalf"], writes=["ra%d" % tbi])
                for c in range(8):
                    S.op("dve", lambda h, e=e, c=c, t0=t0, tn=tn: h.scalar_tensor_tensor(
                        out=YAb[e][:, c, 0:tn], in0=OA[:, c, t0:t0 + tn], scalar=vecs[:, V_NATT + c:V_NATT + c + 1],
                        in1=SGb[e][:, c, 0:tn], op0=ALU.mult, op1=ALU.mult),
                        reads=["OA_%d" % tbi, "SGb%d" % e, "vecs"], writes=["YAb%d" % e])
                for tt in range(ntile):
                    rows = min(128, tn - 128 * tt)
                    t = tb0 + tt
                    xb = t % 2
                    src = xp_d[t * 128:(t + 1) * 128, :] if t < 16 else xs_d
                    S.dma("sp", lambda h, xb=xb, rows=rows, src=src: h.dma_start(out=xre[xb][0:rows, :], in_=src),
                          writes=["xre%d" % xb])
                    yb = t % 2
                    for hf in range(2):
                        ka = prot_d["n"] % 4
                        kl = (prot_d["n"] + 1) % 4
                        prot_d["n"] += 2
                        for kc in range(8):
                            S.op("pe", _mm(ps[ka][0:rows, :], YAb[e][:, kc, 128 * tt:128 * tt + rows],
                                           wo[:, kc, 512 * hf:512 * hf + 512], kc == 0, kc == 7),
                                 reads=["YAb%d" % e, "wo"], writes=[psn[ka]])
                        for kc in range(8):
                            S.op("pe", _mm(ps[kl][0:rows, :], YLb[e][:, kc, 128 * tt:128 * tt + rows],
                                           wo[:, 8 + kc, 512 * hf:512 * hf + 512], kc == 0, kc == 7),
                                 reads=["YLb%d" % e, "wo"], writes=[psn[kl]])
                        S.op("dve", lambda h, rows=rows, t=t, ka=ka, yb=yb, xb=xb, hf=hf: h.scalar_tensor_tensor(
                            out=yt[yb][0:rows, 512 * hf:512 * hf + 512], in0=ps[ka][0:rows, :],
                            scalar=small[0:rows, SM_RA + t:SM_RA + t + 1], in1=xre[xb][0:rows, 512 * hf:512 * hf + 512],
                            op0=ALU.mult, op1=ALU.add), reads=[psn[ka], "ra%d" % tbi, "xre%d" % xb], writes=["yt%d" % yb])
                        S.op("dve", lambda h, rows=rows, t=t, kl=kl, yb=yb, hf=hf: h.scalar_tensor_tensor(
                            out=yt[yb][0:rows, 512 * hf:512 * hf + 512], in0=ps[kl][0:rows, :],
                            scalar=small[0:rows, SM_RL + t:SM_RL + t + 1], in1=yt[yb][0:rows, 512 * hf:512 * hf + 512],
                            op0=ALU.mult, op1=ALU.add), reads=[psn[kl], "rl", "yt%d" % yb], writes=["yt%d" % yb])
                    S.op("act", lambda h, rows=rows, t=t, yb=yb: h.activation(
                        out=junk2[0:rows, 0:D], in_=yt[yb][0:rows, :], func=ACTF.Square,
                        accum_out=small[0:rows, SM_SSY + t:SM_SSY + t + 1]), reads=["yt%d" % yb], writes=["junk2", "ssy%d" % t])
                    S.op("dve", lambda h, rows=rows, t=t: h.tensor_scalar(
                        out=small[0:rows, SM_SSQX + t:SM_SSQX + t + 1], in0=small[0:rows, SM_SSY + t:SM_SSY + t + 1],
                        scalar1=1.0 / D, scalar2=EPS, op0=ALU.mult, op1=ALU.add), reads=["ssy%d" % t], writes=["msy%d" % t])
                    S.op("pool", lambda h, rows=rows, t=t: h.tensor_tensor(
                        out=small[0:rows, SM_RSY + t:SM_RSY + t + 1], in0=small[0:rows, SM_SSQX + t:SM_SSQX + t + 1],
                        in1=nhalf[0:rows, 0:1], op=ALU.pow), reads=["msy%d" % t, "nhalf"], writes=["rsy%d" % t])

                    def final(rows=rows, t=t, yb=yb):
                        S.op("dve", lambda h: h.scalar_tensor_tensor(
                            out=yt[yb][0:rows, :], in0=yt[yb][0:rows, :], scalar=small[0:rows, SM_RSY + t:SM_RSY + t + 1],
                            in1=gfin[0:rows, :], op0=ALU.mult, op1=ALU.mult), reads=["yt%d" % yb, "rsy%d" % t, "gfin"],
                            writes=["yt%d" % yb])
                        dst = yp_d[t * 128:(t + 1) * 128, :] if t < 16 else ys_d
                        S.dma("sp", lambda h: h.dma_start(out=dst, in_=yt[yb][0:rows, :]),
                              reads=["yt%d" % yb], is_output=True)
                    if pending["f"] is not None:
                        pending["f"]()
                    pending["f"] = final
            if pending["f"] is not None:
                pending["f"]()

        S.finish()
        S.replay()
    return nc


def _cols(v):
    return np.ascontiguousarray(np.asarray(v, np.float32).reshape(8, 128).T)


def _blockdiag(w):
    out = np.zeros((128, 8, 128), np.float32)
    w = np.asarray(w, np.float32)
    for c in range(8):
        out[0:64, c, 0:64] = w[2 * c]
        out[64:128, c, 64:128] = w[2 * c + 1]
    return out


_CONSTS = None
_NC_CACHE = {}


def make_in_maps(inputs):
    global _CONSTS
    if _CONSTS is None:
        _CONSTS = _structural_constants()
    cs = _CONSTS
    f = lambda a: np.asarray(a, np.float32)
    vecs = np.concatenate(
        [_cols(f(inputs["norm_in"])[0]), _cols(f(inputs["norm_attn"])[0]), _cols(f(inputs["norm_lru"])[0])]
        + [_cols(f(inputs["conv_w"])[0, tap]) for tap in range(4)]
        + [_cols(f(inputs["conv_b"])[0]), _cols(f(inputs["b_gate_x"])[0]), _cols(f(inputs["b_gate_a"])[0]),
           _cols(f(inputs["lru_param"])[0])], axis=1)
    assert vecs.shape == (128, NVEC)
    shared = dict(
        w_in=np.ascontiguousarray(f(inputs["w_in"])[0]),
        w_out=np.ascontiguousarray(f(inputs["w_out"])[0]),
        vecs=np.ascontiguousarray(vecs),
        ginrow=np.ascontiguousarray(np.broadcast_to(f(inputs["norm_in"])[0][None, :], (128, D))),
        gfinrow=np.ascontiguousarray(np.broadcast_to(f(inputs["norm_final"])[None, :], (128, D))),
        wgx=_blockdiag(f(inputs["w_gate_x"])[0]),
        wga=_blockdiag(f(inputs["w_gate_a"])[0]),
        rb_aug=np.ascontiguousarray(np.concatenate([f(inputs["rel_bias"]), cs["rbc"]], axis=0)),
        ohp=cs["ohp"], ohs=cs["ohs"], ident=cs["ident"], jm=cs["jm"], sel=cs["sel"],
    )
    xp = f(inputs["x_prompt"])
    xs = f(inputs["x_sample"])
    ck = f(inputs["cache_win_k"])[0]
    cv = f(inputs["cache_win_v"])[0]
    sc = f(inputs["state_conv"])[0]
    sl = f(inputs["state_lru"])[0]
    maps = []
    for c in range(NCORES):
        bs = slice(NSEQ * c, NSEQ * c + NSEQ)
        m = dict(shared)
        m["xp"] = np.ascontiguousarray(xp[c])
        m["xs"] = np.ascontiguousarray(xs[bs].reshape(NS, D))
        m["ck"] = np.ascontiguousarray(ck[bs].reshape(NSEQ, WB, D))
        m["cv"] = np.ascontiguousarray(cv[bs].reshape(NSEQ, WB, D))
        m["sconv"] = np.ascontiguousarray(sc[bs].reshape(NSEQ, 3, 8, 128).transpose(3, 2, 0, 1))
        m["slru"] = np.ascontiguousarray(sl[bs].reshape(NSEQ, 8, 128).transpose(2, 1, 0))
        maps.append(m)
    return maps


def assemble(results):
    cat = lambda k: np.concatenate([np.asarray(r[k]) for r in results], axis=0)
    y_p = np.stack([np.asarray(r["yp"]) for r in results], axis=0)
    y_s = cat("ys").reshape(NCORES * NSEQ, TS, D)
    kp = np.stack([np.asarray(r["kp"]) for r in results], axis=0).reshape(1, NCORES, L, 16, 64)
    vp = np.stack([np.asarray(r["vp"]) for r in results], axis=0).reshape(1, NCORES, L, 16, 64)
    convp = np.stack([np.asarray(r["convp"]) for r in results], axis=0).reshape(1, NCORES, 3, D)
    lrup = np.stack([np.asarray(r["lrup"]) for r in results], axis=0).reshape(1, NCORES, D)
    ks = cat("ks").reshape(1, NCORES * NSEQ, WB, 16, 64)
    vs = cat("vs").reshape(1, NCORES * NSEQ, WB, 16, 64)
    convs = cat("convs").reshape(1, NCORES * NSEQ, 3, D)
    lrus = cat("lrus").reshape(1, NCORES * NSEQ, D)
    return tuple(np.ascontiguousarray(a, dtype=np.float32)
                 for a in (y_p, y_s, kp, vp, convp, lrup, ks, vs, convs, lrus))


def kernel(**inputs):
    maps = make_in_maps(inputs)
    nc = build_nc()
    res = run_bass_kernel_spmd(nc, maps, core_ids=list(range(NCORES)))
    return assemble(res.results)
```

```python
import math
from contextlib import ExitStack

import numpy as np
import concourse.bass as bass
import concourse.mybir as mybir
from concourse.bass_utils import run_bass_kernel_spmd

F32 = mybir.dt.float32
BF16 = mybir.dt.bfloat16
ALU = mybir.AluOpType
ACTF = mybir.ActivationFunctionType

NCORES = 8
D = 1024
L = 2048
NS = 32
NT = L + NS
NSEQ = 4
TS = 8
WB = 2048
DPROJ = 6144
EPS = 1e-6
NEG = -30000.0
TBLK = [(0, 512), (512, 512), (1024, 512), (1536, 512), (2048, NS)]
TPL = 383
TSL = 2063
ENGS = ("pe", "act", "dve", "pool", "sp")


class Sched:
    NDMA = 28

    def __init__(self, nc, stack):
        self.nc = nc
        self.q = {e: [] for e in ENGS}
        self.sem = {e: stack.enter_context(nc.semaphore("c_" + e)) for e in ENGS}
        self.cnt = {e: 0 for e in ENGS}
        self.seen = {e: {} for e in ENGS}
        self.dsem = [stack.enter_context(nc.semaphore("d%d" % i)) for i in range(self.NDMA)]
        self.dval = [0] * self.NDMA
        self.dnext = 0
        self.lastw = {}
        self.readers = {}
        self.out_events = []

    def _deps(self, reads, writes):
        ev = []
        for r in reads:
            w = self.lastw.get(r)
            if w is not None:
                ev.append(w)
        for w_ in writes:
            w = self.lastw.get(w_)
            if w is not None:
                ev.append(w)
            ev.extend(self.readers.get(w_, ()))
        return ev

    def _emit_waits(self, eng, events):
        need = {}
        for ev in events:
            if ev[0] == "dma":
                key = ("dma", ev[1])
                val = ev[2]
            else:
                if ev[0] == eng and eng in ("pe", "sp"):
                    continue
                key = ev[0]
                val = ev[1]
            if self.seen[eng].get(key, 0) >= val:
                continue
            if need.get(key, 0) < val:
                need[key] = val
        for key, val in need.items():
            self.seen[eng][key] = val
            sem = self.dsem[key[1]] if isinstance(key, tuple) else self.sem[key]
            self.q[eng].append(("wait", sem, val))

    def _record(self, reads, writes, event):
        for r in reads:
            self.readers.setdefault(r, []).append(event)
        for w in writes:
            self.lastw[w] = event
            self.readers[w] = []

    def op(self, eng, fn, reads=(), writes=()):
        self._emit_waits(eng, self._deps(reads, writes))
        self.cnt[eng] += 1
        event = (eng, self.cnt[eng])
        self.q[eng].append(("op", fn, self.sem[eng], 1))
        self._record(reads, writes, event)
        return event

    def dma(self, eng, fn, reads=(), writes=(), is_output=False):
        k = self.dnext
        self.dnext = (self.dnext + 1) % self.NDMA
        events = self._deps(reads, writes)
        if self.dval[k] > 0:
            events.append(("dma", k, self.dval[k]))
        self._emit_waits(eng, events)
        self.dval[k] += 16
        event = ("dma", k, self.dval[k])
        self.q[eng].append(("op", fn, self.dsem[k], 16))
        self._record(reads, writes, event)
        if is_output:
            self.out_events.append(event)
        return event

    def all_events(self):
        evs = []
        for k in range(self.NDMA):
            if self.dval[k] > 0:
                evs.append(("dma", k, self.dval[k]))
        for e in ENGS:
            if self.cnt[e] > 0:
                evs.append((e, self.cnt[e]))
        return evs

    def barrier(self):
        evs = self.all_events()
        for e in ENGS:
            self._emit_waits(e, [x for x in evs if x[0] != e or e not in ("pe", "sp")])
        self.lastw = {}
        self.readers = {}

    def finish(self):
        self._emit_waits("sp", self.all_events())

    def replay(self):
        nc = self.nc
        sched = self

        def run(name, h):
            for item in sched.q[name]:
                if item[0] == "wait":
                    h.wait_ge(item[1], item[2])
                else:
                    item[1](h).then_inc(item[2], item[3])

        with nc.Block() as block:
            @block.tensor
            def _(h):
                run("pe", h)

            @block.scalar
            def _(h):
                run("act", h)

            @block.vector
            def _(h):
                run("dve", h)

            @block.gpsimd
            def _(h):
                run("pool", h)

            @block.sync
            def _(h):
                run("sp", h)


class Arena:
    def __init__(self, nc, base=16512, end=229376):
        self.nc = nc
        self.cur = base
        self.end = end
        self.n = 0

    def mark(self):
        return self.cur

    def release(self, m):
        self.cur = m

    def alloc(self, name, shape, dt):
        nbytes = int(np.prod(shape[1:])) * (2 if dt == BF16 else 4)
        off = (self.cur + 63) // 64 * 64
        assert off + nbytes <= self.end, (name, off, nbytes, self.end)
        self.cur = off + nbytes
        self.n += 1
        return self.nc.alloc_sbuf_tensor_at("%s_%d" % (name, self.n), shape, dt, offset=off)


def _t5_bucket(dist):
    dist = np.asarray(dist)
    nf = np.maximum(dist, 1).astype(np.float32)
    large = 16 + (np.log(nf / np.float32(16)) / np.float32(math.log(2048 / 16)) * np.float32(16)).astype(np.int32)
    large = np.minimum(large, 31)
    return np.where(dist < 16, dist, large)


def _structural_constants():
    bucket = _t5_bucket
    ohp = np.zeros((35, 3, TPL), np.float32)
    for p, dil in enumerate((1, 4, 16)):
        for x in range(TPL):
            diff = 255 - x
            if 0 <= diff <= 128:
                ohp[int(bucket(np.array([diff * dil]))[0]), p, x] = 8.0
            else:
                ohp[32, p, x] = 1.0
    ohs = np.zeros((35, TSL), np.float32)
    dists = 2055 - np.arange(TSL)
    bks = bucket(np.maximum(dists, 0))
    for y in range(TSL):
        dist = int(dists[y])
        mult = 0
        if dist >= 0:
            mult += 1 if dist <= 128 else 0
            mult += 1 if (dist % 4 == 0 and dist <= 512) else 0
            mult += 1 if (dist % 16 == 0 and dist <= 2048) else 0
        if mult == 0:
            ohs[32, y] = 1.0
        else:
            ohs[int(bks[y]), y] = 8.0
            if mult == 2:
                ohs[33, y] = 8.0
            elif mult == 3:
                ohs[34, y] = 8.0
    rbc = np.zeros((3, 16), np.float32)
    rbc[0] = NEG
    rbc[1] = math.log(2.0)
    rbc[2] = math.log(3.0)
    ident = np.eye(128, dtype=np.float32)
    jm = np.zeros((128, 128), np.float32)
    for m in range(128):
        jm[m, (m // 64) * 64 + 63 - (m % 64)] = 1.0
    sel = np.zeros((128, 16, 8), np.float32)
    for h in range(16):
        for tp in range(8):
            sel[8 * h + tp, h, 7 - tp] = 1.0
    return dict(ohp=ohp, ohs=ohs, rbc=rbc, ident=ident, jm=jm, sel=sel)


V_NIN, V_NATT, V_NLRU, V_CW, V_CB, V_BGX, V_BGA, V_LAM = 0, 8, 16, 24, 56, 64, 72, 80
NVEC = 88


def _mm(out, lhsT, rhs, start, stop):
    return lambda h: h.matmul(out, lhsT=lhsT, rhs=rhs, start=start, stop=stop)


def build_nc(stages=("a0", "a1", "a2", "b", "c", "d"), debug=False, cache_copy=True):
    nc = bass.Bass("TRN2", target_bir_lowering=False)
    di = lambda name, shape: nc.dram_tensor(name, shape, F32, kind="ExternalInput").ap()
    do = lambda name, shape: nc.dram_tensor(name, shape, F32, kind="ExternalOutput").ap()
    xp_d = di("xp", [L, D])
    xs_d = di("xs", [NS, D])
    ck_d = di("ck", [NSEQ, WB, D])
    cv_d = di("cv", [NSEQ, WB, D])
    sconv_d = di("sconv", [128, 8, NSEQ, 3])
    slru_d = di("slru", [128, 8, NSEQ])
    win_d = di("w_in", [D, DPROJ])
    wout_d = di("w_out", [2 * D, D])
    vecs_d = di("vecs", [128, NVEC])
    ginr_d = di("ginrow", [128, D])
    gfin_d = di("gfinrow", [128, D])
    wgx_d = di("wgx", [128, 8, 128])
    wga_d = di("wga", [128, 8, 128])
    rba_d = di("rb_aug", [35, 16])
    ohp_d = di("ohp", [35, 3, TPL])
    ohs_d = di("ohs", [35, TSL])
    id_d = di("ident", [128, 128])
    jm_d = di("jm", [128, 128])
    sel_d = di("sel", [128, 16, 8])

    yp_d = do("yp", [L, D])
    ys_d = do("ys", [NS, D])
    kp_d = do("kp", [L, D])
    vp_d = do("vp", [L, D])
    convp_d = do("convp", [3, D])
    lrup_d = do("lrup", [D])
    ks_d = do("ks", [NSEQ, WB, D])
    vs_d = do("vs", [NSEQ, WB, D])
    convs_d = do("convs", [NSEQ, 3, D])
    lrus_d = do("lrus", [NSEQ, D])

    tp_scr = nc.dram_tensor("tp_scr", [3, 16, TPL], F32, kind="Internal").ap()
    ts_scr = nc.dram_tensor("ts_scr", [16, TSL], F32, kind="Internal").ap()
    yl_scr = nc.dram_tensor("yl_scr", [8, 128, NT], BF16, kind="Internal").ap()
    sga_scr = nc.dram_tensor("sga_scr", [8, 128, NT], BF16, kind="Internal").ap()

    with ExitStack() as st:
        S = Sched(nc, st)
        A = Arena(nc)
        ps = [nc.alloc_psum_tensor("ps%d" % i, [128, 512], F32) for i in range(8)]
        psn = ["ps%d" % i for i in range(8)]

        identb = A.alloc("identb", [128, 128], BF16)
        jb = A.alloc("jb", [128, 128], BF16)
        onesb = A.alloc("onesb", [128, 128], BF16)
        vecs = A.alloc("vecs", [128, NVEC], F32)
        dcol = A.alloc("dcol", [128, 64], F32)
        HL = A.alloc("HL", [128, 8, 5], F32)
        small = A.alloc("small", [128, 256], F32)
        nhalf = A.alloc("nhalf", [128, 32], F32)
        hT = A.alloc("hT", [128, 8, NT], BF16)
        OA = hT
        wbuf = [A.alloc("wbuf", [128, 8, 512], BF16) for _ in range(2)]
        QTs = A.alloc("QTs", [128, 8, NS], BF16)
        KTs = A.alloc("KTs", [128, 8, NS], BF16)
        VTs = A.alloc("VTs", [128, 8, NS], BF16)
        PT = [A.alloc("PT", [128, 512], BF16) for _ in range(3)]
        BTp = [A.alloc("BTp", [128, 3, 2, 256], BF16) for _ in range(2)]
        Vpair = A.alloc("Vpair", [128, 48, 192], BF16)
        stage = [A.alloc("stage", [128, 512], F32) for _ in range(4)]
        sgast = [A.alloc("sgast", [128, NT], BF16) for _ in range(2)]
        rec = [A.alloc("rec", [128, 512], F32) for _ in range(2)]
        arena0 = A.mark()

        SM_SSQX, SM_MSX, SM_RSTDX = 0, 20, 40
        SM_LRU, SM_ATT = 60, 80
        SM_RL, SM_RA = 100, 120
        SM_SSY, SM_RSY = 140, 160
        DC_HBGX, DC_HBGA, DC_HC, DC_C, DC_GL05, DC_GA05 = 0, 8, 16, 24, 32, 40

        S.dma("pool", lambda h: h.dma_start(out=identb[:], in_=id_d), writes=["identb"])
        S.dma("pool", lambda h: h.dma_start(out=jb[:], in_=jm_d), writes=["jb"])
        S.dma("sp", lambda h: h.dma_start(out=vecs[:], in_=vecs_d), writes=["vecs"])
        S.op("pool", lambda h: h.memset(onesb[:], 1.0), writes=["onesb"])
        S.op("pool", lambda h: h.memset(nhalf[:], -0.5), writes=["nhalf"])
        S.op("pool", lambda h: h.memset(HL[:], 0.0), writes=["HL"])
        S.op("dve", lambda h: h.tensor_scalar(out=dcol[:, DC_HBGX:DC_HBGX + 8], in0=vecs[:, V_BGX:V_BGX + 8],
                                              scalar1=0.5, scalar2=None, op0=ALU.mult), reads=["vecs"], writes=["dc_hbgx"])
        S.op("dve", lambda h: h.tensor_scalar(out=dcol[:, DC_HBGA:DC_HBGA + 8], in0=vecs[:, V_BGA:V_BGA + 8],
                                              scalar1=0.5, scalar2=None, op0=ALU.mult), reads=["vecs"], writes=["dc_hbga"])
        S.op("dve", lambda h: h.tensor_scalar(out=dcol[:, DC_GL05:DC_GL05 + 8], in0=vecs[:, V_NLRU:V_NLRU + 8],
                                              scalar1=0.5, scalar2=None, op0=ALU.mult), reads=["vecs"], writes=["dc_gl"])
        S.op("dve", lambda h: h.tensor_scalar(out=dcol[:, DC_GA05:DC_GA05 + 8], in0=vecs[:, V_NATT:V_NATT + 8],
                                              scalar1=0.5, scalar2=None, op0=ALU.mult), reads=["vecs"], writes=["dc_ga"])
        S.op("act", lambda h: h.activation(out=dcol[:, 48:56], in_=vecs[:, V_LAM:V_LAM + 8], func=ACTF.Exp, scale=-1.0),
             reads=["vecs"], writes=["dc_t0"])
        S.op("act", lambda h: h.activation(out=dcol[:, 56:64], in_=dcol[:, 48:56], func=ACTF.Ln, bias=1.0),
             reads=["dc_t0"], writes=["dc_t1"])
        S.op("dve", lambda h: h.tensor_scalar(out=dcol[:, DC_C:DC_C + 8], in0=dcol[:, 56:64], scalar1=-8.0, scalar2=None,
                                              op0=ALU.mult), reads=["dc_t1"], writes=["dc_c"])
        S.op("dve", lambda h: h.tensor_scalar(out=dcol[:, DC_HC:DC_HC + 8], in0=dcol[:, 56:64], scalar1=-4.0, scalar2=None,
                                              op0=ALU.mult), reads=["dc_t1"], writes=["dc_hc"])

        m0 = A.mark()
        rba = A.alloc("rba", [35, 16], F32)
        ohp = A.alloc("ohp", [35, 3, TPL], F32)
        ohs = A.alloc("ohs", [35, TSL], F32)
        tps = A.alloc("tps", [16, 3, TPL], F32)
        tss = A.alloc("tss", [16, TSL], F32)
        S.dma("sp", lambda h: h.dma_start(out=rba[:], in_=rba_d), writes=["rba"])
        S.dma("sp", lambda h: h.dma_start(out=ohp[:], in_=ohp_d), writes=["ohp"])
        S.dma("sp", lambda h: h.dma_start(out=ohs[:], in_=ohs_d), writes=["ohs"])
        for p in range(3):
            S.op("pe", _mm(ps[6][0:16, 0:TPL], rba[:], ohp[:, p, :], True, True), reads=["rba", "ohp"], writes=[psn[6]])
            S.op("act", lambda h, p=p: h.activation(out=tps[:, p, :], in_=ps[6][0:16, 0:TPL], func=ACTF.Copy),
                 reads=[psn[6]], writes=["tps"])
        for i, c0 in enumerate(range(0, TSL, 512)):
            n = min(512, TSL - c0)
            S.op("pe", _mm(ps[6][0:16, 0:n], rba[:], ohs[:, c0:c0 + n], True, True), reads=["rba", "ohs"], writes=[psn[6]])
            S.op("act", lambda h, c0=c0, n=n: h.activation(out=tss[:, c0:c0 + n], in_=ps[6][0:16, 0:n], func=ACTF.Copy),
                 reads=[psn[6]], writes=["tss"])
        S.dma("sp", lambda h: h.dma_start(out=tp_scr.rearrange("p h x -> h p x"), in_=tps[:]), reads=["tps"], writes=["tp_scr"])
        S.dma("sp", lambda h: h.dma_start(out=ts_scr, in_=tss[:]), reads=["tss"], writes=["ts_scr"])

        def load_bt_pair(j, buf):
            for hp in range(2):
                for half in range(2):
                    src = bass.AP(tp_scr.tensor, (2 * j + hp) * TPL + 64 - 64 * half,
                                  [[1, 64], [16 * TPL, 3], [1, 256]])
                    S.dma("pool", lambda h, src=src, hp=hp, half=half, buf=buf: h.dma_start(
                        out=BTp[buf][hp * 64:hp * 64 + 64, :, half, :], in_=src),
                        reads=["tp_scr"], writes=["BTp%d" % buf])

        wstate = {"n": 0}

        def load_w_in(col0):
            b = wstate["n"] % 2
            wstate["n"] += 1
            src = win_d.rearrange("(kc p) n -> p kc n", p=128)[:, :, col0:col0 + 512]
            S.dma("pool", lambda h, b=b, src=src: h.dma_start(out=wbuf[b][:], in_=src), writes=["wbuf%d" % b])
            return b

        if "a0" in stages:
            mA0 = A.mark()
            xin = [A.alloc("xin", [128, D], F32) for _ in range(2)]
            xsb = [A.alloc("xsb", [128, D], BF16) for _ in range(2)]
            ginr = A.alloc("ginr", [128, D], F32)
            junk = A.alloc("junk", [128, D], BF16)
            S.dma("sp", lambda h: h.dma_start(out=ginr[:], in_=ginr_d), writes=["ginr"])
            for t in range(17):
                b = t % 2
                rows = 128 if t < 16 else NS
                src = xp_d[t * 128:(t + 1) * 128, :] if t < 16 else xs_d
                S.dma("sp", lambda h, b=b, rows=rows, src=src: h.dma_start(out=xin[b][0:rows, :], in_=src), writes=["xin%d" % b])
                S.op("act", lambda h, b=b, rows=rows, t=t: h.activation(
                    out=junk[0:rows, :], in_=xin[b][0:rows, :], func=ACTF.Square,
                    accum_out=small[0:rows, SM_SSQX + t:SM_SSQX + t + 1]), reads=["xin%d" % b], writes=["junk", "ssqx%d" % t])
                S.op("dve", lambda h, rows=rows, t=t: h.tensor_scalar(
                    out=small[0:rows, SM_MSX + t:SM_MSX + t + 1], in0=small[0:rows, SM_SSQX + t:SM_SSQX + t + 1],
                    scalar1=1.0 / D, scalar2=EPS, op0=ALU.mult, op1=ALU.add), reads=["ssqx%d" % t], writes=["msx%d" % t])
                S.op("pool", lambda h, rows=rows, t=t: h.tensor_tensor(
                    out=small[0:rows, SM_RSTDX + t:SM_RSTDX + t + 1], in0=small[0:rows, SM_MSX + t:SM_MSX + t + 1],
                    in1=nhalf[0:rows, 0:1], op=ALU.pow), reads=["msx%d" % t, "nhalf"], writes=["rstdx%d" % t])
                S.op("dve", lambda h, b=b, rows=rows, t=t: h.scalar_tensor_tensor(
                    out=xsb[b][0:rows, :], in0=xin[b][0:rows, :], scalar=small[0:rows, SM_RSTDX + t:SM_RSTDX + t + 1],
                    in1=ginr[0:rows, :], op0=ALU.mult, op1=ALU.mult),
                    reads=["xin%d" % b, "rstdx%d" % t, "ginr"], writes=["xsb%d" % b])
                pb = t % 2
                pv = ps[pb][:].bitcast(BF16)
                for kc in range(8):
                    S.op("pe", lambda h, pv=pv, b=b, rows=rows, kc=kc: h.transpose(
                        out=pv[:, kc * 128:kc * 128 + rows], in_=xsb[b][0:rows, kc * 128:(kc + 1) * 128],
                        identity=identb[0:rows, 0:rows]), reads=["xsb%d" % b, "identb"], writes=[psn[pb]])
                blk = min(t // 4, 4)
                dst = hT[:, :, t * 128:t * 128 + rows]
                srcp = pv.rearrange("p (k t) -> p k t", k=8)[:, :, 0:rows]
                if t % 2 == 0:
                    S.op("act", lambda h, dst=dst, srcp=srcp: h.activation(out=dst, in_=srcp, func=ACTF.Copy),
                         reads=[psn[pb]], writes=["hT%d" % blk])
                else:
                    S.op("dve", lambda h, dst=dst, srcp=srcp: h.tensor_copy(out=dst, in_=srcp),
                         reads=[psn[pb]], writes=["hT%d" % blk])
        A.release(arena0)
        S.barrier()

        hTn = ["hT%d" % i for i in range(5)]

        prot = {"n": 0}

        def proj_fm(wb, wcol, consume):
            for tbi, (t0, tn) in enumerate(TBLK):
                k = prot["n"] % 4
                prot["n"] += 1
                for kc in range(8):
                    S.op("pe", _mm(ps[k][:, 0:tn], wbuf[wb][:, kc, wcol:wcol + 128], hT[:, kc, t0:t0 + tn],
                                   kc == 0, kc == 7), reads=["wbuf%d" % wb, hTn[tbi]], writes=[psn[k]])
                consume(tbi, t0, tn, k)

        if "a1" in stages:
            mA1 = A.mark()
            wgx = A.alloc("wgx", [128, 8, 128], BF16)
            wga = A.alloc("wga", [128, 8, 128], BF16)
            XPq = [A.alloc("XPq", [128, NT + 3], F32) for _ in range(2)]
            XPs = [A.alloc("XPs", [128, NSEQ, 3 + TS], F32) for _ in range(2)]
            XCh = [A.alloc("XCh", [128, NT], F32) for _ in range(2)]
            XCb = [A.alloc("XCb", [128, NT], BF16) for _ in range(2)]
            THx = [A.alloc("THx", [128, NT], F32) for _ in range(2)]
            THa = [A.alloc("THa", [128, NT], F32) for _ in range(2)]
            SQb = [A.alloc("SQb", [128, NT], BF16) for _ in range(2)]
            OLb = [A.alloc("OLb", [128, NT], BF16) for _ in range(2)]
            SGc = A.alloc("SGc", [128, NT], BF16)
            slru = A.alloc("slru", [128, 8, NSEQ], F32)
            YLst = sgast
            S.dma("pool", lambda h: h.dma_start(out=wgx[:], in_=wgx_d), writes=["wgx"])
            S.dma("pool", lambda h: h.dma_start(out=wga[:], in_=wga_d), writes=["wga"])
            S.dma("sp", lambda h: h.dma_start(out=slru[:], in_=slru_d), writes=["slru"])
            ssq_first = {"v": True}
            gcnt = {"n": 0}
            wbx = {}
            wbg = {}

            def lru_p1(c):
                s_ = c % 2
                half, cc = c // 4, c % 4
                if cc == 0:
                    wb_x = load_w_in(4 * D + 512 * half)
                    wbx[half] = wb_x
                    for (lo, m, nm) in ((L - 3, 3, "p"), (L, NS, "s")):
                        for kc in range(8):
                            S.op("pe", _mm(ps[6][0:m, :], hT[:, kc, lo:lo + m], wbuf[wb_x][:, kc, :], kc == 0, kc == 7),
                                 reads=["wbuf%d" % wb_x, hTn[3] if nm == "p" else hTn[4]], writes=[psn[6]])
                        sb_ = stage[0] if nm == "p" else stage[1]
                        sn = "stage0" if nm == "p" else "stage1"
                        S.op("act", lambda h, sb_=sb_, m=m: h.activation(out=sb_[0:m, :], in_=ps[6][0:m, :], func=ACTF.Copy),
                             reads=[psn[6]], writes=[sn])
                        if nm == "p":
                            S.dma("sp", lambda h, half=half: h.dma_start(out=convp_d[:, 512 * half:512 * half + 512],
                                                                          in_=stage[0][0:3, :]), reads=[sn], is_output=True)
                        else:
                            for b in range(NSEQ):
                                S.dma("sp", lambda h, half=half, b=b: h.dma_start(
                                    out=convs_d[b, :, 512 * half:512 * half + 512],
                                    in_=stage[1][TS * b + TS - 3:TS * b + TS, :]), reads=[sn], is_output=True)
                wb_x = wbx[half]
                S.dma("sp", lambda h, c=c, s_=s_: h.dma_start(out=XPs[s_][:, :, 0:3], in_=sconv_d[:, c, :, :]),
                      writes=["XPs%d" % s_])
                S.op("pool", lambda h, s_=s_: h.memset(XPq[s_][:, 0:3], 0.0), writes=["XPq%d" % s_])

                def cons_x(tbi, t0, tn, k, s_=s_):
                    if tbi < 4:
                        S.op("act", lambda h: h.activation(out=XPq[s_][:, 3 + t0:3 + t0 + tn], in_=ps[k][:, 0:tn], func=ACTF.Copy),
                             reads=[psn[k]], writes=["XPq%d" % s_])
                    else:
                        S.op("act", lambda h: h.activation(
                            out=XPs[s_][:, :, 3:3 + TS], in_=ps[k][:, 0:NS].rearrange("p (b t) -> p b t", b=NSEQ), func=ACTF.Copy),
                            reads=[psn[k]], writes=["XPs%d" % s_])
                proj_fm(wb_x, 128 * cc, cons_x)

            def lru_p2(c):
                s_ = c % 2
                XC = XCh[s_]
                cw = lambda tap: vecs[:, V_CW + 8 * tap + c:V_CW + 8 * tap + c + 1]
                cb = vecs[:, V_CB + c:V_CB + c + 1]
                XCsv = XC[:, L:NT].rearrange("p (b t) -> p b t", b=NSEQ)
                for (src_of, dstv, rn) in ((lambda tap: XPq[s_][:, tap:tap + L], XC[:, 0:L], "XPq%d" % s_),
                                           (lambda tap: XPs[s_][:, :, tap:tap + TS], XCsv, "XPs%d" % s_)):
                    S.op("dve", lambda h, src_of=src_of, dstv=dstv: h.tensor_scalar(
                        out=dstv, in0=src_of(0), scalar1=cw(0), scalar2=cb, op0=ALU.mult, op1=ALU.add),
                        reads=[rn, "vecs"], writes=["XC%d" % s_])
                    for tap in (1, 2, 3):
                        S.op("dve", lambda h, src_of=src_of, dstv=dstv, tap=tap: h.scalar_tensor_tensor(
                            out=dstv, in0=src_of(tap), scalar=cw(tap), in1=dstv, op0=ALU.mult, op1=ALU.add),
                            reads=[rn, "vecs", "XC%d" % s_], writes=["XC%d" % s_])
                S.op("act", lambda h: h.activation(out=XCb[s_][:], in_=XC[:], func=ACTF.Copy),
                     reads=["XC%d" % s_], writes=["XCb%d" % s_])
                for (wg, wgn, TH, thn, dcb) in ((wgx, "wgx", THx[s_], "THx%d" % s_, DC_HBGX),
                                                (wga, "wga", THa[s_], "THa%d" % s_, DC_HBGA)):
                    for tbi, (t0, tn) in enumerate(TBLK):
                        k = 4 + gcnt["n"] % 2
                        gcnt["n"] += 1
                        S.op("pe", _mm(ps[k][:, 0:tn], wg[:, c, :], XCb[s_][:, t0:t0 + tn], True, True),
                             reads=[wgn, "XCb%d" % s_], writes=[psn[k]])
                        S.op("act", lambda h, TH=TH, t0=t0, tn=tn, k=k, dcb=dcb: h.activation(
                            out=TH[:, t0:t0 + tn], in_=ps[k][:, 0:tn], func=ACTF.Tanh, scale=0.5,
                            bias=dcol[:, dcb + c:dcb + c + 1]), reads=[psn[k], "dc_hbgx", "dc_hbga"], writes=[thn])

            def lru_p3a(c):
                s_ = c % 2
                SQ = XPq[s_][:, 0:NT]
                Aa = THa[s_]
                sqn, than = "XPq%d" % s_, "THa%d" % s_
                S.op("act", lambda h: h.activation(out=SQ, in_=THa[s_][:], func=ACTF.Exp,
                                                   scale=dcol[:, DC_C + c:DC_C + c + 1], bias=dcol[:, DC_C + c:DC_C + c + 1]),
                     reads=[than, "dc_c"], writes=[sqn])
                S.op("act", lambda h: h.activation(out=Aa[:], in_=THa[s_][:], func=ACTF.Exp,
                                                   scale=dcol[:, DC_HC + c:DC_HC + c + 1], bias=dcol[:, DC_HC + c:DC_HC + c + 1]),
                     reads=[than, "dc_hc"], writes=[than])
                S.op("act", lambda h: h.activation(out=SQ, in_=SQ, func=ACTF.Sqrt, scale=-1.0, bias=1.0),
                     reads=[sqn], writes=[sqn])

            def lru_p3b(c):
                s_ = c % 2
                XC = XCh[s_]
                Hh = XCh[s_]
                SQ = XPq[s_][:, 0:NT]
                Aa = THa[s_]
                xn, hn, sqn, thxn, than = "XC%d" % s_, "XC%d" % s_, "XPq%d" % s_, "THx%d" % s_, "THa%d" % s_
                S.op("dve", lambda h: h.scalar_tensor_tensor(out=THx[s_][:], in0=THx[s_][:], scalar=1.0, in1=XC[:],
                                                             op0=ALU.add, op1=ALU.mult), reads=[thxn, xn], writes=[thxn])
                S.op("dve", lambda h: h.scalar_tensor_tensor(out=THx[s_][:], in0=THx[s_][:], scalar=0.5, in1=SQ,
                                                             op0=ALU.mult, op1=ALU.mult), reads=[thxn, sqn], writes=[thxn])
                S.op("dve", lambda h: h.tensor_tensor_scan(out=Hh[:, 0:L], data0=Aa[:, 0:L], data1=THx[s_][:, 0:L], initial=0.0,
                                                           op0=ALU.mult, op1=ALU.add), reads=[than, thxn], writes=[hn])
                for b in range(NSEQ):
                    S.op("dve", lambda h, b=b: h.tensor_tensor_scan(
                        out=Hh[:, L + TS * b:L + TS * b + TS], data0=Aa[:, L + TS * b:L + TS * b + TS],
                        data1=THx[s_][:, L + TS * b:L + TS * b + TS], initial=slru[:, c, b:b + 1],
                        op0=ALU.mult, op1=ALU.add), reads=[than, thxn, "slru"], writes=[hn])

            def lru_gr(c):
                half, cc = c // 4, c % 4
                if cc == 0:
                    wbg[half] = load_w_in(5 * D + 512 * half)

                def cons_g(tbi, t0, tn, k):
                    S.op("act", lambda h: h.activation(out=SGc[:, t0:t0 + tn], in_=ps[k][:, 0:tn], func=ACTF.Silu),
                         reads=[psn[k]], writes=["SGc"])
                proj_fm(wbg[half], 128 * cc, cons_g)

            def lru_p3c(c):
                s_ = c % 2
                Hh = XCh[s_]
                hn = "XC%d" % s_
                S.op("pool", lambda h: h.tensor_copy(out=HL[:, c, 0:1], in_=Hh[:, L - 1:L]), reads=[hn], writes=["HL"])
                S.op("pool", lambda h: h.tensor_copy(
                    out=HL[:, c, 1:5], in_=Hh[:, L:NT].rearrange("p (b t) -> p b t", b=NSEQ)[:, :, TS - 1]),
                    reads=[hn], writes=["HL"])
                S.op("pool", lambda h: h.tensor_tensor(out=SQb[s_][:], in0=Hh[:], in1=Hh[:], op=ALU.mult),
                     reads=[hn], writes=["SQb%d" % s_])
                S.op("act", lambda h: h.activation(out=OLb[s_][:], in_=Hh[:], func=ACTF.Copy),
                     reads=[hn], writes=["OLb%d" % s_])
                for t in range(17):
                    rows = 128 if t < 16 else NS
                    S.op("pe", _mm(ps[7][0:rows, t:t + 1], SQb[s_][:, t * 128:t * 128 + rows], onesb[:, 0:1],
                                   ssq_first["v"], c == 7 and t == 16), reads=["SQb%d" % s_, "onesb"], writes=[psn[7]])
                    ssq_first["v"] = False
                yb = c % 2
                S.op("dve", lambda h: h.scalar_tensor_tensor(
                    out=YLst[yb][:], in0=SGc[:], scalar=vecs[:, V_NLRU + c:V_NLRU + c + 1],
                    in1=OLb[s_][:], op0=ALU.mult, op1=ALU.mult),
                    reads=["SGc", "OLb%d" % s_, "vecs"], writes=["YLst%d" % yb])
                S.dma("sp", lambda h: h.dma_start(out=yl_scr[c], in_=YLst[yb][:]),
                      reads=["YLst%d" % yb], writes=["yl_scr"])

            lru_p1(0)
            lru_p2(0)
            lru_p1(1)
            for c in range(8):
                lru_p3a(c)
                if c + 1 < 8:
                    lru_p2(c + 1)
                lru_p3b(c)
                lru_gr(c)
                if c + 2 < 8:
                    lru_p1(c + 2)
                lru_p3c(c)
            S.op("dve", lambda h: h.tensor_copy(out=small[:, SM_LRU:SM_LRU + 17], in_=ps[7][:, 0:17]),
                 reads=[psn[7]], writes=["ssq_lru"])
            S.dma("sp", lambda h: h.dma_start(out=lrup_d.rearrange("(c p) -> p c", p=128), in_=HL[:, :, 0],
                                              allow_slow_non_contiguous=True), reads=["HL"], is_output=True)
            for b in range(NSEQ):
                S.dma("sp", lambda h, b=b: h.dma_start(out=lrus_d[b].rearrange("(c p) -> p c", p=128), in_=HL[:, :, 1 + b],
                                                       allow_slow_non_contiguous=True), reads=["HL"], is_output=True)
            A.release(mA1)
            S.barrier()

        QT = KT = VT = None
        if "a2" in stages:
            QT = A.alloc("QT", [128, 8, L], BF16)
            KT = A.alloc("KT", [128, 8, L], BF16)
            VT = A.alloc("VT", [128, 8, L], BF16)
            stg = {"n": 0}
            for (which, colbase, dstT, dstS, nm) in (("q", 0, QT, QTs, "QT"), ("k", D, KT, KTs, "KT"), ("v", 2 * D, VT, VTs, "VT")):
                for half in range(2):
                    wb = load_w_in(colbase + 512 * half)
                    for cc in range(4):
                        j = 4 * half + cc

                        def cons_qkv(tbi, t0, tn, k, j=j, dstT=dstT, dstS=dstS, nm=nm):
                            dst = dstT[:, j, t0:t0 + tn] if tbi < 4 else dstS[:, j, :]
                            if (tbi + j) % 2 == 0:
                                S.op("act", lambda h: h.activation(out=dst, in_=ps[k][:, 0:tn], func=ACTF.Copy),
                                     reads=[psn[k]], writes=["%s%d_%d" % (nm, j, tbi)])
                            else:
                                S.op("dve", lambda h: h.tensor_copy(out=dst, in_=ps[k][:, 0:tn]),
                                     reads=[psn[k]], writes=["%s%d_%d" % (nm, j, tbi)])
                        proj_fm(wb, 128 * cc, cons_qkv)
                    if which in ("k", "v"):
                        o_p = kp_d if which == "k" else vp_d
                        o_s = ks_d if which == "k" else vs_d
                        for t in range(17):
                            rows = 128 if t < 16 else NS
                            k = 4 + stg["n"] % 2
                            sg = stg["n"] % 4
                            stg["n"] += 1
                            for kc in range(8):
                                S.op("pe", _mm(ps[k][0:rows, :], hT[:, kc, t * 128:t * 128 + rows], wbuf[wb][:, kc, :],
                                               kc == 0, kc == 7), reads=["wbuf%d" % wb, hTn[min(t // 4, 4)]], writes=[psn[k]])
                            if t % 2 == 0:
                                S.op("act", lambda h, sg=sg, rows=rows, k=k: h.activation(
                                    out=stage[sg][0:rows, :], in_=ps[k][0:rows, :], func=ACTF.Copy),
                                    reads=[psn[k]], writes=["stage%d" % sg])
                            else:
                                S.op("dve", lambda h, sg=sg, rows=rows, k=k: h.tensor_copy(
                                    out=stage[sg][0:rows, :], in_=ps[k][0:rows, :]), reads=[psn[k]], writes=["stage%d" % sg])
                            if t < 16:
                                S.dma("sp", lambda h, sg=sg, t=t, o_p=o_p, half=half: h.dma_start(
                                    out=o_p[t * 128:(t + 1) * 128, 512 * half:512 * half + 512], in_=stage[sg][:]),
                                    reads=["stage%d" % sg], is_output=True)
                            else:
                                for b in range(NSEQ):
                                    S.dma("sp", lambda h, sg=sg, b=b, o_s=o_s, half=half: h.dma_start(
                                        out=o_s[b, WB - TS:WB, 512 * half:512 * half + 512],
                                        in_=stage[sg][TS * b:TS * b + TS, :]), reads=["stage%d" % sg], is_output=True)
            mg = A.mark()
            for half in range(2):
                wb = load_w_in(3 * D + 512 * half)
                for cc in range(4):
                    c = 4 * half + cc
                    sb_i = c % 2

                    def cons_ga(tbi, t0, tn, k, sb_i=sb_i):
                        S.op("act", lambda h: h.activation(out=sgast[sb_i][:, t0:t0 + tn], in_=ps[k][:, 0:tn], func=ACTF.Silu),
                             reads=[psn[k]], writes=["sgast%d" % sb_i])
                    proj_fm(wb, 128 * cc, cons_ga)
                    S.dma("sp", lambda h, c=c, sb_i=sb_i: h.dma_start(out=sga_scr[c], in_=sgast[sb_i][:]),
                          reads=["sgast%d" % sb_i], writes=["sga_scr"])
            A.release(mg)
            S.barrier()

        if "b" in stages:
            for b in range(NSEQ if cache_copy else 0):
                for (src, dst) in ((ck_d, ks_d), (cv_d, vs_d)):
                    for half in range(2):
                        r0 = 1020 * half
                        S.dma("sp", lambda h, src=src, dst=dst, b=b, r0=r0: h.dma_start(
                            out=dst[b, r0:r0 + 1020, :], in_=src[b, TS + r0:TS + r0 + 1020, :]), is_output=True)

            S.op("pool", lambda h: h.memset(Vpair[:, :, 64:128], 1.0), writes=["Vpair"])
            sb_rot = {"n": 0}
            ob_rot = {"n": 0}
            pt_rot = {"n": 0}
            load_bt_pair(0, 0)
            assert nc.lookup_mloc(stage[1]).addr == nc.lookup_mloc(stage[0]).addr + 2048
            PT16 = nc.alloc_sbuf_tensor_at("PT16_al", [128, 2048], BF16, offset=nc.lookup_mloc(stage[0]).addr)
            PTB = list(PT) + [nc.alloc_sbuf_tensor_at("PTx%d" % i, [128, 512], BF16,
                                                      offset=nc.lookup_mloc(stage[2]).addr + 1024 * i) for i in range(2)]
            SBK = [0, 1, 2, 7]

            def vtile_ap(arr, ti, hp, nkeys=128):
                return Vpair[0:nkeys, arr * 16 + ti, 64 * hp:64 * hp + 128]

            for j in range(8):
                bt = j % 2
                if j + 1 < 8:
                    load_bt_pair(j + 1, (j + 1) % 2)
                for arr in range(3):
                    for g in range(2):
                        pb = 5 + (arr * 2 + g) % 2
                        pv = ps[pb][:].bitcast(BF16)
                        for u in range(8):
                            ti = 8 * g + u
                            if arr == 0:
                                cols = VT[:, j, ti * 128:(ti + 1) * 128]
                            elif arr == 1:
                                r, n = ti // 4, ti % 4
                                cols = VT[:, j, 512 * n + r:512 * n + 512:4]
                            else:
                                cols = VT[:, j, ti:L:16]
                            S.op("pe", lambda h, pv=pv, u=u, cols=cols: h.transpose(
                                out=pv[:, u * 128:(u + 1) * 128], in_=cols, identity=identb[:]),
                                reads=["VT%d_%d" % (j, q) for q in range(4)] + ["identb"], writes=[psn[pb]])
                        t0_ = arr * 16 + 8 * g
                        vdst = Vpair[:, t0_:t0_ + 8, :].rearrange("p t (a f) -> p t a f", a=3)[:, :, 0:3:2, :]
                        vsrc = pv.rearrange("p (t a f) -> p t a f", t=8, a=2)
                        if (arr + g) % 2 == 0:
                            S.op("dve", lambda h, vdst=vdst, vsrc=vsrc: h.tensor_copy(out=vdst, in_=vsrc),
                                 reads=[psn[pb]], writes=["Vpair"])
                        else:
                            S.op("act", lambda h, vdst=vdst, vsrc=vsrc: h.activation(out=vdst, in_=vsrc, func=ACTF.Copy),
                                 reads=[psn[pb]], writes=["Vpair"])
                for hp in range(2):
                    P0 = 64 * hp
                    qn = lambda c: ["QT%d_%d" % (j, c)]
                    for g16 in range(4):
                        sb_i = SBK[sb_rot["n"] % 4]
                        sb_rot["n"] += 1
                        Sb = ps[sb_i]
                        for u in range(4):
                            r = 4 * g16 + u
                            S.op("pe", _mm(Sb[:, 128 * u:128 * u + 128], KT[P0:P0 + 64, j, r:L:16], QT[P0:P0 + 64, j, r:L:16],
                                           u == 0, False),
                                 reads=["KT%d_%d" % (j, q) for q in range(4)] + ["QT%d_%d" % (j, q) for q in range(4)],
                                 writes=[psn[sb_i]])
                        for u in range(4):
                            for half in range(2):
                                S.op("pe", _mm(Sb[:, 128 * u + 64 * half:128 * u + 64 * half + 64],
                                               BTp[bt][P0:P0 + 64, 2, half, 128:256], jb[P0:P0 + 64, P0:P0 + 64],
                                               False, u == 3 and half == 1), reads=["BTp%d" % bt, "jb"], writes=[psn[sb_i]])
                        S.op("act", lambda h, Sb=Sb, g16=g16: h.activation(
                            out=PT16[:, 512 * g16:512 * g16 + 512], in_=Sb[:, :], func=ACTF.Exp, scale=0.125),
                            reads=[psn[sb_i]], writes=["PT16_%d" % g16])
                    for c in range(4):
                        groups = []
                        g1 = []
                        for u in range(4):
                            n = 4 * c + u
                            qa = QT[P0:P0 + 64, j, 128 * n:128 * n + 128]
                            oc = slice(128 * u, 128 * u + 128)
                            if n > 0:
                                g1.append((0, n - 1, KT[P0:P0 + 64, j, 128 * (n - 1):128 * n], qa, oc, (0, 0), 128, 128, 0))
                            g1.append((0, n, KT[P0:P0 + 64, j, 128 * n:128 * n + 128], qa, oc, (0, 128), 128, 128, 0))
                        groups.append(g1[:4])
                        if len(g1) > 4:
                            groups.append(g1[4:])
                        g2 = []
                        for r in range(4):
                            qa = QT[P0:P0 + 64, j, 512 * c + r:512 * c + 512:4]
                            oc = slice(r, 512, 4)
                            if c > 0:
                                g2.append((1, 4 * r + c - 1, KT[P0:P0 + 64, j, 512 * (c - 1) + r:512 * c:4], qa, oc, (1, 0), 128, 128, 0))
                            g2.append((1, 4 * r + c, KT[P0:P0 + 64, j, 512 * c + r:512 * c + 512:4], qa, oc, (1, 128), 128, 128, 0))
                        groups.append(g2[:4])
                        if len(g2) > 4:
                            groups.append(g2[4:])
                        g3 = []
                        nk = 32 * (c + 1)
                        for r in range(16):
                            qa = QT[P0:P0 + 64, j, r + 512 * c:512 * c + 512:16]
                            oc = slice(r, 512, 16)
                            g3.append((2, r, KT[P0:P0 + 64, j, r:min(L, r + 16 * nk):16], qa, oc, (2, 128), nk, 32, 32 * c))
                        groups.append(g3)

                        ob = 3 + ob_rot["n"] % 2
                        ob_rot["n"] += 1
                        first_o = True
                        for grp in groups:
                            if grp is groups[-1]:
                                for ti_, (arr, vt, ka, qa, oc, (bp, jj0), nkeys, nq, qoff) in enumerate(grp):
                                    lhs = vtile_ap(arr, vt, hp, nkeys)
                                    S.op("pe", _mm(ps[ob][:, oc], lhs, PT16[0:nkeys, 128 * vt + qoff:128 * vt + qoff + nq],
                                                   first_o, ti_ == len(grp) - 1),
                                         reads=["Vpair", "PT16_%d" % (vt // 4)], writes=[psn[ob]])
                                    first_o = False
                                continue
                            sb_i = SBK[sb_rot["n"] % 4]
                            sb_rot["n"] += 1
                            pt_i = pt_rot["n"] % 5
                            pt_rot["n"] += 1
                            Sb = ps[sb_i]
                            col = 0
                            cols_of = []
                            maxk = 0
                            for ti_, (arr, vt, ka, qa, oc, (bp, jj0), nkeys, nq, qoff) in enumerate(grp):
                                S.op("pe", _mm(Sb[0:nkeys, col:col + nq], ka, qa, ti_ == 0, False),
                                     reads=["KT%d_%d" % (j, q) for q in range(4)] + qn(c), writes=[psn[sb_i]])
                                cols_of.append(col)
                                col += nq
                                maxk = max(maxk, nkeys)
                            for ti_, (arr, vt, ka, qa, oc, (bp, jj0), nkeys, nq, qoff) in enumerate(grp):
                                last = ti_ == len(grp) - 1
                                if nq == 128:
                                    for half in range(2):
                                        S.op("pe", _mm(Sb[0:nkeys, cols_of[ti_] + 64 * half:cols_of[ti_] + 64 * half + 64],
                                                       BTp[bt][P0:P0 + 64, bp, half, jj0:jj0 + nkeys],
                                                       jb[P0:P0 + 64, P0:P0 + 64], False, last and half == 1),
                                             reads=["BTp%d" % bt, "jb"], writes=[psn[sb_i]])
                                else:
                                    half = qoff // 64
                                    u0 = qoff % 64
                                    S.op("pe", _mm(Sb[0:nkeys, cols_of[ti_]:cols_of[ti_] + nq],
                                                   BTp[bt][P0:P0 + 64, bp, half, jj0:jj0 + nkeys],
                                                   jb[P0:P0 + 64, P0 + u0:P0 + u0 + nq], False, last),
                                         reads=["BTp%d" % bt, "jb"], writes=[psn[sb_i]])
                            ncols = col
                            S.op("act", lambda h, Sb=Sb, pt_i=pt_i, maxk=maxk, ncols=ncols: h.activation(
                                out=PTB[pt_i][0:maxk, 0:ncols], in_=Sb[0:maxk, 0:ncols], func=ACTF.Exp, scale=0.125),
                                reads=[psn[sb_i]], writes=["PT%d" % pt_i])
                            for ti_, (arr, vt, ka, qa, oc, (bp, jj0), nkeys, nq, qoff) in enumerate(grp):
                                lhs = vtile_ap(arr, vt, hp, nkeys)
                                is_last = (grp is groups[-1]) and ti_ == len(grp) - 1
                                S.op("pe", _mm(ps[ob][:, oc], lhs, PTB[pt_i][0:nkeys, cols_of[ti_]:cols_of[ti_] + nq],
                                               first_o, is_last), reads=["Vpair", "PT%d" % pt_i], writes=[psn[ob]])
                                first_o = False
                        rb = c % 2
                        D0 = 64 - P0
                        S.op("dve", lambda h, ob=ob, rb=rb, D0=D0, P0=P0: h.reciprocal(
                            out=rec[rb][P0:P0 + 64, :], in_=ps[ob][D0:D0 + 64, :]), reads=[psn[ob]], writes=["rec%d" % rb])
                        S.op("dve", lambda h, ob=ob, rb=rb, P0=P0, c=c, j=j: h.tensor_tensor(
                            out=OA[P0:P0 + 64, j, 512 * c:512 * c + 512], in0=ps[ob][P0:P0 + 64, :],
                            in1=rec[rb][P0:P0 + 64, :], op=ALU.mult), reads=[psn[ob], "rec%d" % rb], writes=["OA_%d" % c])
            S.barrier()

        if "c" in stages:
            mC = A.mark()
            A.release(arena0)
            KH = 1024
            Kc = [A.alloc("Kc", [128, 8, D], BF16) for _ in range(2)]
            Vc = [A.alloc("Vc", [128, 8, D], BF16) for _ in range(2)]
            KcT = [A.alloc("KcT", [128, 8, KH], BF16) for _ in range(2)]
            BSb = A.alloc("BSb", [128, WB + TS], BF16)
            selb = A.alloc("selb", [128, 16, 8], BF16)
            Vn = [A.alloc("Vn", [8, D], BF16) for _ in range(2)]
            BSf = nc.alloc_sbuf_tensor_at("BSf_al", [128, WB + TS], F32, offset=nc.lookup_mloc(Vpair).addr)
            for hh in range(16):
                src = bass.AP(ts_scr.tensor, hh * TSL, [[1, 8], [1, WB + TS]])
                S.dma("sp", lambda h, src=src, hh=hh: h.dma_start(out=BSf[8 * hh:8 * hh + 8, :], in_=src),
                      reads=["ts_scr"], writes=["BSf"])
            S.op("dve", lambda h: h.tensor_copy(out=BSb[:], in_=BSf[:]), reads=["BSf"], writes=["BSb"])
            S.dma("pool", lambda h: h.dma_start(out=selb[:], in_=sel_d), writes=["selb"])
            selflat = selb[:].rearrange("p h t -> p (h t)")
            QPs = A.alloc("QPs", [128, 8, NSEQ, 16], BF16)
            onesf = A.alloc("onesf", [128, 128], F32)
            Pr = [A.alloc("Pr", [128, 16], F32) for _ in range(2)]
            recs = A.alloc("recs", [128, 8, 16], F32)
            S.op("pool", lambda h: h.memset(QPs[:], 0.0), writes=["QPs"])
            S.op("pool", lambda h: h.memset(onesf[:], 1.0), writes=["onesf"])
            for hp in range(2):
                S.op("pool", lambda h, hp=hp: h.tensor_copy(
                    out=QPs[64 * hp:64 * hp + 64, :, :, 8 * hp:8 * hp + 8],
                    in_=QTs[64 * hp:64 * hp + 64, :, :].rearrange("p j (b t) -> p j b t", b=NSEQ)),
                    reads=["QTs"], writes=["QPs"])
            rot = {"s": 0, "p": 0, "t": 0, "r": 0}

            def c_loads(sg):
                b, kh = sg // 2, sg % 2
                e = sg % 2
                for (src_d, dstt, nm) in ((ck_d, Kc, "Kc"), (cv_d, Vc, "Vc")):
                    S.dma("pool", lambda h, b=b, kh=kh, e=e, src_d=src_d, dstt=dstt: h.dma_start(
                        out=dstt[e][:].rearrange("p t f -> p (t f)"),
                        in_=src_d[b, KH * kh:KH * kh + KH, :].rearrange("(p t) f -> p (t f)", t=8),
                        max_dma_last_dim=8192), writes=["%s%d" % (nm, e)])

            def c_compute(sg):
                b, kh = sg // 2, sg % 2
                e = sg % 2
                vb = b % 2
                ob = 3 + b % 2
                ntile = 8 + kh
                if kh == 0:
                    for kc in range(8):
                        S.op("pe", lambda h, b=b, kc=kc: h.transpose(
                            out=ps[7][:].bitcast(BF16)[0:TS, kc * 128:(kc + 1) * 128],
                            in_=VTs[:, kc, TS * b:TS * b + TS], identity=identb[:]), reads=["VTs", "identb"], writes=[psn[7]])
                    S.op("dve", lambda h, vb=vb: h.tensor_copy(out=Vn[vb][:], in_=ps[7][:].bitcast(BF16)[0:TS, 0:D]),
                         reads=[psn[7]], writes=["Vn%d" % vb])
                for t in range(8):
                    pb = 5 + rot["t"] % 2
                    rot["t"] += 1
                    pv = ps[pb][:].bitcast(BF16)
                    for jj in range(8):
                        S.op("pe", lambda h, pv=pv, t=t, jj=jj, e=e: h.transpose(
                            out=pv[:, jj * 128:(jj + 1) * 128], in_=Kc[e][:, t, jj * 128:(jj + 1) * 128], identity=identb[:]),
                            reads=["Kc%d" % e, "identb"], writes=[psn[pb]])
                    dst = KcT[e][:, :, t * 128:(t + 1) * 128]
                    srcp = pv.rearrange("p (k t) -> p k t", k=8)
                    if t % 2 == 0:
                        S.op("act", lambda h, dst=dst, srcp=srcp: h.activation(out=dst, in_=srcp, func=ACTF.Copy),
                             reads=[psn[pb]], writes=["KcT%d" % e])
                    else:
                        S.op("dve", lambda h, dst=dst, srcp=srcp: h.tensor_copy(out=dst, in_=srcp),
                             reads=[psn[pb]], writes=["KcT%d" % e])
                for j in range(8):
                    sb_i = rot["s"] % 3
                    rot["s"] += 1
                    pt_i = rot["p"] % 3
                    rot["p"] += 1
                    Sb = ps[sb_i]
                    qa = QPs[:, j, b, :]
                    for t in range(ntile):
                        nk = 128 if t < 8 else TS
                        ka = KcT[e][:, j, t * 128:(t + 1) * 128] if t < 8 else KTs[:, j, TS * b:TS * b + TS]
                        S.op("pe", _mm(Sb[0:nk, t * 16:(t + 1) * 16], ka, qa, t == 0, False),
                             reads=["KcT%d" % e, "KTs", "QPs"], writes=[psn[sb_i]])
                    for t in range(ntile):
                        nk = 128 if t < 8 else TS
                        bl = BSb[:, KH * kh + t:KH * kh + KH:8] if t < 8 else BSb[:, WB:WB + TS]
                        S.op("pe", _mm(Sb[0:nk, t * 16:(t + 1) * 16], bl, selflat[:, 16 * j:16 * j + 16], False, t == ntile - 1),
                             reads=["BSb", "selb"], writes=[psn[sb_i]])
                    S.op("act", lambda h, Sb=Sb, pt_i=pt_i, ntile=ntile: h.activation(
                        out=PT[pt_i][:, 0:ntile * 16], in_=Sb[:, 0:ntile * 16], func=ACTF.Exp, scale=0.125),
                        reads=[psn[sb_i]], writes=["PT%d" % pt_i])
                    pr_i = rot["r"] % 2
                    rot["r"] += 1
                    S.op("dve", lambda h, pt_i=pt_i, pr_i=pr_i: h.tensor_reduce(
                        out=Pr[pr_i][:], in_=PT[pt_i][:, 0:128].rearrange("p (k q) -> p q k", q=16),
                        axis=mybir.AxisListType.X, op=ALU.add), reads=["PT%d" % pt_i], writes=["Pr%d" % pr_i])
                    first = (kh == 0 and j == 0)
                    for t in range(ntile):
                        nk = 128 if t < 8 else TS
                        lhs = Vc[e][:, t, 128 * j:128 * j + 128] if t < 8 else Vn[vb][:, 128 * j:128 * j + 128]
                        S.op("pe", _mm(ps[ob][:, 32 * j:32 * j + 16], lhs, PT[pt_i][0:nk, t * 16:(t + 1) * 16],
                                       first and t == 0, False),
                             reads=["Vc%d" % e, "Vn%d" % vb, "PT%d" % pt_i], writes=[psn[ob]])
                    S.op("pe", _mm(ps[ob][:, 32 * j + 16:32 * j + 32], onesf[:], Pr[pr_i][:], False, kh == 0),
                         reads=["onesf", "Pr%d" % pr_i], writes=[psn[ob]])
                    if kh == 1:
                        S.op("pe", _mm(ps[ob][:, 32 * j + 16:32 * j + 32], onesb[0:TS, :], PT[pt_i][0:TS, 128:144], False, True),
                             reads=["onesb", "PT%d" % pt_i], writes=[psn[ob]])
                if kh == 1:
                    Ov = ps[ob][:, 0:256].rearrange("p (j x) -> p j x", x=32)
                    S.op("dve", lambda h, Ov=Ov: h.reciprocal(out=recs[:], in_=Ov[:, :, 16:32]),
                         reads=[psn[ob]], writes=["recs"])
                    for hp in range(2):
                        P0 = 64 * hp
                        S.op("dve", lambda h, Ov=Ov, P0=P0, hp=hp, b=b: h.tensor_tensor(
                            out=OA[P0:P0 + 64, :, L + TS * b:L + TS * b + TS], in0=Ov[P0:P0 + 64, :, 8 * hp:8 * hp + 8],
                            in1=recs[P0:P0 + 64, :, 8 * hp:8 * hp + 8], op=ALU.mult),
                            reads=[psn[ob], "recs"], writes=["OA_4"])

            c_loads(0)
            for sg in range(2 * NSEQ):
                if sg + 1 < 2 * NSEQ:
                    c_loads(sg + 1)
                c_compute(sg)
            S.barrier()

        if debug:
            dbg_oa = nc.dram_tensor("dbg_oa", [128, 8, NT], BF16, kind="ExternalOutput").ap()
            dbg_yl = nc.dram_tensor("dbg_yl", [8, 128, NT], BF16, kind="ExternalOutput").ap()
            dbg_sg = nc.dram_tensor("dbg_sg", [8, 128, NT], BF16, kind="ExternalOutput").ap()
            S.dma("sp", lambda h: h.dma_start(out=dbg_oa, in_=OA[:]), is_output=True)
            S.dma("sp", lambda h: h.dma_start(out=dbg_yl, in_=yl_scr), is_output=True)
            S.dma("sp", lambda h: h.dma_start(out=dbg_sg, in_=sga_scr), is_output=True)
            S.barrier()

        if "d" in stages:
            A.release(arena0)
            wo = A.alloc("wo", [128, 16, D], BF16)
            gfin = A.alloc("gfin", [128, D], F32)
            YLb = [A.alloc("YLb", [128, 8, 512], BF16) for _ in range(2)]
            SGb = [A.alloc("SGb", [128, 8, 512], BF16) for _ in range(2)]
            YAb = [A.alloc("YAb", [128, 8, 512], BF16) for _ in range(2)]
            SQe = A.alloc("SQe", [128, 8, 512], BF16)
            xre = [A.alloc("xre", [128, D], F32) for _ in range(2)]
            yt = [A.alloc("yt", [128, D], F32) for _ in range(2)]
            junk2 = sgast[0]
            for q in range(4):
                S.dma("pool", lambda h, q=q: h.dma_start(
                    out=wo[:, 4 * q:4 * q + 4, :],
                    in_=wout_d[512 * q:512 * q + 512, :].rearrange("(kc p) n -> p kc n", p=128)), writes=["wo"])
            S.dma("sp", lambda h: h.dma_start(out=gfin[:], in_=gfin_d), writes=["gfin"])
            att_first = {"v": True}
            prot_d = {"n": 0}
            pending = {"f": None}
            S.op("dve", lambda h: h.tensor_scalar(
                out=small[:, SM_MSX:SM_MSX + 17], in0=small[:, SM_LRU:SM_LRU + 17],
                scalar1=1.0 / D, scalar2=EPS, op0=ALU.mult, op1=ALU.add), reads=["ssq_lru"], writes=["msl"])
            S.op("pool", lambda h: h.tensor_tensor(
                out=small[:, SM_RL:SM_RL + 17], in0=small[:, SM_MSX:SM_MSX + 17],
                in1=nhalf[:, 0:17], op=ALU.pow), reads=["msl", "nhalf"], writes=["rl"])
            def d_loads(tbi):
                t0, tn = TBLK[tbi]
                e = tbi % 2
                S.dma("sp", lambda h: h.dma_start(
                    out=YLb[e][:, :, 0:tn], in_=yl_scr[:, :, t0:t0 + tn].rearrange("c p t -> p c t")),
                    reads=["yl_scr"], writes=["YLb%d" % e])
                S.dma("sp", lambda h: h.dma_start(
                    out=SGb[e][:, :, 0:tn], in_=sga_scr[:, :, t0:t0 + tn].rearrange("c p t -> p c t")),
                    reads=["sga_scr"], writes=["SGb%d" % e])

            d_loads(0)
            for tbi, (t0, tn) in enumerate(TBLK):
                e = tbi % 2
                S.op("pool", lambda h, t0=t0, tn=tn: h.tensor_tensor(
                    out=SQe[:, :, 0:tn], in0=OA[:, :, t0:t0 + tn], in1=OA[:, :, t0:t0 + tn], op=ALU.mult),
                    reads=["OA_%d" % tbi], writes=["SQe"])
                ntile = (tn + 127) // 128
                tb0 = t0 // 128
                for tt in range(ntile):
                    rows = min(128, tn - 128 * tt)
                    t = tb0 + tt
                    for c in range(8):
                        S.op("pe", _mm(ps[7][0:rows, 32 + t:33 + t], SQe[:, c, 128 * tt:128 * tt + rows], onesb[:, 0:1],
                                       att_first["v"], c == 7), reads=["SQe", "onesb"], writes=[psn[7]])
                        att_first["v"] = False
                rws = 128 if tbi < 4 else NS
                S.op("dve", lambda h, rws=rws, tb0=tb0, ntile=ntile: h.tensor_scalar(
                    out=small[0:rws, SM_ATT + tb0:SM_ATT + tb0 + ntile], in0=ps[7][0:rws, 32 + tb0:32 + tb0 + ntile],
                    scalar1=1.0 / D, scalar2=EPS, op0=ALU.mult, op1=ALU.add), reads=[psn[7]], writes=["msa%d" % tbi])
                S.op("pool", lambda h, rws=rws, tb0=tb0, ntile=ntile: h.tensor_tensor(
                    out=small[0:rws, SM_RA + tb0:SM_RA + tb0 + ntile], in0=small[0:rws, SM_ATT + tb0:SM_ATT + tb0 + ntile],
                    in1=nhalf[0:rws, 0:ntile], op=ALU.pow), reads=["msa%d" % tbi, "nhalf"], writes=["ra%d" % tbi])
                for c in range(8):
                    S.op("dve", lambda h, e=e, c=c, t0=t0, tn=tn: h.scalar_tensor_tensor(
                        out=YAb[e][:, c, 0:tn], in0=OA[:, c, t0:t0 + tn], scalar=vecs[:, V_NATT + c:V_NATT + c + 1],
                        in1=SGb[e][:, c, 0:tn], op0=ALU.mult, op1=ALU.mult),
                        reads=["OA_%d" % tbi, "SGb%d" % e, "vecs"], writes=["YAb%d" % e])
                for tt in range(ntile):
                    rows = min(128, tn - 128 * tt)
                    t = tb0 + tt
                    xb = t % 2
                    src = xp_d[t * 128:(t + 1) * 128, :] if t < 16 else xs_d
                    S.dma("sp", lambda h, xb=xb, rows=rows, src=src: h.dma_start(out=xre[xb][0:rows, :], in_=src),
                          writes=["xre%d" % xb])
                    if tt == 0 and tbi + 1 < len(TBLK):
                        d_loads(tbi + 1)
                    yb = t % 2
                    for hf in range(2):
                        ka = prot_d["n"] % 4
                        kl = (prot_d["n"] + 1) % 4
                        prot_d["n"] += 2
                        for kc in range(8):
                            S.op("pe", _mm(ps[ka][0:rows, :], YAb[e][:, kc, 128 * tt:128 * tt + rows],
                                           wo[:, kc, 512 * hf:512 * hf + 512], kc == 0, kc == 7),
                                 reads=["YAb%d" % e, "wo"], writes=[psn[ka]])
                        for kc in range(8):
                            S.op("pe", _mm(ps[kl][0:rows, :], YLb[e][:, kc, 128 * tt:128 * tt + rows],
                                           wo[:, 8 + kc, 512 * hf:512 * hf + 512], kc == 0, kc == 7),
                                 reads=["YLb%d" % e, "wo"], writes=[psn[kl]])
                        S.op("dve", lambda h, rows=rows, t=t, ka=ka, yb=yb, xb=xb, hf=hf: h.scalar_tensor_tensor(
                            out=yt[yb][0:rows, 512 * hf:512 * hf + 512], in0=ps[ka][0:rows, :],
                            scalar=small[0:rows, SM_RA + t:SM_RA + t + 1], in1=xre[xb][0:rows, 512 * hf:512 * hf + 512],
                            op0=ALU.mult, op1=ALU.add), reads=[psn[ka], "ra%d" % tbi, "xre%d" % xb], writes=["yt%d" % yb])
                        S.op("dve", lambda h, rows=rows, t=t, kl=kl, yb=yb, hf=hf: h.scalar_tensor_tensor(
                            out=yt[yb][0:rows, 512 * hf:512 * hf + 512], in0=ps[kl][0:rows, :],
                            scalar=small[0:rows, SM_RL + t:SM_RL + t + 1], in1=yt[yb][0:rows, 512 * hf:512 * hf + 512],
                            op0=ALU.mult, op1=ALU.add), reads=[psn[kl], "rl", "yt%d" % yb], writes=["yt%d" % yb])
                    S.op("act", lambda h, rows=rows, t=t, yb=yb: h.activation(
                        out=junk2[0:rows, 0:D], in_=yt[yb][0:rows, :], func=ACTF.Square,
                        accum_out=small[0:rows, SM_SSY + t:SM_SSY + t + 1]), reads=["yt%d" % yb], writes=["junk2", "ssy%d" % t])
                    S.op("dve", lambda h, rows=rows, t=t: h.tensor_scalar(
                        out=small[0:rows, SM_SSQX + t:SM_SSQX + t + 1], in0=small[0:rows, SM_SSY + t:SM_SSY + t + 1],
                        scalar1=1.0 / D, scalar2=EPS, op0=ALU.mult, op1=ALU.add), reads=["ssy%d" % t], writes=["msy%d" % t])
                    S.op("pool", lambda h, rows=rows, t=t: h.tensor_tensor(
                        out=small[0:rows, SM_RSY + t:SM_RSY + t + 1], in0=small[0:rows, SM_SSQX + t:SM_SSQX + t + 1],
                        in1=nhalf[0:rows, 0:1], op=ALU.pow), reads=["msy%d" % t, "nhalf"], writes=["rsy%d" % t])

                    def final(rows=rows, t=t, yb=yb):
                        S.op("dve", lambda h: h.scalar_tensor_tensor(
                            out=yt[yb][0:rows, :], in0=yt[yb][0:rows, :], scalar=small[0:rows, SM_RSY + t:SM_RSY + t + 1],
                            in1=gfin[0:rows, :], op0=ALU.mult, op1=ALU.mult), reads=["yt%d" % yb, "rsy%d" % t, "gfin"],
                            writes=["yt%d" % yb])
                        dst = yp_d[t * 128:(t + 1) * 128, :] if t < 16 else ys_d
                        S.dma("sp", lambda h: h.dma_start(out=dst, in_=yt[yb][0:rows, :]),
                              reads=["yt%d" % yb], is_output=True)
                    if pending["f"] is not None:
                        pending["f"]()
                    pending["f"] = final
            if pending["f"] is not None:
                pending["f"]()

        S.finish()
        S.replay()
    return nc


def _cols(v):
    return np.ascontiguousarray(np.asarray(v, np.float32).reshape(8, 128).T)


def _blockdiag(w):
    out = np.zeros((128, 8, 128), np.float32)
    w = np.asarray(w, np.float32)
    for c in range(8):
        out[0:64, c, 0:64] = w[2 * c]
        out[64:128, c, 64:128] = w[2 * c + 1]
    return out


_CONSTS = None
_NC_CACHE = {}


def make_in_maps(inputs):
    global _CONSTS
    if _CONSTS is None:
        _CONSTS = _structural_constants()
    cs = _CONSTS
    f = lambda a: np.asarray(a, np.float32)
    vecs = np.concatenate(
        [_cols(f(inputs["norm_in"])[0]), _cols(f(inputs["norm_attn"])[0]), _cols(f(inputs["norm_lru"])[0])]
        + [_cols(f(inputs["conv_w"])[0, tap]) for tap in range(4)]
        + [_cols(f(inputs["conv_b"])[0]), _cols(f(inputs["b_gate_x"])[0]), _cols(f(inputs["b_gate_a"])[0]),
           _cols(f(inputs["lru_param"])[0])], axis=1)
    assert vecs.shape == (128, NVEC)
    shared = dict(
        w_in=np.ascontiguousarray(f(inputs["w_in"])[0]),
        w_out=np.ascontiguousarray(f(inputs["w_out"])[0]),
        vecs=np.ascontiguousarray(vecs),
        ginrow=np.ascontiguousarray(np.broadcast_to(f(inputs["norm_in"])[0][None, :], (128, D))),
        gfinrow=np.ascontiguousarray(np.broadcast_to(f(inputs["norm_final"])[None, :], (128, D))),
        wgx=_blockdiag(f(inputs["w_gate_x"])[0]),
        wga=_blockdiag(f(inputs["w_gate_a"])[0]),
        rb_aug=np.ascontiguousarray(np.concatenate([f(inputs["rel_bias"]), cs["rbc"]], axis=0)),
        ohp=cs["ohp"], ohs=cs["ohs"], ident=cs["ident"], jm=cs["jm"], sel=cs["sel"],
    )
    xp = f(inputs["x_prompt"])
    xs = f(inputs["x_sample"])
    ck = f(inputs["cache_win_k"])[0]
    cv = f(inputs["cache_win_v"])[0]
    sc = f(inputs["state_conv"])[0]
    sl = f(inputs["state_lru"])[0]
    maps = []
    for c in range(NCORES):
        bs = slice(NSEQ * c, NSEQ * c + NSEQ)
        m = dict(shared)
        m["xp"] = np.ascontiguousarray(xp[c])
        m["xs"] = np.ascontiguousarray(xs[bs].reshape(NS, D))
        m["ck"] = np.ascontiguousarray(ck[bs].reshape(NSEQ, WB, D))
        m["cv"] = np.ascontiguousarray(cv[bs].reshape(NSEQ, WB, D))
        m["sconv"] = np.ascontiguousarray(sc[bs].reshape(NSEQ, 3, 8, 128).transpose(3, 2, 0, 1))
        m["slru"] = np.ascontiguousarray(sl[bs].reshape(NSEQ, 8, 128).transpose(2, 1, 0))
        maps.append(m)
    return maps


def assemble(results):
    cat = lambda k: np.concatenate([np.asarray(r[k]) for r in results], axis=0)
    y_p = np.stack([np.asarray(r["yp"]) for r in results], axis=0)
    y_s = cat("ys").reshape(NCORES * NSEQ, TS, D)
    kp = np.stack([np.asarray(r["kp"]) for r in results], axis=0).reshape(1, NCORES, L, 16, 64)
    vp = np.stack([np.asarray(r["vp"]) for r in results], axis=0).reshape(1, NCORES, L, 16, 64)
    convp = np.stack([np.asarray(r["convp"]) for r in results], axis=0).reshape(1, NCORES, 3, D)
    lrup = np.stack([np.asarray(r["lrup"]) for r in results], axis=0).reshape(1, NCORES, D)
    ks = cat("ks").reshape(1, NCORES * NSEQ, WB, 16, 64)
    vs = cat("vs").reshape(1, NCORES * NSEQ, WB, 16, 64)
    convs = cat("convs").reshape(1, NCORES * NSEQ, 3, D)
    lrus = cat("lrus").reshape(1, NCORES * NSEQ, D)
    return tuple(np.ascontiguousarray(a, dtype=np.float32)
                 for a in (y_p, y_s, kp, vp, convp, lrup, ks, vs, convs, lrus))


def kernel(**inputs):
    maps = make_in_maps(inputs)
    nc = build_nc()
    res = run_bass_kernel_spmd(nc, maps, core_ids=list(range(NCORES)))
    return assemble(res.results)
```

```python
import math
from contextlib import ExitStack

import numpy as np
import concourse.bass as bass
import concourse.mybir as mybir
from concourse.bass_utils import run_bass_kernel_spmd

F32 = mybir.dt.float32
BF16 = mybir.dt.bfloat16
ALU = mybir.AluOpType
ACTF = mybir.ActivationFunctionType

NCORES = 8
D = 1024
L = 2048
NS = 32
NT = L + NS
NSEQ = 4
TS = 8
WB = 2048
DPROJ = 6144
EPS = 1e-6
NEG = -30000.0
TBLK = [(0, 512), (512, 512), (1024, 512), (1536, 512), (2048, NS)]
TPL = 383
TSL = 2063
ENGS = ("pe", "act", "dve", "pool", "sp")


class Sched:
    NDMA = 28

    def __init__(self, nc, stack):
        self.nc = nc
        self.q = {e: [] for e in ENGS}
        self.sem = {e: stack.enter_context(nc.semaphore("c_" + e)) for e in ENGS}
        self.cnt = {e: 0 for e in ENGS}
        self.seen = {e: {} for e in ENGS}
        self.dsem = [stack.enter_context(nc.semaphore("d%d" % i)) for i in range(self.NDMA)]
        self.dval = [0] * self.NDMA
        self.dnext = 0
        self.lastw = {}
        self.readers = {}
        self.out_events = []

    def _deps(self, reads, writes):
        ev = []
        for r in reads:
            w = self.lastw.get(r)
            if w is not None:
                ev.append(w)
        for w_ in writes:
            w = self.lastw.get(w_)
            if w is not None:
                ev.append(w)
            ev.extend(self.readers.get(w_, ()))
        return ev

    def _emit_waits(self, eng, events):
        need = {}
        for ev in events:
            if ev[0] == "dma":
                key = ("dma", ev[1])
                val = ev[2]
            else:
                if ev[0] == eng and eng in ("pe", "sp"):
                    continue
                key = ev[0]
                val = ev[1]
            if self.seen[eng].get(key, 0) >= val:
                continue
            if need.get(key, 0) < val:
                need[key] = val
        for key, val in need.items():
            self.seen[eng][key] = val
            sem = self.dsem[key[1]] if isinstance(key, tuple) else self.sem[key]
            self.q[eng].append(("wait", sem, val))

    def _record(self, reads, writes, event):
        for r in reads:
            self.readers.setdefault(r, []).append(event)
        for w in writes:
            self.lastw[w] = event
            self.readers[w] = []

    def op(self, eng, fn, reads=(), writes=()):
        self._emit_waits(eng, self._deps(reads, writes))
        self.cnt[eng] += 1
        event = (eng, self.cnt[eng])
        self.q[eng].append(("op", fn, self.sem[eng], 1))
        self._record(reads, writes, event)
        return event

    def dma(self, eng, fn, reads=(), writes=(), is_output=False):
        k = self.dnext
        self.dnext = (self.dnext + 1) % self.NDMA
        events = self._deps(reads, writes)
        if self.dval[k] > 0:
            events.append(("dma", k, self.dval[k]))
        self._emit_waits(eng, events)
        self.dval[k] += 16
        event = ("dma", k, self.dval[k])
        self.q[eng].append(("op", fn, self.dsem[k], 16))
        self._record(reads, writes, event)
        if is_output:
            self.out_events.append(event)
        return event

    def all_events(self):
        evs = []
        for k in range(self.NDMA):
            if self.dval[k] > 0:
                evs.append(("dma", k, self.dval[k]))
        for e in ENGS:
            if self.cnt[e] > 0:
                evs.append((e, self.cnt[e]))
        return evs

    def barrier(self):
        evs = self.all_events()
        for e in ENGS:
            self._emit_waits(e, [x for x in evs if x[0] != e or e not in ("pe", "sp")])
        self.lastw = {}
        self.readers = {}

    def finish(self):
        self._emit_waits("sp", self.all_events())

    def replay(self):
        nc = self.nc
        sched = self

        def run(name, h):
            for item in sched.q[name]:
                if item[0] == "wait":
                    h.wait_ge(item[1], item[2])
                else:
                    item[1](h).then_inc(item[2], item[3])

        with nc.Block() as block:
            @block.tensor
            def _(h):
                run("pe", h)

            @block.scalar
            def _(h):
                run("act", h)

            @block.vector
            def _(h):
                run("dve", h)

            @block.gpsimd
            def _(h):
                run("pool", h)

            @block.sync
            def _(h):
                run("sp", h)


class Arena:
    def __init__(self, nc, base=16512, end=229376):
        self.nc = nc
        self.cur = base
        self.end = end
        self.n = 0

    def mark(self):
        return self.cur

    def release(self, m):
        self.cur = m

    def alloc(self, name, shape, dt):
        nbytes = int(np.prod(shape[1:])) * (2 if dt == BF16 else 4)
        off = (self.cur + 63) // 64 * 64
        assert off + nbytes <= self.end, (name, off, nbytes, self.end)
        self.cur = off + nbytes
        self.n += 1
        return self.nc.alloc_sbuf_tensor_at("%s_%d" % (name, self.n), shape, dt, offset=off)


def _t5_bucket(dist):
    dist = np.asarray(dist)
    nf = np.maximum(dist, 1).astype(np.float32)
    large = 16 + (np.log(nf / np.float32(16)) / np.float32(math.log(2048 / 16)) * np.float32(16)).astype(np.int32)
    large = np.minimum(large, 31)
    return np.where(dist < 16, dist, large)


def _structural_constants():
    bucket = _t5_bucket
    ohp = np.zeros((35, 3, TPL), np.float32)
    for p, dil in enumerate((1, 4, 16)):
        for x in range(TPL):
            diff = 255 - x
            if 0 <= diff <= 128:
                ohp[int(bucket(np.array([diff * dil]))[0]), p, x] = 8.0
            else:
                ohp[32, p, x] = 1.0
    ohs = np.zeros((35, TSL), np.float32)
    dists = 2055 - np.arange(TSL)
    bks = bucket(np.maximum(dists, 0))
    for y in range(TSL):
        dist = int(dists[y])
        mult = 0
        if dist >= 0:
            mult += 1 if dist <= 128 else 0
            mult += 1 if (dist % 4 == 0 and dist <= 512) else 0
            mult += 1 if (dist % 16 == 0 and dist <= 2048) else 0
        if mult == 0:
            ohs[32, y] = 1.0
        else:
            ohs[int(bks[y]), y] = 8.0
            if mult == 2:
                ohs[33, y] = 8.0
            elif mult == 3:
                ohs[34, y] = 8.0
    rbc = np.zeros((3, 16), np.float32)
    rbc[0] = NEG
    rbc[1] = math.log(2.0)
    rbc[2] = math.log(3.0)
    ident = np.eye(128, dtype=np.float32)
    jm = np.zeros((128, 128), np.float32)
    for m in range(128):
        jm[m, (m // 64) * 64 + 63 - (m % 64)] = 1.0
    sel = np.zeros((128, 16, 8), np.float32)
    for h in range(16):
        for tp in range(8):
            sel[8 * h + tp, h, 7 - tp] = 1.0
    return dict(ohp=ohp, ohs=ohs, rbc=rbc, ident=ident, jm=jm, sel=sel)


V_NIN, V_NATT, V_NLRU, V_CW, V_CB, V_BGX, V_BGA, V_LAM = 0, 8, 16, 24, 56, 64, 72, 80
NVEC = 88


def _mm(out, lhsT, rhs, start, stop):
    return lambda h: h.matmul(out, lhsT=lhsT, rhs=rhs, start=start, stop=stop)


def build_nc(stages=("a0", "a1", "a2", "b", "c", "d"), debug=False, cache_copy=True):
    nc = bass.Bass("TRN2", target_bir_lowering=False)
    di = lambda name, shape: nc.dram_tensor(name, shape, F32, kind="ExternalInput").ap()
    do = lambda name, shape: nc.dram_tensor(name, shape, F32, kind="ExternalOutput").ap()
    xp_d = di("xp", [L, D])
    xs_d = di("xs", [NS, D])
    ck_d = di("ck", [NSEQ, WB, D])
    cv_d = di("cv", [NSEQ, WB, D])
    sconv_d = di("sconv", [128, 8, NSEQ, 3])
    slru_d = di("slru", [128, 8, NSEQ])
    win_d = di("w_in", [D, DPROJ])
    wout_d = di("w_out", [2 * D, D])
    vecs_d = di("vecs", [128, NVEC])
    ginr_d = di("ginrow", [128, D])
    gfin_d = di("gfinrow", [128, D])
    wgx_d = di("wgx", [128, 8, 128])
    wga_d = di("wga", [128, 8, 128])
    rba_d = di("rb_aug", [35, 16])
    ohp_d = di("ohp", [35, 3, TPL])
    ohs_d = di("ohs", [35, TSL])
    id_d = di("ident", [128, 128])
    jm_d = di("jm", [128, 128])
    sel_d = di("sel", [128, 16, 8])

    yp_d = do("yp", [L, D])
    ys_d = do("ys", [NS, D])
    kp_d = do("kp", [L, D])
    vp_d = do("vp", [L, D])
    convp_d = do("convp", [3, D])
    lrup_d = do("lrup", [D])
    ks_d = do("ks", [NSEQ, WB, D])
    vs_d = do("vs", [NSEQ, WB, D])
    convs_d = do("convs", [NSEQ, 3, D])
    lrus_d = do("lrus", [NSEQ, D])

    tp_scr = nc.dram_tensor("tp_scr", [3, 16, TPL], F32, kind="Internal").ap()
    ts_scr = nc.dram_tensor("ts_scr", [16, TSL], F32, kind="Internal").ap()
    yl_scr = nc.dram_tensor("yl_scr", [8, 128, NT], BF16, kind="Internal").ap()
    sga_scr = nc.dram_tensor("sga_scr", [8, 128, NT], BF16, kind="Internal").ap()

    with ExitStack() as st:
        S = Sched(nc, st)
        A = Arena(nc)
        ps = [nc.alloc_psum_tensor("ps%d" % i, [128, 512], F32) for i in range(8)]
        psn = ["ps%d" % i for i in range(8)]

        identb = A.alloc("identb", [128, 128], BF16)
        jb = A.alloc("jb", [128, 128], BF16)
        onesb = A.alloc("onesb", [128, 128], BF16)
        vecs = A.alloc("vecs", [128, NVEC], F32)
        dcol = A.alloc("dcol", [128, 64], F32)
        HL = A.alloc("HL", [128, 8, 5], F32)
        small = A.alloc("small", [128, 256], F32)
        nhalf = A.alloc("nhalf", [128, 32], F32)
        hT = A.alloc("hT", [128, 8, NT], BF16)
        OA = hT
        wbuf = [A.alloc("wbuf", [128, 8, 512], BF16) for _ in range(2)]
        QTs = A.alloc("QTs", [128, 8, NS], BF16)
        KTs = A.alloc("KTs", [128, 8, NS], BF16)
        VTs = A.alloc("VTs", [128, 8, NS], BF16)
        PT = [A.alloc("PT", [128, 512], BF16) for _ in range(3)]
        BTp = [A.alloc("BTp", [128, 3, 2, 256], BF16) for _ in range(2)]
        Vpair = A.alloc("Vpair", [128, 48, 192], BF16)
        stage = [A.alloc("stage", [128, 512], F32) for _ in range(4)]
        sgast = [A.alloc("sgast", [128, NT], BF16) for _ in range(2)]
        rec = [A.alloc("rec", [128, 512], F32) for _ in range(2)]
        arena0 = A.mark()

        SM_SSQX, SM_MSX, SM_RSTDX = 0, 20, 40
        SM_LRU, SM_ATT = 60, 80
        SM_RL, SM_RA = 100, 120
        SM_SSY, SM_RSY = 140, 160
        DC_HBGX, DC_HBGA, DC_HC, DC_C, DC_GL05, DC_GA05 = 0, 8, 16, 24, 32, 40

        S.dma("pool", lambda h: h.dma_start(out=identb[:], in_=id_d), writes=["identb"])
        S.dma("pool", lambda h: h.dma_start(out=jb[:], in_=jm_d), writes=["jb"])
        S.dma("sp", lambda h: h.dma_start(out=vecs[:], in_=vecs_d), writes=["vecs"])
        S.op("pool", lambda h: h.memset(onesb[:], 1.0), writes=["onesb"])
        S.op("pool", lambda h: h.memset(nhalf[:], -0.5), writes=["nhalf"])
        S.op("pool", lambda h: h.memset(HL[:], 0.0), writes=["HL"])
        S.op("dve", lambda h: h.tensor_scalar(out=dcol[:, DC_HBGX:DC_HBGX + 8], in0=vecs[:, V_BGX:V_BGX + 8],
                                              scalar1=0.5, scalar2=None, op0=ALU.mult), reads=["vecs"], writes=["dc_hbgx"])
        S.op("dve", lambda h: h.tensor_scalar(out=dcol[:, DC_HBGA:DC_HBGA + 8], in0=vecs[:, V_BGA:V_BGA + 8],
                                              scalar1=0.5, scalar2=None, op0=ALU.mult), reads=["vecs"], writes=["dc_hbga"])
        S.op("dve", lambda h: h.tensor_scalar(out=dcol[:, DC_GL05:DC_GL05 + 8], in0=vecs[:, V_NLRU:V_NLRU + 8],
                                              scalar1=0.5, scalar2=None, op0=ALU.mult), reads=["vecs"], writes=["dc_gl"])
        S.op("dve", lambda h: h.tensor_scalar(out=dcol[:, DC_GA05:DC_GA05 + 8], in0=vecs[:, V_NATT:V_NATT + 8],
                                              scalar1=0.5, scalar2=None, op0=ALU.mult), reads=["vecs"], writes=["dc_ga"])
        S.op("act", lambda h: h.activation(out=dcol[:, 48:56], in_=vecs[:, V_LAM:V_LAM + 8], func=ACTF.Exp, scale=-1.0),
             reads=["vecs"], writes=["dc_t0"])
        S.op("act", lambda h: h.activation(out=dcol[:, 56:64], in_=dcol[:, 48:56], func=ACTF.Ln, bias=1.0),
             reads=["dc_t0"], writes=["dc_t1"])
        S.op("dve", lambda h: h.tensor_scalar(out=dcol[:, DC_C:DC_C + 8], in0=dcol[:, 56:64], scalar1=-8.0, scalar2=None,
                                              op0=ALU.mult), reads=["dc_t1"], writes=["dc_c"])
        S.op("dve", lambda h: h.tensor_scalar(out=dcol[:, DC_HC:DC_HC + 8], in0=dcol[:, 56:64], scalar1=-4.0, scalar2=None,
                                              op0=ALU.mult), reads=["dc_t1"], writes=["dc_hc"])

        m0 = A.mark()
        rba = A.alloc("rba", [35, 16], F32)
        ohp = A.alloc("ohp", [35, 3, TPL], F32)
        ohs = A.alloc("ohs", [35, TSL], F32)
        tps = A.alloc("tps", [16, 3, TPL], F32)
        tss = A.alloc("tss", [16, TSL], F32)
        S.dma("sp", lambda h: h.dma_start(out=rba[:], in_=rba_d), writes=["rba"])
        S.dma("sp", lambda h: h.dma_start(out=ohp[:], in_=ohp_d), writes=["ohp"])
        S.dma("sp", lambda h: h.dma_start(out=ohs[:], in_=ohs_d), writes=["ohs"])
        for p in range(3):
            S.op("pe", _mm(ps[6][0:16, 0:TPL], rba[:], ohp[:, p, :], True, True), reads=["rba", "ohp"], writes=[psn[6]])
            S.op("act", lambda h, p=p: h.activation(out=tps[:, p, :], in_=ps[6][0:16, 0:TPL], func=ACTF.Copy),
                 reads=[psn[6]], writes=["tps"])
        for i, c0 in enumerate(range(0, TSL, 512)):
            n = min(512, TSL - c0)
            S.op("pe", _mm(ps[6][0:16, 0:n], rba[:], ohs[:, c0:c0 + n], True, True), reads=["rba", "ohs"], writes=[psn[6]])
            S.op("act", lambda h, c0=c0, n=n: h.activation(out=tss[:, c0:c0 + n], in_=ps[6][0:16, 0:n], func=ACTF.Copy),
                 reads=[psn[6]], writes=["tss"])
        S.dma("sp", lambda h: h.dma_start(out=tp_scr.rearrange("p h x -> h p x"), in_=tps[:]), reads=["tps"], writes=["tp_scr"])
        S.dma("sp", lambda h: h.dma_start(out=ts_scr, in_=tss[:]), reads=["tss"], writes=["ts_scr"])

        def load_bt_pair(j, buf):
            for hp in range(2):
                for half in range(2):
                    src = bass.AP(tp_scr.tensor, (2 * j + hp) * TPL + 64 - 64 * half,
                                  [[1, 64], [16 * TPL, 3], [1, 256]])
                    S.dma("pool", lambda h, src=src, hp=hp, half=half, buf=buf: h.dma_start(
                        out=BTp[buf][hp * 64:hp * 64 + 64, :, half, :], in_=src),
                        reads=["tp_scr"], writes=["BTp%d" % buf])

        wstate = {"n": 0}

        def load_w_in(col0):
            b = wstate["n"] % 2
            wstate["n"] += 1
            src = win_d.rearrange("(kc p) n -> p kc n", p=128)[:, :, col0:col0 + 512]
            S.dma("pool", lambda h, b=b, src=src: h.dma_start(out=wbuf[b][:], in_=src), writes=["wbuf%d" % b])
            return b

        if "a0" in stages:
            mA0 = A.mark()
            xin = [A.alloc("xin", [128, D], F32) for _ in range(2)]
            xsb = [A.alloc("xsb", [128, D], BF16) for _ in range(2)]
            ginr = A.alloc("ginr", [128, D], F32)
            junk = A.alloc("junk", [128, D], BF16)
            S.dma("sp", lambda h: h.dma_start(out=ginr[:], in_=ginr_d), writes=["ginr"])
            for t in range(17):
                b = t % 2
                rows = 128 if t < 16 else NS
                src = xp_d[t * 128:(t + 1) * 128, :] if t < 16 else xs_d
                S.dma("sp", lambda h, b=b, rows=rows, src=src: h.dma_start(out=xin[b][0:rows, :], in_=src), writes=["xin%d" % b])
                S.op("act", lambda h, b=b, rows=rows, t=t: h.activation(
                    out=junk[0:rows, :], in_=xin[b][0:rows, :], func=ACTF.Square,
                    accum_out=small[0:rows, SM_SSQX + t:SM_SSQX + t + 1]), reads=["xin%d" % b], writes=["junk", "ssqx%d" % t])
                S.op("dve", lambda h, rows=rows, t=t: h.tensor_scalar(
                    out=small[0:rows, SM_MSX + t:SM_MSX + t + 1], in0=small[0:rows, SM_SSQX + t:SM_SSQX + t + 1],
                    scalar1=1.0 / D, scalar2=EPS, op0=ALU.mult, op1=ALU.add), reads=["ssqx%d" % t], writes=["msx%d" % t])
                S.op("pool", lambda h, rows=rows, t=t: h.tensor_tensor(
                    out=small[0:rows, SM_RSTDX + t:SM_RSTDX + t + 1], in0=small[0:rows, SM_MSX + t:SM_MSX + t + 1],
                    in1=nhalf[0:rows, 0:1], op=ALU.pow), reads=["msx%d" % t, "nhalf"], writes=["rstdx%d" % t])
                S.op("dve", lambda h, b=b, rows=rows, t=t: h.scalar_tensor_tensor(
                    out=xsb[b][0:rows, :], in0=xin[b][0:rows, :], scalar=small[0:rows, SM_RSTDX + t:SM_RSTDX + t + 1],
                    in1=ginr[0:rows, :], op0=ALU.mult, op1=ALU.mult),
                    reads=["xin%d" % b, "rstdx%d" % t, "ginr"], writes=["xsb%d" % b])
                pb = t % 2
                pv = ps[pb][:].bitcast(BF16)
                for kc in range(8):
                    S.op("pe", lambda h, pv=pv, b=b, rows=rows, kc=kc: h.transpose(
                        out=pv[:, kc * 128:kc * 128 + rows], in_=xsb[b][0:rows, kc * 128:(kc + 1) * 128],
                        identity=identb[0:rows, 0:rows]), reads=["xsb%d" % b, "identb"], writes=[psn[pb]])
                blk = min(t // 4, 4)
                dst = hT[:, :, t * 128:t * 128 + rows]
                srcp = pv.rearrange("p (k t) -> p k t", k=8)[:, :, 0:rows]
                if t % 2 == 0:
                    S.op("act", lambda h, dst=dst, srcp=srcp: h.activation(out=dst, in_=srcp, func=ACTF.Copy),
                         reads=[psn[pb]], writes=["hT%d" % blk])
                else:
                    S.op("dve", lambda h, dst=dst, srcp=srcp: h.tensor_copy(out=dst, in_=srcp),
                         reads=[psn[pb]], writes=["hT%d" % blk])
        A.release(arena0)
        S.barrier()

        hTn = ["hT%d" % i for i in range(5)]

        prot = {"n": 0}

        def proj_fm(wb, wcol, consume):
            for tbi, (t0, tn) in enumerate(TBLK):
                k = prot["n"] % 4
                prot["n"] += 1
                for kc in range(8):
                    S.op("pe", _mm(ps[k][:, 0:tn], wbuf[wb][:, kc, wcol:wcol + 128], hT[:, kc, t0:t0 + tn],
                                   kc == 0, kc == 7), reads=["wbuf%d" % wb, hTn[tbi]], writes=[psn[k]])
                consume(tbi, t0, tn, k)

        if "a1" in stages:
            mA1 = A.mark()
            wgx = A.alloc("wgx", [128, 8, 128], BF16)
            wga = A.alloc("wga", [128, 8, 128], BF16)
            XPq = [A.alloc("XPq", [128, NT + 3], F32) for _ in range(2)]
            XPs = [A.alloc("XPs", [128, NSEQ, 3 + TS], F32) for _ in range(2)]
            XCh = [A.alloc("XCh", [128, NT], F32) for _ in range(2)]
            XCb = [A.alloc("XCb", [128, NT], BF16) for _ in range(2)]
            THx = [A.alloc("THx", [128, NT], F32) for _ in range(2)]
            THa = [A.alloc("THa", [128, NT], F32) for _ in range(2)]
            SQb = [A.alloc("SQb", [128, NT], BF16) for _ in range(2)]
            OLb = [A.alloc("OLb", [128, NT], BF16) for _ in range(2)]
            SGc = A.alloc("SGc", [128, NT], BF16)
            slru = A.alloc("slru", [128, 8, NSEQ], F32)
            YLst = sgast
            S.dma("pool", lambda h: h.dma_start(out=wgx[:], in_=wgx_d), writes=["wgx"])
            S.dma("pool", lambda h: h.dma_start(out=wga[:], in_=wga_d), writes=["wga"])
            S.dma("sp", lambda h: h.dma_start(out=slru[:], in_=slru_d), writes=["slru"])
            ssq_first = {"v": True}
            gcnt = {"n": 0}
            wbx = {}
            wbg = {}

            def lru_p1(c):
                s_ = c % 2
                half, cc = c // 4, c % 4
                if cc == 0:
                    wb_x = load_w_in(4 * D + 512 * half)
                    wbx[half] = wb_x
                    for (lo, m, nm) in ((L - 3, 3, "p"), (L, NS, "s")):
                        for kc in range(8):
                            S.op("pe", _mm(ps[6][0:m, :], hT[:, kc, lo:lo + m], wbuf[wb_x][:, kc, :], kc == 0, kc == 7),
                                 reads=["wbuf%d" % wb_x, hTn[3] if nm == "p" else hTn[4]], writes=[psn[6]])
                        sb_ = stage[0] if nm == "p" else stage[1]
                        sn = "stage0" if nm == "p" else "stage1"
                        S.op("act", lambda h, sb_=sb_, m=m: h.activation(out=sb_[0:m, :], in_=ps[6][0:m, :], func=ACTF.Copy),
                             reads=[psn[6]], writes=[sn])
                        if nm == "p":
                            S.dma("sp", lambda h, half=half: h.dma_start(out=convp_d[:, 512 * half:512 * half + 512],
                                                                          in_=stage[0][0:3, :]), reads=[sn], is_output=True)
                        else:
                            for b in range(NSEQ):
                                S.dma("sp", lambda h, half=half, b=b: h.dma_start(
                                    out=convs_d[b, :, 512 * half:512 * half + 512],
                                    in_=stage[1][TS * b + TS - 3:TS * b + TS, :]), reads=[sn], is_output=True)
                wb_x = wbx[half]
                S.dma("sp", lambda h, c=c, s_=s_: h.dma_start(out=XPs[s_][:, :, 0:3], in_=sconv_d[:, c, :, :]),
                      writes=["XPs%d" % s_])
                S.op("pool", lambda h, s_=s_: h.memset(XPq[s_][:, 0:3], 0.0), writes=["XPq%d" % s_])

                def cons_x(tbi, t0, tn, k, s_=s_):
                    if tbi < 4:
                        S.op("act", lambda h: h.activation(out=XPq[s_][:, 3 + t0:3 + t0 + tn], in_=ps[k][:, 0:tn], func=ACTF.Copy),
                             reads=[psn[k]], writes=["XPq%d" % s_])
                    else:
                        S.op("act", lambda h: h.activation(
                            out=XPs[s_][:, :, 3:3 + TS], in_=ps[k][:, 0:NS].rearrange("p (b t) -> p b t", b=NSEQ), func=ACTF.Copy),
                            reads=[psn[k]], writes=["XPs%d" % s_])
                proj_fm(wb_x, 128 * cc, cons_x)

            def lru_p2(c):
                s_ = c % 2
                XC = XCh[s_]
                cw = lambda tap: vecs[:, V_CW + 8 * tap + c:V_CW + 8 * tap + c + 1]
                cb = vecs[:, V_CB + c:V_CB + c + 1]
                XCsv = XC[:, L:NT].rearrange("p (b t) -> p b t", b=NSEQ)
                for (src_of, dstv, rn) in ((lambda tap: XPq[s_][:, tap:tap + L], XC[:, 0:L], "XPq%d" % s_),
                                           (lambda tap: XPs[s_][:, :, tap:tap + TS], XCsv, "XPs%d" % s_)):
                    S.op("dve", lambda h, src_of=src_of, dstv=dstv: h.tensor_scalar(
                        out=dstv, in0=src_of(0), scalar1=cw(0), scalar2=cb, op0=ALU.mult, op1=ALU.add),
                        reads=[rn, "vecs"], writes=["XC%d" % s_])
                    for tap in (1, 2, 3):
                        S.op("dve", lambda h, src_of=src_of, dstv=dstv, tap=tap: h.scalar_tensor_tensor(
                            out=dstv, in0=src_of(tap), scalar=cw(tap), in1=dstv, op0=ALU.mult, op1=ALU.add),
                            reads=[rn, "vecs", "XC%d" % s_], writes=["XC%d" % s_])
                S.op("act", lambda h: h.activation(out=XCb[s_][:], in_=XC[:], func=ACTF.Copy),
                     reads=["XC%d" % s_], writes=["XCb%d" % s_])
                for (wg, wgn, TH, thn, dcb) in ((wgx, "wgx", THx[s_], "THx%d" % s_, DC_HBGX),
                                                (wga, "wga", THa[s_], "THa%d" % s_, DC_HBGA)):
                    for tbi, (t0, tn) in enumerate(TBLK):
                        k = 4 + gcnt["n"] % 2
                        gcnt["n"] += 1
                        S.op("pe", _mm(ps[k][:, 0:tn], wg[:, c, :], XCb[s_][:, t0:t0 + tn], True, True),
                             reads=[wgn, "XCb%d" % s_], writes=[psn[k]])
                        S.op("act", lambda h, TH=TH, t0=t0, tn=tn, k=k, dcb=dcb: h.activation(
                            out=TH[:, t0:t0 + tn], in_=ps[k][:, 0:tn], func=ACTF.Tanh, scale=0.5,
                            bias=dcol[:, dcb + c:dcb + c + 1]), reads=[psn[k], "dc_hbgx", "dc_hbga"], writes=[thn])

            def lru_p3a(c):
                s_ = c % 2
                SQ = XPq[s_][:, 0:NT]
                Aa = THa[s_]
                sqn, than = "XPq%d" % s_, "THa%d" % s_
                S.op("act", lambda h: h.activation(out=SQ, in_=THa[s_][:], func=ACTF.Exp,
                                                   scale=dcol[:, DC_C + c:DC_C + c + 1], bias=dcol[:, DC_C + c:DC_C + c + 1]),
                     reads=[than, "dc_c"], writes=[sqn])
                S.op("act", lambda h: h.activation(out=Aa[:], in_=THa[s_][:], func=ACTF.Exp,
                                                   scale=dcol[:, DC_HC + c:DC_HC + c + 1], bias=dcol[:, DC_HC + c:DC_HC + c + 1]),
                     reads=[than, "dc_hc"], writes=[than])
                S.op("act", lambda h: h.activation(out=SQ, in_=SQ, func=ACTF.Sqrt, scale=-1.0, bias=1.0),
                     reads=[sqn], writes=[sqn])

            def lru_p3b(c):
                s_ = c % 2
                XC = XCh[s_]
                Hh = XCh[s_]
                SQ = XPq[s_][:, 0:NT]
                Aa = THa[s_]
                xn, hn, sqn, thxn, than = "XC%d" % s_, "XC%d" % s_, "XPq%d" % s_, "THx%d" % s_, "THa%d" % s_
                S.op("dve", lambda h: h.scalar_tensor_tensor(out=THx[s_][:], in0=THx[s_][:], scalar=1.0, in1=XC[:],
                                                             op0=ALU.add, op1=ALU.mult), reads=[thxn, xn], writes=[thxn])
                S.op("dve", lambda h: h.scalar_tensor_tensor(out=THx[s_][:], in0=THx[s_][:], scalar=0.5, in1=SQ,
                                                             op0=ALU.mult, op1=ALU.mult), reads=[thxn, sqn], writes=[thxn])
                S.op("dve", lambda h: h.tensor_tensor_scan(out=Hh[:, 0:L], data0=Aa[:, 0:L], data1=THx[s_][:, 0:L], initial=0.0,
                                                           op0=ALU.mult, op1=ALU.add), reads=[than, thxn], writes=[hn])
                for b in range(NSEQ):
                    S.op("dve", lambda h, b=b: h.tensor_tensor_scan(
                        out=Hh[:, L + TS * b:L + TS * b + TS], data0=Aa[:, L + TS * b:L + TS * b + TS],
                        data1=THx[s_][:, L + TS * b:L + TS * b + TS], initial=slru[:, c, b:b + 1],
                        op0=ALU.mult, op1=ALU.add), reads=[than, thxn, "slru"], writes=[hn])

            def lru_gr(c):
                half, cc = c // 4, c % 4
                if cc == 0:
                    wbg[half] = load_w_in(5 * D + 512 * half)

                def cons_g(tbi, t0, tn, k):
                    S.op("act", lambda h: h.activation(out=SGc[:, t0:t0 + tn], in_=ps[k][:, 0:tn], func=ACTF.Silu),
                         reads=[psn[k]], writes=["SGc"])
                proj_fm(wbg[half], 128 * cc, cons_g)

            def lru_p3c(c):
                s_ = c % 2
                Hh = XCh[s_]
                hn = "XC%d" % s_
                S.op("pool", lambda h: h.tensor_copy(out=HL[:, c, 0:1], in_=Hh[:, L - 1:L]), reads=[hn], writes=["HL"])
                S.op("pool", lambda h: h.tensor_copy(
                    out=HL[:, c, 1:5], in_=Hh[:, L:NT].rearrange("p (b t) -> p b t", b=NSEQ)[:, :, TS - 1]),
                    reads=[hn], writes=["HL"])
                S.op("pool", lambda h: h.tensor_tensor(out=SQb[s_][:], in0=Hh[:], in1=Hh[:], op=ALU.mult),
                     reads=[hn], writes=["SQb%d" % s_])
                S.op("act", lambda h: h.activation(out=OLb[s_][:], in_=Hh[:], func=ACTF.Copy),
                     reads=[hn], writes=["OLb%d" % s_])
                for t in range(17):
                    rows = 128 if t < 16 else NS
                    S.op("pe", _mm(ps[7][0:rows, t:t + 1], SQb[s_][:, t * 128:t * 128 + rows], onesb[:, 0:1],
                                   ssq_first["v"], c == 7 and t == 16), reads=["SQb%d" % s_, "onesb"], writes=[psn[7]])
                    ssq_first["v"] = False
                yb = c % 2
                S.op("dve", lambda h: h.scalar_tensor_tensor(
                    out=YLst[yb][:], in0=SGc[:], scalar=vecs[:, V_NLRU + c:V_NLRU + c + 1],
                    in1=OLb[s_][:], op0=ALU.mult, op1=ALU.mult),
                    reads=["SGc", "OLb%d" % s_, "vecs"], writes=["YLst%d" % yb])
                S.dma("sp", lambda h: h.dma_start(out=yl_scr[c], in_=YLst[yb][:]),
                      reads=["YLst%d" % yb], writes=["yl_scr"])

            lru_p1(0)
            lru_p2(0)
            lru_p1(1)
            for c in range(8):
                lru_p3a(c)
                if c + 1 < 8:
                    lru_p2(c + 1)
                lru_p3b(c)
                lru_gr(c)
                if c + 2 < 8:
                    lru_p1(c + 2)
                lru_p3c(c)
            S.op("dve", lambda h: h.tensor_copy(out=small[:, SM_LRU:SM_LRU + 17], in_=ps[7][:, 0:17]),
                 reads=[psn[7]], writes=["ssq_lru"])
            S.dma("sp", lambda h: h.dma_start(out=lrup_d.rearrange("(c p) -> p c", p=128), in_=HL[:, :, 0],
                                              allow_slow_non_contiguous=True), reads=["HL"], is_output=True)
            for b in range(NSEQ):
                S.dma("sp", lambda h, b=b: h.dma_start(out=lrus_d[b].rearrange("(c p) -> p c", p=128), in_=HL[:, :, 1 + b],
                                                       allow_slow_non_contiguous=True), reads=["HL"], is_output=True)
            A.release(mA1)
            S.barrier()

        QT = KT = VT = None
        if "a2" in stages:
            QT = A.alloc("QT", [128, 8, L], BF16)
            KT = A.alloc("KT", [128, 8, L], BF16)
            VT = A.alloc("VT", [128, 8, L], BF16)
            stg = {"n": 0}
            for (which, colbase, dstT, dstS, nm) in (("q", 0, QT, QTs, "QT"), ("k", D, KT, KTs, "KT"), ("v", 2 * D, VT, VTs, "VT")):
                for half in range(2):
                    wb = load_w_in(colbase + 512 * half)
                    for cc in range(4):
                        j = 4 * half + cc

                        def cons_qkv(tbi, t0, tn, k, j=j, dstT=dstT, dstS=dstS, nm=nm):
                            dst = dstT[:, j, t0:t0 + tn] if tbi < 4 else dstS[:, j, :]
                            if (tbi + j) % 2 == 0:
                                S.op("act", lambda h: h.activation(out=dst, in_=ps[k][:, 0:tn], func=ACTF.Copy),
                                     reads=[psn[k]], writes=["%s%d_%d" % (nm, j, tbi)])
                            else:
                                S.op("dve", lambda h: h.tensor_copy(out=dst, in_=ps[k][:, 0:tn]),
                                     reads=[psn[k]], writes=["%s%d_%d" % (nm, j, tbi)])
                        proj_fm(wb, 128 * cc, cons_qkv)
                    if which in ("k", "v"):
                        o_p = kp_d if which == "k" else vp_d
                        o_s = ks_d if which == "k" else vs_d
                        for t in range(17):
                            rows = 128 if t < 16 else NS
                            k = 4 + stg["n"] % 2
                            sg = stg["n"] % 4
                            stg["n"] += 1
                            for kc in range(8):
                                S.op("pe", _mm(ps[k][0:rows, :], hT[:, kc, t * 128:t * 128 + rows], wbuf[wb][:, kc, :],
                                               kc == 0, kc == 7), reads=["wbuf%d" % wb, hTn[min(t // 4, 4)]], writes=[psn[k]])
                            if t % 2 == 0:
                                S.op("act", lambda h, sg=sg, rows=rows, k=k: h.activation(
                                    out=stage[sg][0:rows, :], in_=ps[k][0:rows, :], func=ACTF.Copy),
                                    reads=[psn[k]], writes=["stage%d" % sg])
                            else:
                                S.op("dve", lambda h, sg=sg, rows=rows, k=k: h.tensor_copy(
                                    out=stage[sg][0:rows, :], in_=ps[k][0:rows, :]), reads=[psn[k]], writes=["stage%d" % sg])
                            if t < 16:
                                S.dma("sp", lambda h, sg=sg, t=t, o_p=o_p, half=half: h.dma_start(
                                    out=o_p[t * 128:(t + 1) * 128, 512 * half:512 * half + 512], in_=stage[sg][:]),
                                    reads=["stage%d" % sg], is_output=True)
                            else:
                                for b in range(NSEQ):
                                    S.dma("sp", lambda h, sg=sg, b=b, o_s=o_s, half=half: h.dma_start(
                                        out=o_s[b, WB - TS:WB, 512 * half:512 * half + 512],
                                        in_=stage[sg][TS * b:TS * b + TS, :]), reads=["stage%d" % sg], is_output=True)
            mg = A.mark()
            for half in range(2):
                wb = load_w_in(3 * D + 512 * half)
                for cc in range(4):
                    c = 4 * half + cc
                    sb_i = c % 2

                    def cons_ga(tbi, t0, tn, k, sb_i=sb_i):
                        S.op("act", lambda h: h.activation(out=sgast[sb_i][:, t0:t0 + tn], in_=ps[k][:, 0:tn], func=ACTF.Silu),
                             reads=[psn[k]], writes=["sgast%d" % sb_i])
                    proj_fm(wb, 128 * cc, cons_ga)
                    S.dma("sp", lambda h, c=c, sb_i=sb_i: h.dma_start(out=sga_scr[c], in_=sgast[sb_i][:]),
                          reads=["sgast%d" % sb_i], writes=["sga_scr"])
            A.release(mg)
            S.barrier()

        if "b" in stages:
            for b in range(NSEQ if cache_copy else 0):
                for (src, dst) in ((ck_d, ks_d), (cv_d, vs_d)):
                    for half in range(2):
                        r0 = 1020 * half
                        S.dma("sp", lambda h, src=src, dst=dst, b=b, r0=r0: h.dma_start(
                            out=dst[b, r0:r0 + 1020, :], in_=src[b, TS + r0:TS + r0 + 1020, :]), is_output=True)

            S.op("pool", lambda h: h.memset(Vpair[:, :, 64:128], 1.0), writes=["Vpair"])
            sb_rot = {"n": 0}
            ob_rot = {"n": 0}
            pt_rot = {"n": 0}
            load_bt_pair(0, 0)
            assert nc.lookup_mloc(stage[1]).addr == nc.lookup_mloc(stage[0]).addr + 2048
            PT16 = nc.alloc_sbuf_tensor_at("PT16_al", [128, 2048], BF16, offset=nc.lookup_mloc(stage[0]).addr)
            PTB = list(PT) + [nc.alloc_sbuf_tensor_at("PTx%d" % i, [128, 512], BF16,
                                                      offset=nc.lookup_mloc(stage[2]).addr + 1024 * i) for i in range(2)]
            SBK = [0, 1, 2, 7]

            def vtile_ap(arr, ti, hp, nkeys=128):
                return Vpair[0:nkeys, arr * 16 + ti, 64 * hp:64 * hp + 128]

            for j in range(8):
                bt = j % 2
                if j + 1 < 8:
                    load_bt_pair(j + 1, (j + 1) % 2)
                for arr in range(3):
                    for g in range(2):
                        pb = 5 + (arr * 2 + g) % 2
                        pv = ps[pb][:].bitcast(BF16)
                        for u in range(8):
                            ti = 8 * g + u
                            if arr == 0:
                                cols = VT[:, j, ti * 128:(ti + 1) * 128]
                            elif arr == 1:
                                r, n = ti // 4, ti % 4
                                cols = VT[:, j, 512 * n + r:512 * n + 512:4]
                            else:
                                cols = VT[:, j, ti:L:16]
                            S.op("pe", lambda h, pv=pv, u=u, cols=cols: h.transpose(
                                out=pv[:, u * 128:(u + 1) * 128], in_=cols, identity=identb[:]),
                                reads=["VT%d_%d" % (j, q) for q in range(4)] + ["identb"], writes=[psn[pb]])
                        t0_ = arr * 16 + 8 * g
                        vdst = Vpair[:, t0_:t0_ + 8, :].rearrange("p t (a f) -> p t a f", a=3)[:, :, 0:3:2, :]
                        vsrc = pv.rearrange("p (t a f) -> p t a f", t=8, a=2)
                        if (arr + g) % 2 == 0:
                            S.op("dve", lambda h, vdst=vdst, vsrc=vsrc: h.tensor_copy(out=vdst, in_=vsrc),
                                 reads=[psn[pb]], writes=["Vpair"])
                        else:
                            S.op("act", lambda h, vdst=vdst, vsrc=vsrc: h.activation(out=vdst, in_=vsrc, func=ACTF.Copy),
                                 reads=[psn[pb]], writes=["Vpair"])
                for hp in range(2):
                    P0 = 64 * hp
                    qn = lambda c: ["QT%d_%d" % (j, c)]
                    for g16 in range(4):
                        sb_i = SBK[sb_rot["n"] % 4]
                        sb_rot["n"] += 1
                        Sb = ps[sb_i]
                        for u in range(4):
                            r = 4 * g16 + u
                            S.op("pe", _mm(Sb[:, 128 * u:128 * u + 128], KT[P0:P0 + 64, j, r:L:16], QT[P0:P0 + 64, j, r:L:16],
                                           u == 0, False),
                                 reads=["KT%d_%d" % (j, q) for q in range(4)] + ["QT%d_%d" % (j, q) for q in range(4)],
                                 writes=[psn[sb_i]])
                        for u in range(4):
                            for half in range(2):
                                S.op("pe", _mm(Sb[:, 128 * u + 64 * half:128 * u + 64 * half + 64],
                                               BTp[bt][P0:P0 + 64, 2, half, 128:256], jb[P0:P0 + 64, P0:P0 + 64],
                                               False, u == 3 and half == 1), reads=["BTp%d" % bt, "jb"], writes=[psn[sb_i]])
                        S.op("act", lambda h, Sb=Sb, g16=g16: h.activation(
                            out=PT16[:, 512 * g16:512 * g16 + 512], in_=Sb[:, :], func=ACTF.Exp, scale=0.125),
                            reads=[psn[sb_i]], writes=["PT16_%d" % g16])
                    for c in range(4):
                        groups = []
                        g1 = []
                        for u in range(4):
                            n = 4 * c + u
                            qa = QT[P0:P0 + 64, j, 128 * n:128 * n + 128]
                            oc = slice(128 * u, 128 * u + 128)
                            if n > 0:
                                g1.append((0, n - 1, KT[P0:P0 + 64, j, 128 * (n - 1):128 * n], qa, oc, (0, 0), 128, 128, 0))
                            g1.append((0, n, KT[P0:P0 + 64, j, 128 * n:128 * n + 128], qa, oc, (0, 128), 128, 128, 0))
                        groups.append(g1[:4])
                        if len(g1) > 4:
                            groups.append(g1[4:])
                        g2 = []
                        for r in range(4):
                            qa = QT[P0:P0 + 64, j, 512 * c + r:512 * c + 512:4]
                            oc = slice(r, 512, 4)
                            if c > 0:
                                g2.append((1, 4 * r + c - 1, KT[P0:P0 + 64, j, 512 * (c - 1) + r:512 * c:4], qa, oc, (1, 0), 128, 128, 0))
                            g2.append((1, 4 * r + c, KT[P0:P0 + 64, j, 512 * c + r:512 * c + 512:4], qa, oc, (1, 128), 128, 128, 0))
                        groups.append(g2[:4])
                        if len(g2) > 4:
                            groups.append(g2[4:])
                        g3 = []
                        nk = 32 * (c + 1)
                        for r in range(16):
                            qa = QT[P0:P0 + 64, j, r + 512 * c:512 * c + 512:16]
                            oc = slice(r, 512, 16)
                            g3.append((2, r, KT[P0:P0 + 64, j, r:min(L, r + 16 * nk):16], qa, oc, (2, 128), nk, 32, 32 * c))
                        groups.append(g3)

                        ob = 3 + ob_rot["n"] % 2
                        ob_rot["n"] += 1
                        first_o = True
                        for grp in groups:
                            if grp is groups[-1]:
                                for ti_, (arr, vt, ka, qa, oc, (bp, jj0), nkeys, nq, qoff) in enumerate(grp):
                                    lhs = vtile_ap(arr, vt, hp, nkeys)
                                    S.op("pe", _mm(ps[ob][:, oc], lhs, PT16[0:nkeys, 128 * vt + qoff:128 * vt + qoff + nq],
                                                   first_o, ti_ == len(grp) - 1),
                                         reads=["Vpair", "PT16_%d" % (vt // 4)], writes=[psn[ob]])
                                    first_o = False
                                continue
                            sb_i = SBK[sb_rot["n"] % 4]
                            sb_rot["n"] += 1
                            pt_i = pt_rot["n"] % 5
                            pt_rot["n"] += 1
                            Sb = ps[sb_i]
                            col = 0
                            cols_of = []
                            maxk = 0
                            for ti_, (arr, vt, ka, qa, oc, (bp, jj0), nkeys, nq, qoff) in enumerate(grp):
                                S.op("pe", _mm(Sb[0:nkeys, col:col + nq], ka, qa, ti_ == 0, False),
                                     reads=["KT%d_%d" % (j, q) for q in range(4)] + qn(c), writes=[psn[sb_i]])
                                cols_of.append(col)
                                col += nq
                                maxk = max(maxk, nkeys)
                            for ti_, (arr, vt, ka, qa, oc, (bp, jj0), nkeys, nq, qoff) in enumerate(grp):
                                last = ti_ == len(grp) - 1
                                if nq == 128:
                                    for half in range(2):
                                        S.op("pe", _mm(Sb[0:nkeys, cols_of[ti_] + 64 * half:cols_of[ti_] + 64 * half + 64],
                                                       BTp[bt][P0:P0 + 64, bp, half, jj0:jj0 + nkeys],
                                                       jb[P0:P0 + 64, P0:P0 + 64], False, last and half == 1),
                                             reads=["BTp%d" % bt, "jb"], writes=[psn[sb_i]])
                                else:
                                    half = qoff // 64
                                    u0 = qoff % 64
                                    S.op("pe", _mm(Sb[0:nkeys, cols_of[ti_]:cols_of[ti_] + nq],
                                                   BTp[bt][P0:P0 + 64, bp, half, jj0:jj0 + nkeys],
                                                   jb[P0:P0 + 64, P0 + u0:P0 + u0 + nq], False, last),
                                         reads=["BTp%d" % bt, "jb"], writes=[psn[sb_i]])
                            ncols = col
                            S.op("act", lambda h, Sb=Sb, pt_i=pt_i, maxk=maxk, ncols=ncols: h.activation(
                                out=PTB[pt_i][0:maxk, 0:ncols], in_=Sb[0:maxk, 0:ncols], func=ACTF.Exp, scale=0.125),
                                reads=[psn[sb_i]], writes=["PT%d" % pt_i])
                            for ti_, (arr, vt, ka, qa, oc, (bp, jj0), nkeys, nq, qoff) in enumerate(grp):
                                lhs = vtile_ap(arr, vt, hp, nkeys)
                                is_last = (grp is groups[-1]) and ti_ == len(grp) - 1
                                S.op("pe", _mm(ps[ob][:, oc], lhs, PTB[pt_i][0:nkeys, cols_of[ti_]:cols_of[ti_] + nq],
                                               first_o, is_last), reads=["Vpair", "PT%d" % pt_i], writes=[psn[ob]])
                                first_o = False
                        rb = c % 2
                        D0 = 64 - P0
                        S.op("dve", lambda h, ob=ob, rb=rb, D0=D0, P0=P0: h.reciprocal(
                            out=rec[rb][P0:P0 + 64, :], in_=ps[ob][D0:D0 + 64, :]), reads=[psn[ob]], writes=["rec%d" % rb])
                        S.op("dve", lambda h, ob=ob, rb=rb, P0=P0, c=c, j=j: h.tensor_tensor(
                            out=OA[P0:P0 + 64, j, 512 * c:512 * c + 512], in0=ps[ob][P0:P0 + 64, :],
                            in1=rec[rb][P0:P0 + 64, :], op=ALU.mult), reads=[psn[ob], "rec%d" % rb], writes=["OA_%d" % c])
            S.barrier()

        if "c" in stages:
            mC = A.mark()
            A.release(arena0)
            KH = 1024
            Kc = [A.alloc("Kc", [128, 8, D], BF16) for _ in range(2)]
            Vc = [A.alloc("Vc", [128, 8, D], BF16) for _ in range(2)]
            KcT = [A.alloc("KcT", [128, 8, KH], BF16) for _ in range(2)]
            BSb = A.alloc("BSb", [128, WB + TS], BF16)
            selb = A.alloc("selb", [128, 16, 8], BF16)
            Vn = [A.alloc("Vn", [8, D], BF16) for _ in range(2)]
            BSf = nc.alloc_sbuf_tensor_at("BSf_al", [128, WB + TS], F32, offset=nc.lookup_mloc(Vpair).addr)
            for hh in range(16):
                src = bass.AP(ts_scr.tensor, hh * TSL, [[1, 8], [1, WB + TS]])
                S.dma("sp", lambda h, src=src, hh=hh: h.dma_start(out=BSf[8 * hh:8 * hh + 8, :], in_=src),
                      reads=["ts_scr"], writes=["BSf"])
            S.op("dve", lambda h: h.tensor_copy(out=BSb[:], in_=BSf[:]), reads=["BSf"], writes=["BSb"])
            S.dma("pool", lambda h: h.dma_start(out=selb[:], in_=sel_d), writes=["selb"])
            selflat = selb[:].rearrange("p h t -> p (h t)")
            QPs = A.alloc("QPs", [128, 8, NSEQ, 16], BF16)
            onesf = A.alloc("onesf", [128, 128], F32)
            Pr = [A.alloc("Pr", [128, 16], F32) for _ in range(2)]
            recs = A.alloc("recs", [128, 8, 16], F32)
            S.op("pool", lambda h: h.memset(QPs[:], 0.0), writes=["QPs"])
            S.op("pool", lambda h: h.memset(onesf[:], 1.0), writes=["onesf"])
            for hp in range(2):
                S.op("pool", lambda h, hp=hp: h.tensor_copy(
                    out=QPs[64 * hp:64 * hp + 64, :, :, 8 * hp:8 * hp + 8],
                    in_=QTs[64 * hp:64 * hp + 64, :, :].rearrange("p j (b t) -> p j b t", b=NSEQ)),
                    reads=["QTs"], writes=["QPs"])
            rot = {"s": 0, "p": 0, "t": 0, "r": 0}

            def c_loads(sg):
                b, kh = sg // 2, sg % 2
                e = sg % 2
                for (src_d, dstt, nm) in ((ck_d, Kc, "Kc"), (cv_d, Vc, "Vc")):
                    S.dma("pool", lambda h, b=b, kh=kh, e=e, src_d=src_d, dstt=dstt: h.dma_start(
                        out=dstt[e][:].rearrange("p t f -> p (t f)"),
                        in_=src_d[b, KH * kh:KH * kh + KH, :].rearrange("(p t) f -> p (t f)", t=8),
                        max_dma_last_dim=8192), writes=["%s%d" % (nm, e)])

            def c_compute(sg):
                b, kh = sg // 2, sg % 2
                e = sg % 2
                vb = b % 2
                ob = 3 + b % 2
                ntile = 8 + kh
                if kh == 0:
                    for kc in range(8):
                        S.op("pe", lambda h, b=b, kc=kc: h.transpose(
                            out=ps[7][:].bitcast(BF16)[0:TS, kc * 128:(kc + 1) * 128],
                            in_=VTs[:, kc, TS * b:TS * b + TS], identity=identb[:]), reads=["VTs", "identb"], writes=[psn[7]])
                    S.op("dve", lambda h, vb=vb: h.tensor_copy(out=Vn[vb][:], in_=ps[7][:].bitcast(BF16)[0:TS, 0:D]),
                         reads=[psn[7]], writes=["Vn%d" % vb])
                for t in range(8):
                    pb = 5 + rot["t"] % 2
                    rot["t"] += 1
                    pv = ps[pb][:].bitcast(BF16)
                    for jj in range(8):
                        S.op("pe", lambda h, pv=pv, t=t, jj=jj, e=e: h.transpose(
                            out=pv[:, jj * 128:(jj + 1) * 128], in_=Kc[e][:, t, jj * 128:(jj + 1) * 128], identity=identb[:]),
                            reads=["Kc%d" % e, "identb"], writes=[psn[pb]])
                    dst = KcT[e][:, :, t * 128:(t + 1) * 128]
                    srcp = pv.rearrange("p (k t) -> p k t", k=8)
                    if t % 2 == 0:
                        S.op("act", lambda h, dst=dst, srcp=srcp: h.activation(out=dst, in_=srcp, func=ACTF.Copy),
                             reads=[psn[pb]], writes=["KcT%d" % e])
                    else:
                        S.op("dve", lambda h, dst=dst, srcp=srcp: h.tensor_copy(out=dst, in_=srcp),
                             reads=[psn[pb]], writes=["KcT%d" % e])
                for j in range(8):
                    sb_i = rot["s"] % 3
                    rot["s"] += 1
                    pt_i = rot["p"] % 3
                    rot["p"] += 1
                    Sb = ps[sb_i]
                    qa = QPs[:, j, b, :]
                    for t in range(ntile):
                        nk = 128 if t < 8 else TS
                        ka = KcT[e][:, j, t * 128:(t + 1) * 128] if t < 8 else KTs[:, j, TS * b:TS * b + TS]
                        S.op("pe", _mm(Sb[0:nk, t * 16:(t + 1) * 16], ka, qa, t == 0, False),
                             reads=["KcT%d" % e, "KTs", "QPs"], writes=[psn[sb_i]])
                    for t in range(ntile):
                        nk = 128 if t < 8 else TS
                        bl = BSb[:, KH * kh + t:KH * kh + KH:8] if t < 8 else BSb[:, WB:WB + TS]
                        S.op("pe", _mm(Sb[0:nk, t * 16:(t + 1) * 16], bl, selflat[:, 16 * j:16 * j + 16], False, t == ntile - 1),
                             reads=["BSb", "selb"], writes=[psn[sb_i]])
                    S.op("act", lambda h, Sb=Sb, pt_i=pt_i, ntile=ntile: h.activation(
                        out=PT[pt_i][:, 0:ntile * 16], in_=Sb[:, 0:ntile * 16], func=ACTF.Exp, scale=0.125),
                        reads=[psn[sb_i]], writes=["PT%d" % pt_i])
                    pr_i = rot["r"] % 2
                    rot["r"] += 1
                    S.op("dve", lambda h, pt_i=pt_i, pr_i=pr_i: h.tensor_reduce(
                        out=Pr[pr_i][:], in_=PT[pt_i][:, 0:128].rearrange("p (k q) -> p q k", q=16),
                        axis=mybir.AxisListType.X, op=ALU.add), reads=["PT%d" % pt_i], writes=["Pr%d" % pr_i])
                    first = (kh == 0 and j == 0)
                    for t in range(ntile):
                        nk = 128 if t < 8 else TS
                        lhs = Vc[e][:, t, 128 * j:128 * j + 128] if t < 8 else Vn[vb][:, 128 * j:128 * j + 128]
                        S.op("pe", _mm(ps[ob][:, 32 * j:32 * j + 16], lhs, PT[pt_i][0:nk, t * 16:(t + 1) * 16],
                                       first and t == 0, False),
                             reads=["Vc%d" % e, "Vn%d" % vb, "PT%d" % pt_i], writes=[psn[ob]])
                    S.op("pe", _mm(ps[ob][:, 32 * j + 16:32 * j + 32], onesf[:], Pr[pr_i][:], False, kh == 0),
                         reads=["onesf", "Pr%d" % pr_i], writes=[psn[ob]])
                    if kh == 1:
                        S.op("pe", _mm(ps[ob][:, 32 * j + 16:32 * j + 32], onesb[0:TS, :], PT[pt_i][0:TS, 128:144], False, True),
                             reads=["onesb", "PT%d" % pt_i], writes=[psn[ob]])
                if kh == 1:
                    Ov = ps[ob][:, 0:256].rearrange("p (j x) -> p j x", x=32)
                    S.op("dve", lambda h, Ov=Ov: h.reciprocal(out=recs[:], in_=Ov[:, :, 16:32]),
                         reads=[psn[ob]], writes=["recs"])
                    for hp in range(2):
                        P0 = 64 * hp
                        S.op("dve", lambda h, Ov=Ov, P0=P0, hp=hp, b=b: h.tensor_tensor(
                            out=OA[P0:P0 + 64, :, L + TS * b:L + TS * b + TS], in0=Ov[P0:P0 + 64, :, 8 * hp:8 * hp + 8],
                            in1=recs[P0:P0 + 64, :, 8 * hp:8 * hp + 8], op=ALU.mult),
                            reads=[psn[ob], "recs"], writes=["OA_4"])

            c_loads(0)
            for sg in range(2 * NSEQ):
                if sg + 1 < 2 * NSEQ:
                    c_loads(sg + 1)
                c_compute(sg)
            S.barrier()

        if debug:
            dbg_oa = nc.dram_tensor("dbg_oa", [128, 8, NT], BF16, kind="ExternalOutput").ap()
            dbg_yl = nc.dram_tensor("dbg_yl", [8, 128, NT], BF16, kind="ExternalOutput").ap()
            dbg_sg = nc.dram_tensor("dbg_sg", [8, 128, NT], BF16, kind="ExternalOutput").ap()
            S.dma("sp", lambda h: h.dma_start(out=dbg_oa, in_=OA[:]), is_output=True)
            S.dma("sp", lambda h: h.dma_start(out=dbg_yl, in_=yl_scr), is_output=True)
            S.dma("sp", lambda h: h.dma_start(out=dbg_sg, in_=sga_scr), is_output=True)
            S.barrier()

        if "d" in stages:
            A.release(arena0)
            wo = A.alloc("wo", [128, 16, D], BF16)
            gfin = A.alloc("gfin", [128, D], F32)
            YLb = [A.alloc("YLb", [128, 8, 512], BF16) for _ in range(2)]
            SGb = [A.alloc("SGb", [128, 8, 512], BF16) for _ in range(2)]
            YAb = [A.alloc("YAb", [128, 8, 512], BF16) for _ in range(2)]
            SQe = A.alloc("SQe", [128, 8, 512], BF16)
            xre = [A.alloc("xre", [128, D], F32) for _ in range(2)]
            yt = [A.alloc("yt", [128, D], F32) for _ in range(2)]
            junk2 = sgast[0]
            for q in range(4):
                S.dma("pool", lambda h, q=q: h.dma_start(
                    out=wo[:, 4 * q:4 * q + 4, :],
                    in_=wout_d[512 * q:512 * q + 512, :].rearrange("(kc p) n -> p kc n", p=128)), writes=["wo"])
            S.dma("sp", lambda h: h.dma_start(out=gfin[:], in_=gfin_d), writes=["gfin"])
            att_first = {"v": True}
            prot_d = {"n": 0}
            pending = {"f": None}
            S.op("dve", lambda h: h.tensor_scalar(
                out=small[:, SM_MSX:SM_MSX + 17], in0=small[:, SM_LRU:SM_LRU + 17],
                scalar1=1.0 / D, scalar2=EPS, op0=ALU.mult, op1=ALU.add), reads=["ssq_lru"], writes=["msl"])
            S.op("pool", lambda h: h.tensor_tensor(
                out=small[:, SM_RL:SM_RL + 17], in0=small[:, SM_MSX:SM_MSX + 17],
                in1=nhalf[:, 0:17], op=ALU.pow), reads=["msl", "nhalf"], writes=["rl"])
            def d_loads(tbi):
                t0, tn = TBLK[tbi]
                e = tbi % 2
                S.dma("sp", lambda h: h.dma_start(
                    out=YLb[e][:, :, 0:tn], in_=yl_scr[:, :, t0:t0 + tn].rearrange("c p t -> p c t")),
                    reads=["yl_scr"], writes=["YLb%d" % e])
                S.dma("sp", lambda h: h.dma_start(
                    out=SGb[e][:, :, 0:tn], in_=sga_scr[:, :, t0:t0 + tn].rearrange("c p t -> p c t")),
                    reads=["sga_scr"], writes=["SGb%d" % e])

            def x_load(t):
                rows = 128 if t < 16 else NS
                xb = t % 2
                src = xp_d[t * 128:(t + 1) * 128, :] if t < 16 else xs_d
                S.dma("sp", lambda h: h.dma_start(out=xre[xb][0:rows, :], in_=src), writes=["xre%d" % xb])

            d_loads(0)
            x_load(0)
            for tbi, (t0, tn) in enumerate(TBLK):
                e = tbi % 2
                S.op("pool", lambda h, t0=t0, tn=tn: h.tensor_tensor(
                    out=SQe[:, :, 0:tn], in0=OA[:, :, t0:t0 + tn], in1=OA[:, :, t0:t0 + tn], op=ALU.mult),
                    reads=["OA_%d" % tbi], writes=["SQe"])
                ntile = (tn + 127) // 128
                tb0 = t0 // 128
                for tt in range(ntile):
                    rows = min(128, tn - 128 * tt)
                    t = tb0 + tt
                    for c in range(8):
                        S.op("pe", _mm(ps[7][0:rows, 32 + t:33 + t], SQe[:, c, 128 * tt:128 * tt + rows], onesb[:, 0:1],
                                       att_first["v"], c == 7), reads=["SQe", "onesb"], writes=[psn[7]])
                        att_first["v"] = False
                rws = 128 if tbi < 4 else NS
                S.op("dve", lambda h, rws=rws, tb0=tb0, ntile=ntile: h.tensor_scalar(
                    out=small[0:rws, SM_ATT + tb0:SM_ATT + tb0 + ntile], in0=ps[7][0:rws, 32 + tb0:32 + tb0 + ntile],
                    scalar1=1.0 / D, scalar2=EPS, op0=ALU.mult, op1=ALU.add), reads=[psn[7]], writes=["msa%d" % tbi])
                S.op("pool", lambda h, rws=rws, tb0=tb0, ntile=ntile: h.tensor_tensor(
                    out=small[0:rws, SM_RA + tb0:SM_RA + tb0 + ntile], in0=small[0:rws, SM_ATT + tb0:SM_ATT + tb0 + ntile],
                    in1=nhalf[0:rws, 0:ntile], op=ALU.pow), reads=["msa%d" % tbi, "nhalf"], writes=["ra%d" % tbi])
                for c in range(8):
                    S.op("dve", lambda h, e=e, c=c, t0=t0, tn=tn: h.scalar_tensor_tensor(
                        out=YAb[e][:, c, 0:tn], in0=OA[:, c, t0:t0 + tn], scalar=vecs[:, V_NATT + c:V_NATT + c + 1],
                        in1=SGb[e][:, c, 0:tn], op0=ALU.mult, op1=ALU.mult),
                        reads=["OA_%d" % tbi, "SGb%d" % e, "vecs"], writes=["YAb%d" % e])
                for tt in range(ntile):
                    rows = min(128, tn - 128 * tt)
                    t = tb0 + tt
                    xb = t % 2
                    if t + 1 <= 16:
                        x_load(t + 1)
                    if tt == 0 and tbi + 1 < len(TBLK):
                        d_loads(tbi + 1)
                    yb = t % 2
                    for hf in range(2):
                        ka = prot_d["n"] % 4
                        kl = (prot_d["n"] + 1) % 4
                        prot_d["n"] += 2
                        for kc in range(8):
                            S.op("pe", _mm(ps[ka][0:rows, :], YAb[e][:, kc, 128 * tt:128 * tt + rows],
                                           wo[:, kc, 512 * hf:512 * hf + 512], kc == 0, kc == 7),
                                 reads=["YAb%d" % e, "wo"], writes=[psn[ka]])
                        for kc in range(8):
                            S.op("pe", _mm(ps[kl][0:rows, :], YLb[e][:, kc, 128 * tt:128 * tt + rows],
                                           wo[:, 8 + kc, 512 * hf:512 * hf + 512], kc == 0, kc == 7),
                                 reads=["YLb%d" % e, "wo"], writes=[psn[kl]])
                        S.op("dve", lambda h, rows=rows, t=t, ka=ka, yb=yb, xb=xb, hf=hf: h.scalar_tensor_tensor(
                            out=yt[yb][0:rows, 512 * hf:512 * hf + 512], in0=ps[ka][0:rows, :],
                            scalar=small[0:rows, SM_RA + t:SM_RA + t + 1], in1=xre[xb][0:rows, 512 * hf:512 * hf + 512],
                            op0=ALU.mult, op1=ALU.add), reads=[psn[ka], "ra%d" % tbi, "xre%d" % xb], writes=["yt%d" % yb])
                        S.op("dve", lambda h, rows=rows, t=t, kl=kl, yb=yb, hf=hf: h.scalar_tensor_tensor(
                            out=yt[yb][0:rows, 512 * hf:512 * hf + 512], in0=ps[kl][0:rows, :],
                            scalar=small[0:rows, SM_RL + t:SM_RL + t + 1], in1=yt[yb][0:rows, 512 * hf:512 * hf + 512],
                            op0=ALU.mult, op1=ALU.add), reads=[psn[kl], "rl", "yt%d" % yb], writes=["yt%d" % yb])
                    S.op("act", lambda h, rows=rows, t=t, yb=yb: h.activation(
                        out=junk2[0:rows, 0:D], in_=yt[yb][0:rows, :], func=ACTF.Square,
                        accum_out=small[0:rows, SM_SSY + t:SM_SSY + t + 1]), reads=["yt%d" % yb], writes=["junk2", "ssy%d" % t])
                    S.op("dve", lambda h, rows=rows, t=t: h.tensor_scalar(
                        out=small[0:rows, SM_SSQX + t:SM_SSQX + t + 1], in0=small[0:rows, SM_SSY + t:SM_SSY + t + 1],
                        scalar1=1.0 / D, scalar2=EPS, op0=ALU.mult, op1=ALU.add), reads=["ssy%d" % t], writes=["msy%d" % t])
                    S.op("pool", lambda h, rows=rows, t=t: h.tensor_tensor(
                        out=small[0:rows, SM_RSY + t:SM_RSY + t + 1], in0=small[0:rows, SM_SSQX + t:SM_SSQX + t + 1],
                        in1=nhalf[0:rows, 0:1], op=ALU.pow), reads=["msy%d" % t, "nhalf"], writes=["rsy%d" % t])

                    def final(rows=rows, t=t, yb=yb):
                        S.op("dve", lambda h: h.scalar_tensor_tensor(
                            out=yt[yb][0:rows, :], in0=yt[yb][0:rows, :], scalar=small[0:rows, SM_RSY + t:SM_RSY + t + 1],
                            in1=gfin[0:rows, :], op0=ALU.mult, op1=ALU.mult), reads=["yt%d" % yb, "rsy%d" % t, "gfin"],
                            writes=["yt%d" % yb])
                        dst = yp_d[t * 128:(t + 1) * 128, :] if t < 16 else ys_d
                        S.dma("sp", lambda h: h.dma_start(out=dst, in_=yt[yb][0:rows, :]),
                              reads=["yt%d" % yb], is_output=True)
                    if pending["f"] is not None:
                        pending["f"]()
                    pending["f"] = final
            if pending["f"] is not None:
                pending["f"]()

        S.finish()
        S.replay()
    return nc


def _cols(v):
    return np.ascontiguousarray(np.asarray(v, np.float32).reshape(8, 128).T)


def _blockdiag(w):
    out = np.zeros((128, 8, 128), np.float32)
    w = np.asarray(w, np.float32)
    for c in range(8):
        out[0:64, c, 0:64] = w[2 * c]
        out[64:128, c, 64:128] = w[2 * c + 1]
    return out


_CONSTS = None
_NC_CACHE = {}


def make_in_maps(inputs):
    global _CONSTS
    if _CONSTS is None:
        _CONSTS = _structural_constants()
    cs = _CONSTS
    f = lambda a: np.asarray(a, np.float32)
    vecs = np.concatenate(
        [_cols(f(inputs["norm_in"])[0]), _cols(f(inputs["norm_attn"])[0]), _cols(f(inputs["norm_lru"])[0])]
        + [_cols(f(inputs["conv_w"])[0, tap]) for tap in range(4)]
        + [_cols(f(inputs["conv_b"])[0]), _cols(f(inputs["b_gate_x"])[0]), _cols(f(inputs["b_gate_a"])[0]),
           _cols(f(inputs["lru_param"])[0])], axis=1)
    assert vecs.shape == (128, NVEC)
    shared = dict(
        w_in=np.ascontiguousarray(f(inputs["w_in"])[0]),
        w_out=np.ascontiguousarray(f(inputs["w_out"])[0]),
        vecs=np.ascontiguousarray(vecs),
        ginrow=np.ascontiguousarray(np.broadcast_to(f(inputs["norm_in"])[0][None, :], (128, D))),
        gfinrow=np.ascontiguousarray(np.broadcast_to(f(inputs["norm_final"])[None, :], (128, D))),
        wgx=_blockdiag(f(inputs["w_gate_x"])[0]),
        wga=_blockdiag(f(inputs["w_gate_a"])[0]),
        rb_aug=np.ascontiguousarray(np.concatenate([f(inputs["rel_bias"]), cs["rbc"]], axis=0)),
        ohp=cs["ohp"], ohs=cs["ohs"], ident=cs["ident"], jm=cs["jm"], sel=cs["sel"],
    )
    xp = f(inputs["x_prompt"])
    xs = f(inputs["x_sample"])
    ck = f(inputs["cache_win_k"])[0]
    cv = f(inputs["cache_win_v"])[0]
    sc = f(inputs["state_conv"])[0]
    sl = f(inputs["state_lru"])[0]
    maps = []
    for c in range(NCORES):
        bs = slice(NSEQ * c, NSEQ * c + NSEQ)
        m = dict(shared)
        m["xp"] = np.ascontiguousarray(xp[c])
        m["xs"] = np.ascontiguousarray(xs[bs].reshape(NS, D))
        m["ck"] = np.ascontiguousarray(ck[bs].reshape(NSEQ, WB, D))
        m["cv"] = np.ascontiguousarray(cv[bs].reshape(NSEQ, WB, D))
        m["sconv"] = np.ascontiguousarray(sc[bs].reshape(NSEQ, 3, 8, 128).transpose(3, 2, 0, 1))
        m["slru"] = np.ascontiguousarray(sl[bs].reshape(NSEQ, 8, 128).transpose(2, 1, 0))
        maps.append(m)
    return maps


def assemble(results):
    cat = lambda k: np.concatenate([np.asarray(r[k]) for r in results], axis=0)
    y_p = np.stack([np.asarray(r["yp"]) for r in results], axis=0)
    y_s = cat("ys").reshape(NCORES * NSEQ, TS, D)
    kp = np.stack([np.asarray(r["kp"]) for r in results], axis=0).reshape(1, NCORES, L, 16, 64)
    vp = np.stack([np.asarray(r["vp"]) for r in results], axis=0).reshape(1, NCORES, L, 16, 64)
    convp = np.stack([np.asarray(r["convp"]) for r in results], axis=0).reshape(1, NCORES, 3, D)
    lrup = np.stack([np.asarray(r["lrup"]) for r in results], axis=0).reshape(1, NCORES, D)
    ks = cat("ks").reshape(1, NCORES * NSEQ, WB, 16, 64)
    vs = cat("vs").reshape(1, NCORES * NSEQ, WB, 16, 64)
    convs = cat("convs").reshape(1, NCORES * NSEQ, 3, D)
    lrus = cat("lrus").reshape(1, NCORES * NSEQ, D)
    return tuple(np.ascontiguousarray(a, dtype=np.float32)
                 for a in (y_p, y_s, kp, vp, convp, lrup, ks, vs, convs, lrus))


def kernel(**inputs):
    maps = make_in_maps(inputs)
    nc = build_nc()
    res = run_bass_kernel_spmd(nc, maps, core_ids=list(range(NCORES)))
    return assemble(res.results)
```
